# Optimizing a Trainium2 kernel written in Bass

```python
import jax, jax.numpy as jnp
from jax import lax
import numpy as np

D_MODEL = 1024
BATCH = 32
SEQ = 2048
DEPTH = 1

HEAD_DIM = 64
NSA_HEADS = 8
NSA_KV_HEADS = 2
NSA_GROUP = NSA_HEADS // NSA_KV_HEADS
NSA_CMP_BLOCK = 32
NSA_CMP_STRIDE = 16
NSA_SEL_BLOCK = 64
NSA_SEL_TOPN = 16
NSA_WINDOW = 512
NSA_Q_CHUNK = 32
NSA_FORCE_BONUS = 1e4
MOBA_HEADS = 8
MOBA_BLOCK = 256
MOBA_TOPK = 3
MOBA_Q_CHUNK = 16
D_FF = 2816
CONV_WIDTH = 3
ROPE_THETA = 10000.0
NORM_EPS = 1e-6
NEG_INF = -1e30

NSA_WIDTH = NSA_HEADS * HEAD_DIM
NSA_KV_WIDTH = NSA_KV_HEADS * HEAD_DIM
MOBA_WIDTH = MOBA_HEADS * HEAD_DIM
IN_SIZES = (NSA_WIDTH, NSA_KV_WIDTH, NSA_KV_WIDTH, NSA_KV_WIDTH, NSA_KV_WIDTH, NSA_KV_WIDTH, NSA_KV_WIDTH,
            3 * NSA_HEADS, MOBA_WIDTH, MOBA_WIDTH, MOBA_WIDTH, D_MODEL, D_MODEL)
IN_WIDTH = sum(IN_SIZES)

kernel_name = "hybrid_nsa_moba_convffn_adaln"


def rms_norm(x, g):
    xf = x.astype(jnp.float32)
    y = xf * lax.rsqrt(jnp.mean(xf * xf, axis=-1, keepdims=True) + NORM_EPS)
    return (y * g.astype(jnp.float32)).astype(x.dtype)


def rope(x, positions):
    half = x.shape[-1] // 2
    inv_freq = ROPE_THETA ** (-jnp.arange(half, dtype=jnp.float32) / half)
    ang = positions.astype(jnp.float32)[:, None, :, None] * inv_freq
    cos, sin = jnp.cos(ang), jnp.sin(ang)
    xf = x.astype(jnp.float32)
    x1, x2 = xf[..., :half], xf[..., half:]
    return jnp.concatenate([x1 * cos - x2 * sin, x2 * cos + x1 * sin], axis=-1).astype(x.dtype)


def masked_softmax(s, mask):
    s = jnp.where(mask, s, NEG_INF)
    m = jnp.max(s, axis=-1, keepdims=True)
    p = jnp.where(mask, jnp.exp(s - m), 0.0)
    return p / jnp.maximum(jnp.sum(p, axis=-1, keepdims=True), 1e-30)


def split_heads(t, n):
    b, s, _ = t.shape
    return t.reshape(b, s, n, HEAD_DIM).transpose(0, 2, 1, 3)


def merge_heads(t):
    b, n, s, d = t.shape
    return t.transpose(0, 2, 1, 3).reshape(b, s, n * d)


def nsa_compress(x, w_pos, w1, w2):
    s = x.shape[2]
    n_cmp = (s - NSA_CMP_BLOCK) // NSA_CMP_STRIDE + 1
    idx = np.arange(n_cmp)[:, None] * NSA_CMP_STRIDE + np.arange(NSA_CMP_BLOCK)[None, :]
    blocks = x[:, :, idx] + w_pos
    flat = blocks.reshape(blocks.shape[0], blocks.shape[1], n_cmp, NSA_CMP_BLOCK * HEAD_DIM)
    return jax.nn.gelu(flat @ w1) @ w2


def nsa_attention(q, k_cmp, v_cmp, k_slc, v_slc, k_win, v_win, gates):
    b, h, s, dh = q.shape
    g_kv, r = NSA_KV_HEADS, NSA_GROUP
    scale = HEAD_DIM ** -0.5
    qg = q.reshape(b, g_kv, r, s, dh)
    t_pos = jnp.arange(s)

    n_cmp = k_cmp.shape[2]
    cmp_end = jnp.arange(n_cmp) * NSA_CMP_STRIDE + NSA_CMP_BLOCK - 1
    s_cmp = jnp.einsum('bgrsd,bgcd->bgrsc', qg, k_cmp).astype(jnp.float32) * scale
    p_cmp = masked_softmax(s_cmp, cmp_end[None, :] <= t_pos[:, None])
    o_cmp = jnp.einsum('bgrsc,bgcd->bgrsd', p_cmp.astype(v_cmp.dtype), v_cmp).reshape(b, h, s, dh)

    n_sb = s // NSA_SEL_BLOCK
    c_start = np.arange(n_cmp)[:, None] * NSA_CMP_STRIDE
    js = np.arange(n_sb)[None, :]
    overlap = ((c_start < (js + 1) * NSA_SEL_BLOCK) & (c_start + NSA_CMP_BLOCK > js * NSA_SEL_BLOCK)).astype(np.float32)
    imp = jnp.einsum('bgrsc,cj->bgsj', p_cmp, jnp.asarray(overlap))
    own = t_pos // NSA_SEL_BLOCK
    jb = jnp.arange(n_sb)[None, :]
    forced = (jb == 0) | (jb == own[:, None]) | (jb == own[:, None] - 1)
    imp = jnp.where(jb <= own[:, None], imp + jnp.where(forced, NSA_FORCE_BONUS, 0.0), NEG_INF)
    n_sel = min(NSA_SEL_TOPN, n_sb)
    _, sel_idx = lax.top_k(imp, n_sel)

    k_blocks = k_slc.reshape(b, g_kv, n_sb, NSA_SEL_BLOCK, dh)
    v_blocks = v_slc.reshape(b, g_kv, n_sb, NSA_SEL_BLOCK, dh)
    pad = ((0, 0), (0, 0), (NSA_WINDOW, 0), (0, 0))
    kw_pad = jnp.pad(k_win, pad)
    vw_pad = jnp.pad(v_win, pad)

    qc = NSA_Q_CHUNK
    nc = s // qc
    band = NSA_WINDOW + qc
    q_chunks = qg.reshape(b, g_kv, r, nc, qc, dh).transpose(3, 0, 1, 2, 4, 5)
    idx_chunks = sel_idx.reshape(b, g_kv, nc, qc, n_sel).transpose(2, 0, 1, 3, 4)
    starts = jnp.arange(nc) * qc
    bi = jnp.arange(b)[:, None, None, None]
    gi = jnp.arange(g_kv)[None, :, None, None]

    def chunk(args):
        q_c, idx_c, start = args
        t = start + jnp.arange(qc)
        kb = k_blocks[bi, gi, idx_c]
        vb = v_blocks[bi, gi, idx_c]
        s_sel = jnp.einsum('bgrqd,bgqnkd->bgrqnk', q_c, kb).astype(jnp.float32) * scale
        kp = idx_c[..., None] * NSA_SEL_BLOCK + jnp.arange(NSA_SEL_BLOCK)
        m_sel = (kp <= t[:, None, None])[:, :, None].reshape(b, g_kv, 1, qc, n_sel * NSA_SEL_BLOCK)
        p_sel = masked_softmax(s_sel.reshape(b, g_kv, r, qc, n_sel * NSA_SEL_BLOCK), m_sel)
        o_sel = jnp.einsum('bgrqnk,bgqnkd->bgrqd', p_sel.reshape(s_sel.shape).astype(vb.dtype), vb)
        kwb = lax.dynamic_slice_in_dim(kw_pad, start, band, axis=2)
        vwb = lax.dynamic_slice_in_dim(vw_pad, start, band, axis=2)
        kpw = start - NSA_WINDOW + jnp.arange(band)
        diff = t[:, None] - kpw[None, :]
        m_win = (kpw[None, :] >= 0) & (diff >= 0) & (diff < NSA_WINDOW)
        s_win = jnp.einsum('bgrqd,bgkd->bgrqk', q_c, kwb).astype(jnp.float32) * scale
        o_win = jnp.einsum('bgrqk,bgkd->bgrqd', masked_softmax(s_win, m_win).astype(vwb.dtype), vwb)
        return o_sel, o_win

    o_sel, o_win = lax.map(chunk, (q_chunks, idx_chunks, starts))
    unchunk = lambda o: o.transpose(1, 2, 3, 0, 4, 5).reshape(b, h, s, dh)
    return gates[..., 0:1] * o_cmp + gates[..., 1:2] * unchunk(o_sel) + gates[..., 2:3] * unchunk(o_win)


def moba_attention(q, k, v):
    b, h, s, dh = q.shape
    blk = MOBA_BLOCK
    scale = HEAD_DIM ** -0.5
    nb = -(-s // blk)
    pad = ((0, 0), (0, 0), (0, nb * blk - s), (0, 0))
    k_pad = jnp.pad(k, pad)
    v_pad = jnp.pad(v, pad)
    k_blocks = k_pad.reshape(b, h, nb, blk, dh)
    v_blocks = v_pad.reshape(b, h, nb, blk, dh)
    own = jnp.arange(s) // blk
    n_top = min(MOBA_TOPK, nb - 1)
    qc = MOBA_Q_CHUNK
    nc = s // qc
    q_chunks = q.reshape(b, h, nc, qc, dh).transpose(2, 0, 1, 3, 4)
    starts = jnp.arange(nc) * qc
    if n_top > 0:
        k_mean = jnp.mean(k_blocks.astype(jnp.float32), axis=3)
        gate = jnp.einsum('bhsd,bhjd->bhsj', q.astype(jnp.float32), k_mean)
        past = jnp.arange(nb)[None, :] < own[:, None]
        _, sel_idx = lax.top_k(jnp.where(past, gate, NEG_INF), n_top)
        idx_chunks = sel_idx.reshape(b, h, nc, qc, n_top).transpose(2, 0, 1, 3, 4)
    else:
        idx_chunks = jnp.zeros((nc, b, h, qc, 0), jnp.int32)
    bi = jnp.arange(b)[:, None, None, None]
    hi = jnp.arange(h)[None, :, None, None]

    def chunk(args):
        q_c, idx_c, start = args
        t = start + jnp.arange(qc)
        cur = start // blk
        blk_start = cur * blk
        k_own = lax.dynamic_slice_in_dim(k_pad, blk_start, blk, axis=2)
        v_own = lax.dynamic_slice_in_dim(v_pad, blk_start, blk, axis=2)
        m_own = jnp.broadcast_to((blk_start + jnp.arange(blk))[None, :] <= t[:, None], (b, h, qc, blk))
        s_own = jnp.einsum('bhqd,bhkd->bhqk', q_c, k_own).astype(jnp.float32) * scale
        if n_top == 0:
            p = masked_softmax(s_own, m_own).astype(v_own.dtype)
            return jnp.einsum('bhqk,bhkd->bhqd', p, v_own)
        kg = k_blocks[bi, hi, idx_c]
        vg = v_blocks[bi, hi, idx_c]
        s_past = jnp.einsum('bhqd,bhqnkd->bhqnk', q_c, kg).astype(jnp.float32).reshape(b, h, qc, n_top * blk) * scale
        m_past = jnp.broadcast_to((idx_c < cur)[..., None], (b, h, qc, n_top, blk)).reshape(b, h, qc, n_top * blk)
        p = masked_softmax(jnp.concatenate([s_past, s_own], axis=-1),
                           jnp.concatenate([m_past, m_own], axis=-1)).astype(v.dtype)
        p_past = p[..., :n_top * blk].reshape(b, h, qc, n_top, blk)
        return (jnp.einsum('bhqnk,bhqnkd->bhqd', p_past, vg)
                + jnp.einsum('bhqk,bhkd->bhqd', p[..., n_top * blk:], v_own))

    o = lax.map(chunk, (q_chunks, idx_chunks, starts))
    return o.transpose(1, 2, 0, 3, 4).reshape(b, h, s, dh)


def causal_dwconv(a, w, bias):
    y = lax.conv_general_dilated(a, w[:, None, :].astype(a.dtype), window_strides=(1,),
                                 padding=[(CONV_WIDTH - 1, 0)],
                                 dimension_numbers=('NWC', 'WIO', 'NWC'),
                                 feature_group_count=a.shape[-1])
    return y + bias


def setup_inputs(seed: int = 0) -> dict:
    key = jax.random.key(seed)
    ks = jax.random.split(key, 32)
    f32 = jnp.float32

    def nrm(k, shape, scale):
        return jax.random.normal(k, shape, f32) * scale

    def gain(k, n):
        return 1.0 + 0.05 * jax.random.normal(k, (DEPTH, n), f32)

    L = NSA_CMP_BLOCK
    offsets = jax.random.randint(ks[2], (BATCH, 1), 0, 1024)
    positions = (jnp.arange(SEQ)[None, :] + offsets).astype(jnp.int32)
    return {
        "x": nrm(ks[0], (BATCH, SEQ, D_MODEL), 1.0),
        "c": nrm(ks[1], (BATCH, D_MODEL), 1.0),
        "positions": positions,
        "w_ada": nrm(ks[3], (DEPTH, D_MODEL, 6 * D_MODEL), 0.5 * D_MODEL ** -0.5),
        "b_ada": nrm(ks[4], (DEPTH, 6 * D_MODEL), 0.01),
        "g_attn_norm": gain(ks[5], D_MODEL),
        "w_in": nrm(ks[6], (DEPTH, D_MODEL, IN_WIDTH), D_MODEL ** -0.5),
        "g_q_nsa": gain(ks[7], HEAD_DIM),
        "g_k_cmp": gain(ks[8], HEAD_DIM),
        "g_k_slc": gain(ks[9], HEAD_DIM),
        "g_k_win": gain(ks[10], HEAD_DIM),
        "cmp_k_pos": nrm(ks[11], (DEPTH, L, HEAD_DIM), 0.02),
        "cmp_k_w1": nrm(ks[12], (DEPTH, L * HEAD_DIM, HEAD_DIM), (L * HEAD_DIM) ** -0.5),
        "cmp_k_w2": nrm(ks[13], (DEPTH, HEAD_DIM, HEAD_DIM), HEAD_DIM ** -0.5),
        "cmp_v_pos": nrm(ks[14], (DEPTH, L, HEAD_DIM), 0.02),
        "cmp_v_w1": nrm(ks[15], (DEPTH, L * HEAD_DIM, HEAD_DIM), (L * HEAD_DIM) ** -0.5),
        "cmp_v_w2": nrm(ks[16], (DEPTH, HEAD_DIM, HEAD_DIM), HEAD_DIM ** -0.5),
        "g_q_moba": gain(ks[17], HEAD_DIM),
        "g_k_moba": gain(ks[18], HEAD_DIM),
        "w_branch_nsa": nrm(ks[19], (DEPTH, NSA_WIDTH, D_MODEL), NSA_WIDTH ** -0.5),
        "w_branch_moba": nrm(ks[20], (DEPTH, MOBA_WIDTH, D_MODEL), MOBA_WIDTH ** -0.5),
        "w_out": nrm(ks[21], (DEPTH, D_MODEL, D_MODEL), D_MODEL ** -0.5),
        "g_ffn_norm": gain(ks[22], D_MODEL),
        "w_ffn_up": nrm(ks[23], (DEPTH, D_MODEL, 2 * D_FF), D_MODEL ** -0.5),
        "conv_w": nrm(ks[24], (DEPTH, CONV_WIDTH, D_FF), CONV_WIDTH ** -0.5),
        "conv_b": nrm(ks[25], (DEPTH, D_FF), 0.01),
        "w_ffn_down": nrm(ks[26], (DEPTH, D_FF, D_MODEL), D_FF ** -0.5),
    }


def reference(x, c, positions, w_ada, b_ada, g_attn_norm, w_in, g_q_nsa, g_k_cmp, g_k_slc, g_k_win,
              cmp_k_pos, cmp_k_w1, cmp_k_w2, cmp_v_pos, cmp_v_w1, cmp_v_w2, g_q_moba, g_k_moba,
              w_branch_nsa, w_branch_moba, w_out, g_ffn_norm, w_ffn_up, conv_w, conv_b, w_ffn_down):
    b, s, _ = x.shape
    offsets = np.cumsum(IN_SIZES)[:-1].tolist()
    for l in range(DEPTH):
        mod = (c @ w_ada[l] + b_ada[l])[:, None, :]
        sh1, sc1, gt1, sh2, sc2, gt2 = jnp.split(mod, 6, axis=-1)

        h = rms_norm(x, g_attn_norm[l]) * (1 + sc1) + sh1
        (q_a, kc, vc, ksl, vsl, kwn, vwn, gate_nsa, q_b, k_b, v_b, gate_a, gate_b) = jnp.split(h @ w_in[l], offsets, axis=-1)

        q_a = rope(rms_norm(split_heads(q_a, NSA_HEADS), g_q_nsa[l]), positions)
        kc = rms_norm(nsa_compress(rope(split_heads(kc, NSA_KV_HEADS), positions),
                                   cmp_k_pos[l], cmp_k_w1[l], cmp_k_w2[l]), g_k_cmp[l])
        vc = nsa_compress(split_heads(vc, NSA_KV_HEADS), cmp_v_pos[l], cmp_v_w1[l], cmp_v_w2[l])
        ksl = rope(rms_norm(split_heads(ksl, NSA_KV_HEADS), g_k_slc[l]), positions)
        kwn = rope(rms_norm(split_heads(kwn, NSA_KV_HEADS), g_k_win[l]), positions)
        g_nsa = jax.nn.sigmoid(gate_nsa).reshape(b, s, NSA_HEADS, 3).transpose(0, 2, 1, 3)
        o_a = merge_heads(nsa_attention(q_a, kc, vc, ksl, split_heads(vsl, NSA_KV_HEADS),
                                        kwn, split_heads(vwn, NSA_KV_HEADS), g_nsa))

        q_b = rope(rms_norm(split_heads(q_b, MOBA_HEADS), g_q_moba[l]), positions)
        k_b = rope(rms_norm(split_heads(k_b, MOBA_HEADS), g_k_moba[l]), positions)
        o_b = merge_heads(moba_attention(q_b, k_b, split_heads(v_b, MOBA_HEADS)))

        mixed = (jax.nn.sigmoid(gate_a) * (o_a @ w_branch_nsa[l])
                 + jax.nn.sigmoid(gate_b) * (o_b @ w_branch_moba[l]))
        x = x + gt1 * (mixed @ w_out[l])

        h = rms_norm(x, g_ffn_norm[l]) * (1 + sc2) + sh2
        a, v = jnp.split(h @ w_ffn_up[l], 2, axis=-1)
        y = jax.nn.gelu(causal_dwconv(a, conv_w[l], conv_b[l])) * v
        x = x + gt2 * (y @ w_ffn_down[l])
    return x
```

```python
import os
import numpy as np
from contextlib import ExitStack
import concourse.bass as bass
import concourse.mybir as mybir
from concourse.bass_utils import run_bass_kernel_spmd

F32 = mybir.dt.float32
BF16 = mybir.dt.bfloat16
I32 = mybir.dt.int32
AF = mybir.ActivationFunctionType
ALU = mybir.AluOpType
AX = mybir.AxisListType

S = 2048
D = 1024
DFF = 2816
INW = 4888
NCORE = 8
EPS = 1e-6
NEGB = -240000.0
ENG = ['sync', 'scalar', 'vector', 'gpsimd', 'tensor']
NSLOT = 8

O_QA, O_KC, O_VC, O_KSL, O_VSL, O_KWN, O_VWN, O_GN, O_QB, O_KB, O_VB, O_GA, O_GB = (
    0, 512, 640, 768, 896, 1024, 1152, 1280, 1304, 1816, 2328, 2840, 3864)


class Sched:
    def __init__(self, nc, es):
        self.nc = nc
        self.streams = {e: [] for e in ENG}
        self.sem = {}
        for e in ENG:
            self.sem[e] = es.enter_context(nc.semaphore('p_' + e))
        self.cnt = {e: 0 for e in ENG}
        self.dq = {'sync': 0, 'gpsimd': 0}
        for q in self.dq:
            for i in range(NSLOT):
                self.sem[(q, i)] = es.enter_context(nc.semaphore('d_%s%d' % (q, i)))
        self.seen = {e: {} for e in ENG}
        self.lw = {}
        self.rd = {}

    def _deps(self, reads, writes):
        deps = {}

        def add(tok):
            if tok is not None:
                deps[tok[0]] = max(deps.get(tok[0], 0), tok[1])
        for k in reads:
            add(self.lw.get(k))
        for k in writes:
            add(self.lw.get(k))
            for sk, v in self.rd.get(k, {}).items():
                add((sk, v))
        return deps

    def _emit_waits(self, eng, deps):
        for sk, v in deps.items():
            if eng == 'tensor' and sk == 'tensor':
                continue
            if self.seen[eng].get(sk, 0) < v:
                self.seen[eng][sk] = v
                h = self.sem[sk]
                self.streams[eng].append(lambda e, h=h, v=v: e.wait_ge(h, v))

    def _commit(self, tok, reads, writes):
        for k in reads:
            d = self.rd.setdefault(k, {})
            d[tok[0]] = max(d.get(tok[0], 0), tok[1])
        for k in writes:
            self.lw[k] = tok
            self.rd[k] = {}

    def group(self, eng, fns, reads=(), writes=()):
        deps = self._deps(reads, writes)
        self._emit_waits(eng, deps)
        self.cnt[eng] += 1
        h = self.sem[eng]
        for f in fns[:-1]:
            self.streams[eng].append(lambda e, f=f: f(e))
        f = fns[-1]
        self.streams[eng].append(lambda e, f=f, h=h: f(e).then_inc(h, 1))
        tok = (eng, self.cnt[eng])
        self._commit(tok, reads, writes)
        return tok

    def op(self, eng, fn, reads=(), writes=()):
        return self.group(eng, [fn], reads, writes)

    def dma(self, q, out, in_, reads=(), writes=()):
        n = self.dq[q]
        self.dq[q] += 1
        slot = n % NSLOT
        val = 16 * (n // NSLOT + 1)
        sk = (q, slot)
        deps = self._deps(reads, writes)
        if val > 16:
            deps[sk] = max(deps.get(sk, 0), val - 16)
        self._emit_waits(q, deps)
        h = self.sem[sk]
        self.streams[q].append(lambda e, out=out, in_=in_, h=h: e.dma_start(out=out, in_=in_).then_inc(h, 16))
        tok = (sk, val)
        self._commit(tok, reads, writes)
        return tok

    def _all_tokens(self):
        d = {e: self.cnt[e] for e in ENG if self.cnt[e] > 0}
        for q, n in self.dq.items():
            for slot in range(NSLOT):
                if n > slot:
                    d[(q, slot)] = 16 * ((n - 1 - slot) // NSLOT + 1)
        return d

    def barrier(self):
        allt = self._all_tokens()
        for e in ENG:
            deps = {k: v for k, v in allt.items() if k != e}
            self._emit_waits(e, deps)

    def finish(self):
        allt = self._all_tokens()
        self._emit_waits('sync', {k: v for k, v in allt.items() if k != 'sync'})


def _consts():
    c = {}
    c['ident'] = np.eye(128, dtype=np.float32)
    bd = np.zeros((128, 128), np.float32)
    bd[:64, :64] = 1
    bd[64:, 64:] = 1
    c['bd64'] = bd
    rot = np.zeros((128, 128), np.float32)
    for m in range(128):
        partner = m + 32 if (m % 64) < 32 else m - 32
        rot[partner, m] = 1
    c['rotm'] = rot
    p = np.arange(128)
    invf = (10000.0 ** (-(p % 32).astype(np.float64) / 32.0)) / (2 * np.pi)
    sgn = np.where((p % 64) < 32, -1.0, 1.0) * 6.28318
    c['invf'] = np.stack([invf, sgn, np.full(128, EPS), np.full(128, 6.28318)], 1).astype(np.float32)
    k = np.arange(128)[:, None]
    t = np.arange(128)[None, :]
    c['mlow'] = (k <= t).astype(np.float32)
    c['mup'] = (k > t).astype(np.float32)
    cm = np.zeros((128, 16, 128), np.float32)
    cc = np.arange(128)[:, None, None]
    tt = (np.arange(16)[None, :, None] * 128 + np.arange(128)[None, None, :])
    cm[:] = (16 * cc + 31 <= tt)
    cm[127] = 0
    c['cmpmask'] = cm
    ovl = np.zeros((128, 33), np.float32)
    ovl[:, 0] = 1
    cs = np.arange(127)[:, None] * 16
    js = np.arange(32)[None, :]
    ovl[:127, 1:] = ((cs < (js + 1) * 64) & (cs + 32 > js * 64)).astype(np.float32)
    c['ovl'] = ovl
    cn = np.zeros((128, 8, 32), np.float32)
    for qi in range(8):
        tq = (qi + 8) * 128 + np.arange(128)
        own = tq // 64
        jb = np.arange(32)[None, :]
        a = np.zeros((128, 32), np.float32)
        a[jb == 0 + 0 * own[:, None]] = 1e4
        a = np.where(jb == own[:, None], 2e4, a)
        a = np.where(jb == own[:, None] - 1, 3e4, a)
        a = np.where(jb > own[:, None], -1e30, a)
        cn[:, qi, :] = a
    c['cn'] = cn
    cmo = np.zeros((128, 8, 8), np.float32)
    for qi in range(8):
        cur = (qi + 8) // 2
        cmo[:, qi, cur:] = -1e30
    c['cmo'] = cmo
    kk = np.arange(2048)[None, :]
    c['ind32'] = (kk // 64 == np.arange(32)[:, None]).astype(np.float32)
    c['ind8'] = (kk // 256 == np.arange(8)[:, None]).astype(np.float32)
    oh = np.zeros((4, 4, 128), np.float32)
    for j in range(4):
        oh[j, j, :] = 1
    c['oh4'] = oh
    return c


CONST_SHAPES = {k: v.shape for k, v in _consts().items()}


class StopBuild(Exception):
    pass


def build(nb=4, dbg=None, stage=99):
    dbg = dbg or set()

    def stg(n):
        if stage <= n:
            raise StopBuild()
    nc = bass.Bass("TRN2", target_bir_lowering=False)
    es = ExitStack()
    uid = [0]

    def din(name, shape, dt=F32):
        return nc.dram_tensor(name, list(shape), dt, kind="ExternalInput").ap()

    def dscr(name, shape, dt=BF16):
        return nc.dram_tensor(name, list(shape), dt, kind="Internal").ap()

    def sbt(shape, dt, scope=None, name='t'):
        uid[0] += 1
        return (scope or es).enter_context(nc.sbuf_tensor("%s_%d" % (name, uid[0]), list(shape), dt))

    x_d = din("x", [nb, S, D])
    pos_d = din("pos", [nb, S], I32)
    cT_d = din("cT", [128, 8, 4])
    bada_d = din("bada", [1, 6144])
    badaT_d = din("badaT", [128, 48])
    gcol_d = din("gcol", [128, 16])
    hg_d = din("hg", [128, 8])
    cw_d = din("cw", [128, 22, 3])
    cb_d = din("cb", [128, 22])
    wposk_d = din("wposk", [64, 32])
    wposv_d = din("wposv", [64, 32])
    wada_d = din("w_ada", [1024, 6144])
    win_d = din("w_in", [1024, INW])
    w1k_d = din("w1k", [2048, 64])
    w2k_d = din("w2k", [64, 64])
    w1v_d = din("w1v", [2048, 64])
    w2v_d = din("w2v", [64, 64])
    wbn_d = din("wbn", [512, 1024])
    wbm_d = din("wbm", [512, 1024])
    wout_d = din("wout", [1024, 1024])
    wup_d = din("wup", [1024, 2 * DFF])
    wdn_d = din("wdn", [DFF, 1024])
    C = {k: din("c_" + k, shp) for k, shp in CONST_SHAPES.items()}
    out_d = nc.dram_tensor("out", [nb, S, D], F32, kind="ExternalOutput").ap()
    dbg_outs = {}

    wada_s = dscr("wada_s", [1024, 6144])
    win_s = dscr("win_s", [1024, INW])
    wbn_s = dscr("wbn_s", [512, 1024])
    wbm_s = dscr("wbm_s", [512, 1024])
    wout_s = dscr("wout_s", [1024, 1024])
    wup_s = dscr("wup_s", [1024, 2 * DFF])
    wdn_s = dscr("wdn_s", [DFF, 1024])
    x1_s = dscr("x1_s", [S, D], F32)
    gt_scr = dscr("gt_scr", [4, 2048], F32)

    sch = Sched(nc, es)
    ps = [es.enter_context(nc.psum_tensor("ps%d" % i, [128, 512], F32)) for i in range(8)]
    SB = [ps[0], ps[1]]
    OB = [ps[2], ps[3]]
    PJ = [ps[4], ps[5]]
    TRf = ps[6]
    MS = ps[7]
    TR = TRf[:].bitcast(BF16)
    TRB = [(TR, 'TR'), (MS[:].bitcast(BF16), 'MS')]
    SK = ['S0', 'S1']
    OK_ = ['O0', 'O1']
    PK = ['PJ0', 'PJ1']

    def act(out, in_, func, reads, writes, bias=None, scale=None, accum=None):
        kw = {}
        if bias is not None:
            kw['bias'] = bias
        if scale is not None:
            kw['scale'] = scale
        if accum is not None:
            kw['accum_out'] = accum
        return sch.op('scalar', lambda e: e.activation(out=out, in_=in_, func=func, **kw), reads, writes)

    def tt(eng, out, in0, in1, op, reads, writes):
        return sch.op(eng, lambda e: e.tensor_tensor(out=out, in0=in0, in1=in1, op=op), reads, writes)

    def ts(eng, out, in0, s1, s2, op0, op1, reads, writes):
        if s2 is None:
            return sch.op(eng, lambda e: e.tensor_scalar(out=out, in0=in0, scalar1=s1, scalar2=None, op0=op0), reads, writes)
        return sch.op(eng, lambda e: e.tensor_scalar(out=out, in0=in0, scalar1=s1, scalar2=s2, op0=op0, op1=op1), reads, writes)

    def stt(eng, out, in0, scalar, in1, op0, op1, reads, writes):
        return sch.op(eng, lambda e: e.scalar_tensor_tensor(out=out, in0=in0, scalar=scalar, in1=in1, op0=op0, op1=op1), reads, writes)

    def cp(eng, out, in_, reads, writes):
        if eng == 'scalar':
            return act(out, in_, AF.Copy, reads, writes)
        return sch.op(eng, lambda e: e.tensor_copy(out=out, in_=in_), reads, writes)

    def mset(eng, ap, val, writes):
        return sch.op(eng, lambda e: e.memset(ap, val), (), writes)

    def mm(out, lhsT, rhs, start=True, stop=True):
        return lambda e: e.matmul(out, lhsT, rhs, start=start, stop=stop)

    def mmg(out, pairs, reads, writes):
        n = len(pairs)
        return sch.group('tensor', [mm(out, l, r, i == 0, i == n - 1) for i, (l, r) in enumerate(pairs)], reads, writes)

    def trp(out, in_, ident_ap, reads, writes):
        return sch.op('tensor', lambda e: e.transpose(out, in_, ident_ap), reads, writes)

    def dump(name, ap, shape, dt, reads):
        if name not in dbg:
            return
        t = nc.dram_tensor("dbg_" + name, list(shape), dt, kind="ExternalOutput").ap()
        dbg_outs[name] = t
        sch.dma('sync', t, ap, reads=reads)

    def wkeys(name, n=8):
        return [(name, i) for i in range(n)]

    def conv(dst, src, rows, name):
        for r in range(0, rows, 128):
            sch.dma('gpsimd', dst[r:r + 128, :], src[r:r + 128, :], writes=[(name, r // 128)])

    conv(wada_s, wada_d, 1024, 'wada')
    cst = {}

    def cload(name, src, shape, dt, q=None):
        t = sbt(shape, dt, name=name)
        q = q or ('gpsimd' if dt == BF16 else 'sync')
        sch.dma(q, t[:], src, writes=[name])
        cst[name] = t
        return t

    ident = cload('ident', C['ident'], [128, 128], BF16)
    bd64 = cload('bd64', C['bd64'], [128, 128], BF16)
    rotm = cload('rotm', C['rotm'], [128, 128], BF16)
    invf = cload('invf', C['invf'], [128, 4], F32)
    mlow = cload('mlow', C['mlow'], [128, 128], BF16)
    mup = cload('mup', C['mup'], [128, 128], BF16)
    cmpmask = cload('cmpmask', C['cmpmask'], [128, 16, 128], BF16)
    cn_t = cload('cn', C['cn'], [128, 8, 32], F32)
    cmo_t = cload('cmo', C['cmo'], [128, 8, 8], F32)
    oh4 = cload('oh4', C['oh4'], [4, 4, 128], F32)
    cTb = cload('cTb', cT_d, [128, 8, 4], BF16)
    badaT = cload('badaT', badaT_d, [128, 48], F32)
    gcol = cload('gcol', gcol_d, [128, 16], F32)
    hg = cload('hg', hg_d, [128, 8], F32)
    cw = cload('cw', cw_d, [128, 22, 3], F32)
    cb = cload('cb', cb_d, [128, 22], F32)
    w1k = cload('w1k', w1k_d.rearrange("(l d) j -> d l j", d=64), [64, 32, 64], BF16)
    w1v = cload('w1v', w1v_d.rearrange("(l d) j -> d l j", d=64), [64, 32, 64], BF16)
    w2k = cload('w2k', w2k_d, [64, 64], BF16)
    w2v = cload('w2v', w2v_d, [64, 64], BF16)
    wposk = cload('wposk', wposk_d, [64, 32], BF16)
    wposv = cload('wposv', wposv_d, [64, 32], BF16)
    VCO = sbt([128, 97], BF16, name='VCO')
    sch.dma('gpsimd', VCO[:, 64:97], C['ovl'], writes=['VCO'])
    bada4 = sbt([4, 2048], F32, name='bada4')
    sch.dma('sync', bada4[:, 0:1024], bada_d[0:1, 2048:3072].to_broadcast([4, 1024]), writes=['bada4a'])
    sch.dma('sync', bada4[:, 1024:2048], bada_d[0:1, 5120:6144].to_broadcast([4, 1024]), writes=['bada4b'])

    conv(win_s, win_d, 1024, 'win')
    conv(wbn_s, wbn_d, 512, 'wbn')
    conv(wbm_s, wbm_d, 512, 'wbm')
    conv(wout_s, wout_d, 1024, 'wout')
    conv(wup_s, wup_d, 1024, 'wup')
    conv(wdn_s, wdn_d, DFF, 'wdn')

    modT = sbt([128, 32, 4], F32, name='modT')
    rows_gt = sbt([4, 2048], F32, name='rowsgt')
    a1 = sbt([128, 8, 4], F32, name='a1')
    a2 = sbt([128, 8, 4], F32, name='a2')
    cposk = sbt([64, 1], F32, name='cposk')
    cposv = sbt([64, 1], F32, name='cposv')
    with ExitStack() as sc:
        wts = [sbt([128, 8, 512], BF16, sc, 'wadat') for _ in range(2)]
        tmp84 = sbt([128, 8, 4], F32, sc, 'tmp84')
        for ct in range(12):
            wt = wts[ct % 2]
            wk = 'wadat%d' % (ct % 2)
            sch.dma('sync', wt[:], wada_s[:, ct * 512:(ct + 1) * 512].rearrange("(kc p) c -> p kc c", p=128),
                    reads=wkeys('wada'), writes=[wk])
            sec = ct // 2
            if sec in (2, 5):
                pj = PJ[ct % 2]
                mmg(pj[0:4, :], [(cTb[:, kc, :], wt[:, kc, :]) for kc in range(8)], [wk, 'cTb'], [PK[ct % 2]])
                gi = (0 if sec == 2 else 1) * 1024 + (ct % 2) * 512
                tt('vector', rows_gt[:, gi:gi + 512], pj[0:4, :], bada4[:, gi:gi + 512], ALU.add,
                   [PK[ct % 2], 'bada4a', 'bada4b'], [('rowsgt', gi // 512)])
            else:
                mi = {0: 0, 1: 8, 3: 16, 4: 24}[sec] + (ct % 2) * 4
                for j in range(4):
                    mmg(MS[:, j * 4:(j + 1) * 4], [(wt[:, kc, j * 128:(j + 1) * 128], cTb[:, kc, :]) for kc in range(8)],
                        [wk, 'cTb'], ['MS'])
                tt('vector', modT[:, mi:mi + 4, :], MS[:, 0:16].rearrange("p (j b) -> p j b", j=4),
                   badaT[:, ct * 4:ct * 4 + 4].unsqueeze(2).to_broadcast([128, 4, 4]), ALU.add,
                   ['MS', 'badaT'], [('modT', mi // 4)])
        ts('vector', tmp84[:], modT[:, 8:16, :], 1.0, None, ALU.add, None, [('modT', 2), ('modT', 3)], ['tmp84'])
        tt('vector', a1[:], tmp84[:], gcol[:, 0:8].unsqueeze(2).to_broadcast([128, 8, 4]), ALU.mult, ['tmp84', 'gcol'], ['a1'])
        ts('vector', tmp84[:], modT[:, 24:32, :], 1.0, None, ALU.add, None, [('modT', 6), ('modT', 7)], ['tmp84'])
        tt('vector', a2[:], tmp84[:], gcol[:, 8:16].unsqueeze(2).to_broadcast([128, 8, 4]), ALU.mult, ['tmp84', 'gcol'], ['a2'])
        for (w1t, wpt, cpo, nm) in ((w1k, wposk, cposk, 'k'), (w1v, wposv, cposv, 'v')):
            mmg(MS[0:64, 0:1], [(w1t[:, l, :], wpt[:, l:l + 1]) for l in range(32)], ['w1' + nm, 'wpos' + nm], ['MS'])
            cp('vector', cpo[:], MS[0:64, 0:1], ['MS'], ['cpos' + nm])
    sch.dma('sync', gt_scr[:, :], rows_gt[:], reads=[('rowsgt', i) for i in range(4)], writes=['gtscr'])
    b1 = modT[:, 0:8, :]
    b2 = modT[:, 16:24, :]
    MODK = [('modT', i) for i in range(8)] + ['a1', 'a2']
    sch.barrier()
    stage0 = stage <= 0

    def norm_transpose(src_tile, srck, aa, bb, b, tti, tmp):
        junk, ss, lnv, rstd, xn = tmp
        CUT = int(os.environ.get('PHASEA_CUT', '99'))
        mset('vector', ss[:], 0.0, ['ss'])
        act(junk[:], src_tile, AF.Square, srck + ['ss'], ['junk', 'ss'], accum=ss[:, 0:1])
        if CUT < 1:
            return
        act(lnv[:], ss[:], AF.Ln, ['ss'], ['lnv'], bias=invf[:, 2:3], scale=1.0 / D)
        act(rstd[:], lnv[:], AF.Exp, ['lnv'], ['rstd'], scale=-0.5)
        if CUT < 2:
            return
        ts('vector', xn[:], src_tile, rstd[:, 0:1], None, ALU.mult, None, srck + ['rstd'], ['xn'])
        if CUT < 3:
            return
        TRx, trk = TRB[tti % 2]
        sch.group('tensor', [(lambda e, c=c: e.transpose(TRx[:, c * 128:(c + 1) * 128], xn[:, c * 128:(c + 1) * 128], ident[:])) for c in range(8)],
                  ['xn', 'ident'], [trk])
        if CUT < 4:
            return
        for c in range(8):
            dst = hT[:, c, tti * 128:(tti + 1) * 128]
            if True:
                act(dst, TRx[:, c * 128:(c + 1) * 128], AF.Identity, [trk] + MODK, [('hT', c, tti // 4)],
                    bias=bb[:, c, b:b + 1], scale=aa[:, c, b:b + 1])
            else:
                stt('vector', dst, TRx[:, c * 128:(c + 1) * 128], aa[:, c, b:b + 1], bb[:, c, b:b + 1].to_broadcast([128, 128]), ALU.mult, ALU.add,
                    [trk] + MODK, [('hT', c, tti // 4)])

    def hTk(tb):
        return [('hT', c, tb) for c in range(8)]

    for b in range(0 if stage0 else nb):
      bs = ExitStack()
      ms = ExitStack()
      asx = ExitStack()
      try:
            cosT = sbt([128, S], BF16, bs, 'cosT')
            sinS = sbt([128, S], BF16, bs, 'sinS')
            gtbc = sbt([128, 2048], F32, bs, 'gtbc')
            hT = sbt([128, 8, S], BF16, bs, 'hT')
            with ExitStack() as sc:
                posi = sbt([128, S], I32, sc, 'posi')
                u = sbt([128, S], F32, sc, 'u')
                ni = sbt([128, S], I32, sc, 'ni')
                nf = sbt([128, S], F32, sc, 'nf')
                sch.dma('sync', posi[:], pos_d[b:b + 1, :].to_broadcast([128, S]), writes=['posi'])
                cp('vector', u[:], posi[:], ['posi'], ['u'])
                ts('vector', u[:], u[:], invf[:, 0:1], None, ALU.mult, None, ['u', 'invf'], ['u'])
                cp('vector', ni[:], u[:], ['u'], ['ni'])
                cp('vector', nf[:], ni[:], ['ni'], ['nf'])
                tt('vector', nf[:], u[:], nf[:], ALU.subtract, ['u', 'nf'], ['nf'])
                act(sinS[:], nf[:], AF.Sin, ['nf', 'invf'], ['sinS'], scale=invf[:, 1:2])
                ts('vector', u[:], u[:], 0.25, None, ALU.add, None, ['u'], ['u'])
                cp('vector', ni[:], u[:], ['u'], ['ni'])
                cp('vector', nf[:], ni[:], ['ni'], ['nf'])
                tt('vector', nf[:], u[:], nf[:], ALU.subtract, ['u', 'nf'], ['nf'])
                act(cosT[:], nf[:], AF.Sin, ['nf', 'invf'], ['cosT'], scale=invf[:, 3:4])
                sch.barrier()
            stg(0.3)
            sch.dma('sync', gtbc[:], gt_scr[b:b + 1, :].to_broadcast([128, 2048]), reads=['gtscr'], writes=[('gtbc', i) for i in range(4)])
            stg(0.6)
            with ExitStack() as sc:
                xts = [sbt([128, D], F32, sc, 'xt') for _ in range(2)]
                tmpA = (sbt([128, D], BF16, sc, 'junk'), sbt([128, 1], F32, sc, 'ss'), sbt([128, 1], F32, sc, 'lnv'),
                        sbt([128, 1], F32, sc, 'rstd'), sbt([128, D], BF16, sc, 'xn'))
                for tti in range(16):
                    xt = xts[tti % 2]
                    sch.dma('sync', xt[:], x_d[b, tti * 128:(tti + 1) * 128, :], writes=['xt%d' % (tti % 2)])
                    norm_transpose(xt[:], ['xt%d' % (tti % 2)], a1, b1, b, tti, tmpA)
                sch.barrier()
            dump('hT', hT[:], [128, 8, S], BF16, [k for tb in range(4) for k in hTk(tb)])
            dump('cosT', cosT[:], [128, S], BF16, ['cosT'])
            dump('sinS', sinS[:], [128, S], BF16, ['sinS'])
            stg(1)

            oT = sbt([128, 8, S], BF16, ms, 'oT')
            Qa = sbt([96, 4, S], BF16, asx, 'Qa')
            Kb = sbt([96, 4, S], BF16, asx, 'Kb')
            Vb = sbt([128, 16, 4, 65], BF16, asx, 'Vb')
            oacc = sbt([128, 16, 4, 64], F32, asx, 'oacc')
            gates = sbt([128, 16, 12], F32, asx, 'gates')
            wt_fm = [sbt([128, 8, 128], BF16, asx, 'wtfm') for _ in range(2)]
            wt_tm = sbt([128, 8, 256], BF16, asx, 'wttm')
            qn = sbt([128, 512], BF16, asx, 'qn')
            sq = sbt([128, 512], BF16, asx, 'sq')
            t3 = sbt([128, 512], BF16, asx, 't3')
            t1 = sbt([128, 512], F32, asx, 't1')
            t2 = sbt([128, 512], F32, asx, 't2')
            lnt = sbt([128, 512], F32, asx, 'lnt')
            rst = sbt([128, 512], F32, asx, 'rst')
            Pt = [sbt([128, 512], BF16, asx, 'P') for _ in range(3)]
            Tst = sbt([128, 96], BF16, asx, 'Tst')
            T2 = sbt([128, 4, 72], BF16, asx, 'T2')
            hid = sbt([64, 128], BF16, asx, 'hid')
            kcn = sbt([64, 128], BF16, asx, 'kcn')
            sm = sbt([128, 64], F32, asx, 'sm')
            imp3 = sbt([128, 4, 32], F32, asx, 'imp3')
            impm = sbt([128, 32], F32, asx, 'impm')
            impr = sbt([128, 32], F32, asx, 'impr')
            m8 = sbt([128, 4, 8], F32, asx, 'm8')
            gm = sbt([128, 4, 8], F32, asx, 'gm')
            lt8 = sbt([128, 4, 8], F32, asx, 'lt8')
            ksf = sbt([64, 4, 8], F32, asx, 'ksf')
            ksb = sbt([64, 4, 8], BF16, asx, 'ksb')
            otmp = sbt([128, 4, 64], F32, asx, 'otmp')
            obf = sbt([128, 4, 256], BF16, asx, 'obf')
            junkT = sbt([128, 1024], BF16, asx, 'junkT')
            mset('vector', Vb[:, :, :, 64:65], 1.0, ['Vb_ones'])
            mset('vector', Tst[:], 0.0, ['Tst'])
            mset('vector', T2[:], 0.0, ['T2'])
            cnt = {'pj': 0, 's': 0, 'o': 0, 'p': 0, 'w': 0}

            def nxt(k, n):
                v = cnt[k] % n
                cnt[k] += 1
                return v

            def load_w_fm(pieces):
                i = nxt('w', 2)
                wt = wt_fm[i]
                c = 0
                for (c0, n) in pieces:
                    sch.dma('sync', wt[:, :, c:c + n], win_s[:, c0:c0 + n].rearrange("(kc p) c -> p kc c", p=128),
                            reads=wkeys('win'), writes=[('wtfm', i, c)])
                    c += n
                return wt, [('wtfm', i, cc) for cc in (0, 64)]

            def proj_fm(pieces, norm, gaincol, dests):
                wt, wk = load_w_fm(pieces)
                anyrope = any(k == 'rope' for k, _, _ in dests)
                for tb in range(4):
                    pi = nxt('pj', 2)
                    pj = PJ[pi]
                    mmg(pj[:, :], [(wt[:, kc, :], hT[:, kc, tb * 512:(tb + 1) * 512]) for kc in range(8)],
                        wk + hTk(tb), [PK[pi]])
                    if norm:
                        act(sq[:], pj[:, :], AF.Square, [PK[pi]], ['sq'])
                        sch.op('tensor', mm(MS[:, :], bd64[:], sq[:]), ['sq', 'bd64'], ['MS'])
                        act(lnt[:], MS[:, :], AF.Ln, ['MS'], ['lnt'], bias=invf[:, 2:3], scale=1.0 / 64)
                        act(rst[:], lnt[:], AF.Exp, ['lnt'], ['rst'], scale=-0.5)
                        stt('vector', qn[:], pj[:, :], gaincol, rst[:], ALU.mult, ALU.mult, [PK[pi], 'rst', 'hg'], ['qn'])
                    else:
                        cp('scalar', qn[:], pj[:, :], [PK[pi]], ['qn'])
                    if anyrope:
                        si = nxt('s', 2)
                        sch.op('tensor', mm(SB[si][:, :], rotm[:], qn[:]), ['qn', 'rotm'], [SK[si]])
                        tt('gpsimd', t1[:], qn[:], cosT[:, tb * 512:(tb + 1) * 512], ALU.mult, ['qn', 'cosT'], ['t1'])
                        tt('vector', t2[:], SB[si][:, :], sinS[:, tb * 512:(tb + 1) * 512], ALU.mult, [SK[si], 'sinS'], ['t2'])
                        tt('vector', t3[:], t1[:], t2[:], ALU.add, ['t1', 't2'], ['t3'])
                    for hf, (kind, dfn, kfn) in enumerate(dests):
                        src = t3 if kind == 'rope' else qn
                        sk = 't3' if kind == 'rope' else 'qn'
                        pr = slice(0, 64) if hf == 0 else slice(64, 128)
                        if hf == 0:
                            cp('gpsimd', dfn(tb), src[pr, :], [sk], kfn(tb))
                        else:
                            cp('scalar', dfn(tb), src[pr, :], [sk], kfn(tb))

            def tblk(tb):
                return slice(tb * 512, (tb + 1) * 512)

            def qkeys(h, tb):
                return [('Qa', h, 4 * tb + i) for i in range(4)]

            def kkeys(i, tb):
                return [('Kb', i, 4 * tb + j) for j in range(4)]

            def attn_round(units, vfn, ofirst_key, norm_fn):
                oi = nxt('o', 2)
                O = OB[oi]
                first = True
                for un in units:
                    si = nxt('s', 2)
                    Sb = SB[si]
                    n, ncol = un['n'], un['ncol']
                    sch.op('tensor', mm(Sb[0:n, 0:ncol], un['lhsT'], un['rhs']), un['reads'], [SK[si]])
                    pi = nxt('p', 3)
                    P = Pt[pi]
                    pk = 'P%d' % pi
                    act(P[0:n, 0:ncol], Sb[0:n, 0:ncol], AF.Exp, [SK[si]], [pk], scale=0.125)
                    if un.get('mask') is not None:
                        mo, mi_, mk = un['mask'](P)
                        tt('gpsimd', mo, mo, mi_, ALU.mult, [pk, mk], [pk])
                    fns = []
                    for (c0, oreg) in un['pv']:
                        fns.append(mm(oreg(O), P[0:n, c0:c0 + 128], un['v'], first, True))
                        first = False
                    sch.group('tensor', fns, [pk] + un['vreads'], [OK_[oi]])
                norm_fn(O, OK_[oi])

            for pas in range(4):
                is_nsa = pas < 2
                g = pas % 2
                if is_nsa:
                    for i in range(2):
                        c0 = O_QA + g * 256 + i * 128
                        proj_fm([(c0, 128)], True, hg[:, 0:1],
                                [('rope', (lambda tb, h=2 * i: Qa[0:64, h, tblk(tb)]), (lambda tb, h=2 * i: qkeys(h, tb))),
                                 ('rope', (lambda tb, h=2 * i + 1: Qa[0:64, h, tblk(tb)]), (lambda tb, h=2 * i + 1: qkeys(h, tb)))])
                    proj_fm([(O_KSL + g * 64, 64), (O_KWN + g * 64, 64)], True, hg[:, 1:2],
                            [('rope', (lambda tb: Kb[0:64, 0, tblk(tb)]), (lambda tb: kkeys(0, tb))),
                             ('rope', (lambda tb: Kb[0:64, 1, tblk(tb)]), (lambda tb: kkeys(1, tb)))])
                    proj_fm([(O_KC + g * 64, 64), (O_VC + g * 64, 64)], False, None,
                            [('rope', (lambda tb: Kb[0:64, 2, tblk(tb)]), (lambda tb: kkeys(2, tb))),
                             ('copy', (lambda tb: Kb[0:64, 3, tblk(tb)]), (lambda tb: kkeys(3, tb)))])
                    sch.dma('gpsimd', Kb[64:96, 0, :], C['ind32'], writes=[('KbI', 0)])
                    wtk = []
                    for (c0, n, cc) in ((O_VSL + g * 64, 64, 0), (O_VWN + g * 64, 64, 64), (O_GN + g * 12, 12, 128)):
                        sch.dma('sync', wt_tm[:, :, cc:cc + n], win_s[:, c0:c0 + n].rearrange("(kc p) c -> p kc c", p=128),
                                reads=wkeys('win'), writes=[('wttm', cc)])
                        wtk.append(('wttm', cc))
                    for tti in range(16):
                        pi = nxt('pj', 2)
                        pj = PJ[pi]
                        mmg(pj[:, 0:140], [(hT[:, kc, tti * 128:(tti + 1) * 128], wt_tm[:, kc, 0:140]) for kc in range(8)],
                            wtk + hTk(tti // 4), [PK[pi]])
                        cp('vector', Vb[:, tti, 0:2, 0:64], pj[:, 0:128].rearrange("p (a d) -> p a d", a=2), [PK[pi]], [('Vb', tti)])
                        act(gates[:, tti, :], pj[:, 128:140], AF.Sigmoid, [PK[pi]], [('gates', tti)])
                    KC_ALL = [k for tb in range(4) for k in kkeys(2, tb)]
                    VC_ALL = [k for tb in range(4) for k in kkeys(3, tb)]
                    for (hi, w1t, cpo, nm, kall) in ((2, w1k, cposk, 'k', KC_ALL), (3, w1v, cposv, 'v', VC_ALL)):
                        mmg(MS[0:64, 0:127], [(w1t[:, l, :], Kb[0:64, hi, l:l + 16 * 126 + 1:16]) for l in range(32)],
                            kall + ['w1' + nm], ['MS'])
                        act(hid[:, 0:127], MS[0:64, 0:127], AF.Gelu_apprx_tanh, ['MS', 'cpos' + nm], ['hid'], bias=cpo[:, 0:1])
                        if nm == 'k':
                            si = nxt('s', 2)
                            sch.op('tensor', mm(SB[si][0:64, 0:127], w2k[:], hid[:, 0:127]), ['hid', 'w2k'], [SK[si]])
                            act(sq[0:64, 0:127], SB[si][0:64, 0:127], AF.Square, [SK[si]], ['sq'])
                            sch.op('tensor', mm(MS[0:64, 0:127], bd64[0:64, 0:64], sq[0:64, 0:127]), ['sq', 'bd64'], ['MS'])
                            act(lnt[0:64, 0:127], MS[0:64, 0:127], AF.Ln, ['MS'], ['lnt'], bias=invf[0:64, 2:3], scale=1.0 / 64)
                            act(rst[0:64, 0:127], lnt[0:64, 0:127], AF.Exp, ['lnt'], ['rst'], scale=-0.5)
                            stt('vector', kcn[:, 0:127], SB[si][0:64, 0:127], hg[0:64, 4:5], rst[0:64, 0:127], ALU.mult, ALU.mult,
                                [SK[si], 'rst', 'hg'], ['kcn'])
                        else:
                            si = nxt('s', 2)
                            sch.op('tensor', mm(SB[si][0:127, 0:64], hid[:, 0:127], w2v[:]), ['hid', 'w2v'], [SK[si]])
                            cp('vector', VCO[0:127, 0:64], SB[si][0:127, 0:64], [SK[si]], ['VCOv'])
                    if b == 0 and pas == 0:
                        dump('Qa', Qa[0:64, :, :], [64, 4, S], BF16, [k for h in range(4) for tb in range(4) for k in qkeys(h, tb)])
                        dump('Kb', Kb[0:64, :, :], [64, 4, S], BF16, [k for h in range(4) for tb in range(4) for k in kkeys(h, tb)])
                        dump('Vb', Vb[:], [128, 16, 4, 65], BF16, [('Vb', i) for i in range(16)] + ['Vb_ones'])
                        dump('gates', gates[:], [128, 16, 12], F32, [('gates', i) for i in range(16)])
                        dump('kcn', kcn[:], [64, 128], BF16, ['kcn'])
                        dump('VCO', VCO[:], [128, 97], BF16, ['VCOv', 'VCO'])
                        stg(2)

                    def nsa_norm(qt, br, first_branch, imp_out):
                        def fn(O, ok):
                            Ov = O[:, 0:388].rearrange("p (r c) -> p r c", r=4) if br == 0 else \
                                O[:, 0:260].rearrange("p (r c) -> p r c", r=4)
                            ts('vector', sm[:, 0:4], Ov[:, :, 64], 1e-30, None, ALU.max, None, [ok], ['sm0'])
                            sch.op('vector', lambda e: e.reciprocal(out=sm[:, 4:8], in_=sm[:, 0:4]), ['sm0'], ['sm1'])
                            tt('vector', sm[:, 8:12], sm[:, 4:8], gates[:, qt, br:12:3], ALU.mult, ['sm1', ('gates', qt)], ['sm2'])
                            fb = sm[:, 8:12].unsqueeze(2).to_broadcast([128, 4, 64])
                            if first_branch:
                                tt('vector', oacc[:, qt, :, :], Ov[:, :, 0:64], fb, ALU.mult, [ok, 'sm2'], [('oacc', qt)])
                            else:
                                tt('vector', otmp[:], Ov[:, :, 0:64], fb, ALU.mult, [ok, 'sm2'], ['otmp'])
                                tt('gpsimd', oacc[:, qt, :, :], oacc[:, qt, :, :], otmp[:], ALU.add, ['otmp', ('oacc', qt)], [('oacc', qt)])
                            if imp_out:
                                tt('vector', imp3[:], Ov[:, :, 65:97], sm[:, 4:8].unsqueeze(2).to_broadcast([128, 4, 32]), ALU.mult,
                                   [ok, 'sm1'], ['imp3'])
                        return fn

                    for qt in range(16):
                        ncv = min(127, 8 * qt + 7)
                        qs = slice(qt * 128, (qt + 1) * 128)
                        unit = dict(lhsT=kcn[:, 0:ncv], rhs=Qa[0:64, 0:4, qs], n=ncv, ncol=512,
                                    reads=['kcn'] + [('Qa', h, qt) for h in range(4)],
                                    mask=(lambda P, ncv=ncv, qt=qt: (P[0:ncv, :].rearrange("p (r t) -> p r t", r=4),
                                                                   cmpmask[0:ncv, qt, :].unsqueeze(1).to_broadcast([ncv, 4, 128]), 'cmpmask')),
                                    pv=[(r * 128, (lambda O, r=r: O[:, r * 97:(r + 1) * 97])) for r in range(4)],
                                    v=VCO[0:ncv, 0:97], vreads=['VCOv', 'VCO'])
                        attn_round([unit], None, None, nsa_norm(qt, 0, True, qt >= 8))
                        if qt >= 8:
                            sch.op('vector', lambda e: e.tensor_reduce(out=impr[:], in_=imp3[:].rearrange("p r j -> p j r"),
                                                                        axis=AX.X, op=ALU.add), ['imp3'], ['impr'])
                            tt('vector', impm[:], impr[:], cn_t[:, qt - 8, :], ALU.add, ['impr', 'cn'], ['impm'])
                            sch.op('vector', lambda e: e.max(out=m8[:, 0, :], in_=impm[:]), ['impm'], ['m8'])
                            sch.op('vector', lambda e: e.match_replace(out=impr[:], in_to_replace=m8[:, 0, :], in_values=impm[:],
                                                                        imm_value=-3e38), ['impm', 'm8'], ['impr'])
                            sch.op('vector', lambda e: e.max(out=m8[:, 1, :], in_=impr[:]), ['impr', 'm8'], ['m8'])
                            ts('vector', Tst[:, 64:96], impm[:], m8[:, 1, 7:8], NEGB, ALU.is_lt, ALU.mult, ['impm', 'm8'], ['Tst'])
                            trp(TR[0:96, 0:128], Tst[:], ident[:], ['Tst', 'ident'], ['TR'])
                            cp('scalar', Qa[64:96, 0:4, qs], TR[64:96, 0:128].unsqueeze(1).to_broadcast([32, 4, 128]),
                               ['TR'], [('QaB', qt)])
                    for qt in range(16):
                        qs = slice(qt * 128, (qt + 1) * 128)
                        units = []
                        for kt in range(max(0, qt - 4), qt + 1):
                            ks = slice(kt * 128, (kt + 1) * 128)
                            mk = None
                            if kt == qt:
                                mk = (lambda P: (P[:, :].rearrange("p (r t) -> p r t", r=4), mlow[:].unsqueeze(1).to_broadcast([128, 4, 128]), 'mlow'))
                            elif kt == qt - 4:
                                mk = (lambda P: (P[:, :].rearrange("p (r t) -> p r t", r=4), mup[:].unsqueeze(1).to_broadcast([128, 4, 128]), 'mup'))
                            units.append(dict(lhsT=Kb[0:64, 1, ks], rhs=Qa[0:64, 0:4, qs], n=128, ncol=512,
                                              reads=[('Kb', 1, kt)] + [('Qa', h, qt) for h in range(4)], mask=mk,
                                              pv=[(r * 128, (lambda O, r=r: O[:, r * 65:(r + 1) * 65])) for r in range(4)],
                                              v=Vb[:, kt, 1, :], vreads=[('Vb', kt), 'Vb_ones']))
                        attn_round(units, None, None, nsa_norm(qt, 2, False, False))
                    for qt in range(16):
                        qs = slice(qt * 128, (qt + 1) * 128)
                        kd = 64 if qt < 8 else 96
                        units = []
                        for kt in range(0, qt + 1):
                            ks = slice(kt * 128, (kt + 1) * 128)
                            mk = None
                            if kt == qt:
                                mk = (lambda P: (P[:, :].rearrange("p (r t) -> p r t", r=4), mlow[:].unsqueeze(1).to_broadcast([128, 4, 128]), 'mlow'))
                            rd = [('Kb', 0, kt)] + [('Qa', h, qt) for h in range(4)]
                            if kd == 96:
                                rd += [('KbI', 0), ('QaB', qt)]
                            units.append(dict(lhsT=Kb[0:kd, 0, ks], rhs=Qa[0:kd, 0:4, qs], n=128, ncol=512, reads=rd, mask=mk,
                                              pv=[(r * 128, (lambda O, r=r: O[:, r * 65:(r + 1) * 65])) for r in range(4)],
                                              v=Vb[:, kt, 0, :], vreads=[('Vb', kt), 'Vb_ones']))
                        attn_round(units, None, None, nsa_norm(qt, 1, False, False))
                    chunk0 = 2 * g
                else:
                    hgp = g
                    for i in range(2):
                        c0 = O_QB + (4 * hgp + 2 * i) * 64
                        proj_fm([(c0, 128)], True, hg[:, 2:3],
                                [('rope', (lambda tb, h=2 * i: Qa[0:64, h, tblk(tb)]), (lambda tb, h=2 * i: qkeys(h, tb))),
                                 ('rope', (lambda tb, h=2 * i + 1: Qa[0:64, h, tblk(tb)]), (lambda tb, h=2 * i + 1: qkeys(h, tb)))])
                    for i in range(2):
                        c0 = O_KB + (4 * hgp + 2 * i) * 64
                        proj_fm([(c0, 128)], True, hg[:, 3:4],
                                [('rope', (lambda tb, h=2 * i: Kb[0:64, h, tblk(tb)]), (lambda tb, h=2 * i: kkeys(h, tb))),
                                 ('rope', (lambda tb, h=2 * i + 1: Kb[0:64, h, tblk(tb)]), (lambda tb, h=2 * i + 1: kkeys(h, tb)))])
                    for h in range(4):
                        sch.dma('gpsimd', Kb[64:72, h, :], C['ind8'], writes=[('KbI', h)])
                    c0 = O_VB + hgp * 256
                    sch.dma('sync', wt_tm[:, :, 0:256], win_s[:, c0:c0 + 256].rearrange("(kc p) c -> p kc c", p=128),
                            reads=wkeys('win'), writes=[('wttm', 0), ('wttm', 64), ('wttm', 128)])
                    for tti in range(16):
                        pi = nxt('pj', 2)
                        pj = PJ[pi]
                        mmg(pj[:, 0:256], [(hT[:, kc, tti * 128:(tti + 1) * 128], wt_tm[:, kc, 0:256]) for kc in range(8)],
                            [('wttm', 0), ('wttm', 64), ('wttm', 128)] + hTk(tti // 4), [PK[pi]])
                        cp('vector', Vb[:, tti, :, 0:64], pj[:, 0:256].rearrange("p (a d) -> p a d", a=4), [PK[pi]], [('Vb', tti)])
                    stg(3.65)
                    for h in range(4):
                        sch.op('vector', lambda e, h=h: e.tensor_reduce(out=ksf[:, h, :], in_=Kb[0:64, h, :].rearrange("p (j k) -> p j k", k=256),
                                                                        axis=AX.X, op=ALU.add),
                               [k for tb in range(4) for k in kkeys(h, tb)], [('ksf', h)])
                    cp('vector', ksb[:], ksf[:], [('ksf', h) for h in range(4)], ['ksb'])
                    for qt in range(8, 16):
                        qs = slice(qt * 128, (qt + 1) * 128)
                        cur = qt // 2
                        for h in range(4):
                            sch.op('tensor', mm(MS[:, h * 8:(h + 1) * 8], Qa[0:64, h, qs], ksb[:, h, :]), [('Qa', h, qt), 'ksb'], ['MS'])
                        tt('vector', gm[:], MS[:, 0:32].rearrange("p (h j) -> p h j", h=4),
                           cmo_t[:, qt - 8, :].unsqueeze(1).to_broadcast([128, 4, 8]), ALU.add, ['MS', 'cmo'], ['gm'])
                        for h in range(4):
                            sch.op('vector', lambda e, h=h: e.max(out=m8[:, h, :], in_=gm[:, h, :]), ['gm', 'm8'], ['m8'])
                        tt('vector', lt8[:], gm[:], m8[:, :, 2:3].to_broadcast([128, 4, 8]), ALU.is_lt, ['gm', 'm8'], ['lt8'])
                        ts('vector', T2[:, :, 64:72], lt8[:], NEGB, None, ALU.mult, None, ['lt8'], ['T2'])
                        mset('vector', T2[:, :, 64 + cur:65 + cur], 0.0, ['T2'])
                        sch.group('tensor', [(lambda e, h=h: e.transpose(TR[0:72, h * 128:(h + 1) * 128], T2[:, h, :], ident[:])) for h in range(4)],
                                  ['T2', 'ident'], ['TR'])
                        cp('scalar', Qa[64:72, 0:4, qs], TR[64:72, 0:512].rearrange("p (h t) -> p h t", h=4),
                           ['TR'], [('QaB', qt)])
                    if b == 0 and pas == 2:
                        dump('Qm', Qa[0:72, :, :], [72, 4, S], BF16, [k for h in range(4) for tb in range(4) for k in qkeys(h, tb)] + [('QaB', q) for q in range(8, 16)])
                        dump('Km', Kb[0:72, :, :], [72, 4, S], BF16, [k for h in range(4) for tb in range(4) for k in kkeys(h, tb)] + [('KbI', h) for h in range(4)])
                    stg(3.7)
                    for h in range(4):
                        for QB in range(4):
                            kd = 64 if QB < 2 else 72
                            units = []
                            for kt in range(0, 4 * QB + 4):
                                ql0 = max(0, kt - 4 * QB)
                                ncol = (4 - ql0) * 128
                                ks = slice(kt * 128, (kt + 1) * 128)
                                q0 = (4 * QB + ql0) * 128
                                mk = None
                                if kt >= 4 * QB:
                                    mk = (lambda P: (P[:, 0:128], mlow[:], 'mlow'))
                                rd = [('Kb', h, kt)] + [('Qa', h, 4 * QB + ql) for ql in range(ql0, 4)]
                                if kd == 72:
                                    rd += [('KbI', h)] + [('QaB', 4 * QB + ql) for ql in range(ql0, 4)]
                                units.append(dict(lhsT=Kb[0:kd, h, ks], rhs=Qa[0:kd, h, q0:(4 * QB + 4) * 128], n=128, ncol=ncol, reads=rd, mask=mk,
                                                  pv=[((ql - ql0) * 128, (lambda O, ql=ql: O[:, ql * 65:(ql + 1) * 65])) for ql in range(ql0, 4)],
                                                  v=Vb[:, kt, h, :], vreads=[('Vb', kt), 'Vb_ones']))

                            def mnorm(O, ok, h=h, QB=QB):
                                Ov = O[:, 0:260].rearrange("p (r c) -> p r c", r=4)
                                sch.op('vector', lambda e: e.reciprocal(out=sm[:, 4:8], in_=Ov[:, :, 64]), [ok], ['sm1'])
                                tt('vector', oacc[:, 4 * QB:4 * QB + 4, h, :], Ov[:, :, 0:64],
                                   sm[:, 4:8].unsqueeze(2).to_broadcast([128, 4, 64]), ALU.mult, [ok, 'sm1'],
                                   [('oacc', 4 * QB + i) for i in range(4)])
                            attn_round(units, None, None, mnorm)
                    chunk0 = 4 + 2 * g
                if b == 0 and pas in (0, 2):
                    dump('oacc%d' % pas, oacc[:], [128, 16, 4, 64], F32, [('oacc', i) for i in range(16)])
                    if pas == 0:
                        stg(3)
                for q4 in range(4):
                    cp('vector', obf[:], oacc[:, 4 * q4:4 * q4 + 4, :, :].rearrange("p q h d -> p q (h d)"),
                       [('oacc', 4 * q4 + i) for i in range(4)], ['obf'])
                    OTC = int(os.environ.get('OT_CUT', '9'))
                    if OTC < 1:
                        continue
                    TRx, trk = TRB[q4 % 2]
                    sch.group('tensor', [(lambda e, ci=ci, ql=ql, TRx=TRx: e.transpose(TRx[:, (ci * 4 + ql) * 128:(ci * 4 + ql + 1) * 128],
                                                                              obf[:, ql, ci * 128:(ci + 1) * 128], ident[:]))
                                         for ci in range(2) for ql in range(4)], ['obf', 'ident'], [trk])
                    if OTC == 3:
                        mset('vector', oT[:, chunk0, q4 * 512:(q4 + 1) * 512], 1.0, [('oT', chunk0, q4)])
                        continue
                    if OTC == 4:
                        act(oT[:, chunk0, q4 * 512:q4 * 512 + 128], obf[:, 0, 0:128], AF.Identity, ['obf'], [('oT', chunk0, q4)])
                        continue
                    if OTC == 6:
                        for k8 in range(8):
                            act(junkT[:, k8 * 128:(k8 + 1) * 128], TRx[:, k8 * 128:(k8 + 1) * 128], AF.Identity, [trk], ['junkT'])
                        for ci in range(2):
                            cp('gpsimd', oT[:, chunk0 + ci, q4 * 512:(q4 + 1) * 512], junkT[:, ci * 512:(ci + 1) * 512], ['junkT'], [('oT', chunk0 + ci, q4)])
                        continue
                    if OTC == 5:
                        act(t3[:, 0:128], TRx[:, 0:128], AF.Identity, [trk], ['t3'])
                        continue
                    for ci in range(2 if OTC >= 2 else 0):
                        for ql in range(4):
                            act(oT[:, chunk0 + ci, (q4 * 4 + ql) * 128:(q4 * 4 + ql + 1) * 128], TRx[:, (ci * 4 + ql) * 128:(ci * 4 + ql + 1) * 128],
                                AF.Identity, [trk], [('oT', chunk0 + ci, q4)])
                stg(3.2 + 0.2 * pas)
            sch.barrier()
            asx.close()
            dump('oT', oT[:], [128, 8, S], BF16, [('oT', c, q) for c in range(8) for q in range(4)])
            stg(4)

            with ExitStack() as xs:
                mixT = sbt([128, 8, S], BF16, xs, 'mixT')
                woutt = sbt([128, 8, 1024], BF16, xs, 'woutt')
                wb_t = [[sbt([128, 4, 128], BF16, xs, 'wbt') for _ in range(2)] for _ in range(2)]
                wg_t = [[sbt([128, 8, 128], BF16, xs, 'wgt') for _ in range(2)] for _ in range(2)]
                sga = sbt([128, 512], F32, xs, 'sga')
                sgb = sbt([128, 512], F32, xs, 'sgb')
                m1 = sbt([128, 512], F32, xs, 'm1')
                m2 = sbt([128, 512], F32, xs, 'm2')
                xts = [sbt([128, D], F32, xs, 'xt') for _ in range(2)]
                x1t = [sbt([128, D], F32, xs, 'x1t') for _ in range(2)]
                tmpx = sbt([128, 512], F32, xs, 'tmpx')
                tmpA = (sbt([128, D], BF16, xs, 'junk'), sbt([128, 1], F32, xs, 'ss'), sbt([128, 1], F32, xs, 'lnv'),
                        sbt([128, 1], F32, xs, 'rstd'), sbt([128, D], BF16, xs, 'xn'))
                sch.dma('sync', woutt[:], wout_s[:, :].rearrange("(kc p) c -> p kc c", p=128), reads=wkeys('wout'), writes=['woutt'])
                for fc in range(8):
                    wi = fc % 2
                    fs = slice(fc * 128, (fc + 1) * 128)
                    sch.dma('sync', wb_t[0][wi][:], wbn_s[:, fs].rearrange("(kc p) c -> p kc c", p=128), reads=wkeys('wbn', 4), writes=[('wbt', 0, wi)])
                    sch.dma('sync', wb_t[1][wi][:], wbm_s[:, fs].rearrange("(kc p) c -> p kc c", p=128), reads=wkeys('wbm', 4), writes=[('wbt', 1, wi)])
                    sch.dma('sync', wg_t[0][wi][:], win_s[:, O_GA + fc * 128:O_GA + (fc + 1) * 128].rearrange("(kc p) c -> p kc c", p=128),
                            reads=wkeys('win'), writes=[('wgt', 0, wi)])
                    sch.dma('sync', wg_t[1][wi][:], win_s[:, O_GB + fc * 128:O_GB + (fc + 1) * 128].rearrange("(kc p) c -> p kc c", p=128),
                            reads=wkeys('win'), writes=[('wgt', 1, wi)])
                    for tb in range(4):
                        tsl = tblk(tb)
                        mmg(PJ[0][:, :], [(wb_t[0][wi][:, kc, :], oT[:, kc, tsl]) for kc in range(4)],
                            [('wbt', 0, wi)] + [('oT', kc, tb) for kc in range(4)], ['PJ0'])
                        mmg(PJ[1][:, :], [(wg_t[0][wi][:, kc, :], hT[:, kc, tsl]) for kc in range(8)],
                            [('wgt', 0, wi)] + hTk(tb), ['PJ1'])
                        act(sga[:], PJ[1][:, :], AF.Sigmoid, ['PJ1'], ['sga'])
                        tt('vector', m1[:], PJ[0][:, :], sga[:], ALU.mult, ['PJ0', 'sga'], ['m1'])
                        mmg(SB[0][:, :], [(wb_t[1][wi][:, kc, :], oT[:, 4 + kc, tsl]) for kc in range(4)],
                            [('wbt', 1, wi)] + [('oT', 4 + kc, tb) for kc in range(4)], ['S0'])
                        mmg(SB[1][:, :], [(wg_t[1][wi][:, kc, :], hT[:, kc, tsl]) for kc in range(8)],
                            [('wgt', 1, wi)] + hTk(tb), ['S1'])
                        act(sgb[:], SB[1][:, :], AF.Sigmoid, ['S1'], ['sgb'])
                        tt('vector', m2[:], SB[0][:, :], sgb[:], ALU.mult, ['S0', 'sgb'], ['m2'])
                        tt('gpsimd', mixT[:, fc, tsl], m1[:], m2[:], ALU.add, ['m1', 'm2'], [('mixT', fc, tb)])
                for tti in range(16):
                    xi = tti % 2
                    tsl = slice(tti * 128, (tti + 1) * 128)
                    sch.dma('sync', xts[xi][:], x_d[b, tsl, :], writes=['xt%d' % xi])
                    for half in range(2):
                        hs = slice(half * 512, (half + 1) * 512)
                        mmg(PJ[half][:, :], [(mixT[:, kc, tsl], woutt[:, kc, hs]) for kc in range(8)],
                            ['woutt'] + [('mixT', kc, tti // 4) for kc in range(8)], [PK[half]])
                        tt('vector', tmpx[:], PJ[half][:, :], gtbc[:, hs], ALU.mult, [PK[half], ('gtbc', half)], ['tmpx'])
                        tt('gpsimd', x1t[xi][:, hs], tmpx[:], xts[xi][:, hs], ALU.add, ['tmpx', 'xt%d' % xi], [('x1t', xi, half)])
                    sch.dma('sync', x1_s[tsl, :], x1t[xi][:], reads=[('x1t', xi, 0), ('x1t', xi, 1)], writes=[('x1s', tti)])
                    norm_transpose(x1t[xi][:], [('x1t', xi, 0), ('x1t', xi, 1)], a2, b2, b, tti, tmpA)
                sch.barrier()
            ms.close()
            dump('h2T', hT[:], [128, 8, S], BF16, [k for tb in range(4) for k in hTk(tb)])
            stg(5)

            with ExitStack() as fs_:
                yT = sbt([128, 22, 1024], BF16, fs_, 'yT')
                wu = [sbt([128, 8, 256], BF16, fs_, 'wu') for _ in range(2)]
                wd = [sbt([128, 22, 256], BF16, fs_, 'wd') for _ in range(2)]
                aS = [sbt([128, 514], F32, fs_, 'aS') for _ in range(2)]
                halo = sbt([128, 22, 2], F32, fs_, 'halo')
                c1 = sbt([128, 512], F32, fs_, 'c1')
                c2 = sbt([128, 512], F32, fs_, 'c2')
                c3 = sbt([128, 512], F32, fs_, 'c3')
                gl = sbt([128, 512], F32, fs_, 'gl')
                x1q = [sbt([128, 256], F32, fs_, 'x1q') for _ in range(2)]
                oq = [sbt([128, 256], F32, fs_, 'oq') for _ in range(2)]
                tmpo = sbt([128, 256], F32, fs_, 'tmpo')
                mset('vector', halo[:], 0.0, ['halo'])
                blk = 0
                for hf in range(2):
                    for fc in range(22):
                        wi = fc % 2
                        sch.dma('sync', wu[wi][:, :, 0:128], wup_s[:, fc * 128:(fc + 1) * 128].rearrange("(kc p) c -> p kc c", p=128),
                                reads=wkeys('wup'), writes=[('wu', wi, 0)])
                        sch.dma('sync', wu[wi][:, :, 128:256], wup_s[:, DFF + fc * 128:DFF + (fc + 1) * 128].rearrange("(kc p) c -> p kc c", p=128),
                                reads=wkeys('wup'), writes=[('wu', wi, 1)])
                        for tb2 in range(2):
                            tok0 = hf * 1024 + tb2 * 512
                            tbg = tok0 // 512
                            ai = blk % 2
                            blk += 1
                            a_ = aS[ai]
                            ak = 'aS%d' % ai
                            mmg(PJ[0][:, :], [(wu[wi][:, kc, 0:128], hT[:, kc, tok0:tok0 + 512]) for kc in range(8)],
                                [('wu', wi, 0)] + hTk(tbg), ['PJ0'])
                            mmg(PJ[1][:, :], [(wu[wi][:, kc, 128:256], hT[:, kc, tok0:tok0 + 512]) for kc in range(8)],
                                [('wu', wi, 1)] + hTk(tbg), ['PJ1'])
                            cp('gpsimd', a_[:, 0:2], halo[:, fc, :], ['halo'], [ak])
                            cp('scalar', a_[:, 2:514], PJ[0][:, :], ['PJ0'], [ak])
                            act(c1[:], a_[:, 0:512], AF.Identity, [ak, 'cw', 'cb'], ['c1'], bias=cb[:, fc:fc + 1], scale=cw[:, fc, 0:1])
                            stt('vector', c2[:], a_[:, 1:513], cw[:, fc, 1:2], c1[:], ALU.mult, ALU.add, [ak, 'c1', 'cw'], ['c2'])
                            stt('vector', c3[:], a_[:, 2:514], cw[:, fc, 2:3], c2[:], ALU.mult, ALU.add, [ak, 'c2', 'cw'], ['c3'])
                            cp('gpsimd', halo[:, fc, :], a_[:, 512:514], [ak], ['halo'])
                            act(gl[:], c3[:], AF.Gelu_apprx_tanh, ['c3'], ['gl'])
                            tt('vector', yT[:, fc, tb2 * 512:(tb2 + 1) * 512], gl[:], PJ[1][:, :], ALU.mult, ['gl', 'PJ1'], [('yT', fc, tb2)])
                    for nq in range(4):
                        wi = nq % 2
                        ns = slice(nq * 256, (nq + 1) * 256)
                        for (r0, r1) in ((0, 8), (8, 16), (16, 22)):
                            sch.dma('sync', wd[wi][:, r0:r1, :], wdn_s[r0 * 128:r1 * 128, ns].rearrange("(kc p) c -> p kc c", p=128),
                                    reads=wkeys('wdn', 22), writes=[('wd', wi, r0)])
                        for t8 in range(8):
                            tti = hf * 8 + t8
                            tsl = slice(tti * 128, (tti + 1) * 128)
                            pi = nxt('pj', 2)
                            oi = t8 % 2
                            mmg(PJ[pi][:, 0:256], [(yT[:, fc, t8 * 128:(t8 + 1) * 128], wd[wi][:, fc, :]) for fc in range(22)],
                                [('wd', wi, 0), ('wd', wi, 8), ('wd', wi, 16)] + [('yT', fc, t8 // 4) for fc in range(22)], [PK[pi]])
                            sch.dma('sync', x1q[oi][:], x1_s[tsl, ns], reads=[('x1s', tti)], writes=['x1q%d' % oi])
                            tt('vector', tmpo[:], PJ[pi][:, 0:256], gtbc[:, 1024 + nq * 256:1024 + (nq + 1) * 256], ALU.mult,
                               [PK[pi], ('gtbc', 2), ('gtbc', 3)], ['tmpo'])
                            tt('gpsimd', oq[oi][:], tmpo[:], x1q[oi][:], ALU.add, ['tmpo', 'x1q%d' % oi], ['oq%d' % oi])
                            sch.dma('sync', out_d[b, tsl, ns], oq[oi][:], reads=['oq%d' % oi], writes=[('out', b, tti, nq)])
                sch.barrier()
            bs.close()
      except StopBuild:
        sch.barrier()
        asx.close(); ms.close(); bs.close()
        break

    sch.finish()
    with nc.Block() as block:
        @block.sync
        def _(e):
            for f in sch.streams['sync']:
                f(e)

        @block.scalar
        def _(e):
            for f in sch.streams['scalar']:
                f(e)

        @block.vector
        def _(e):
            for f in sch.streams['vector']:
                f(e)

        @block.gpsimd
        def _(e):
            for f in sch.streams['gpsimd']:
                f(e)

        @block.tensor
        def _(e):
            for f in sch.streams['tensor']:
                f(e)
    es.close()
    return nc, dbg_outs, sch


def make_in_maps(inputs, nb=4, ncores=NCORE, batches=None):
    f = lambda a: np.ascontiguousarray(np.asarray(a, dtype=np.float32))
    x = np.asarray(inputs['x'])
    c = np.asarray(inputs['c'], dtype=np.float32)
    pos = np.asarray(inputs['positions']).astype(np.int32)
    col = lambda v: np.ascontiguousarray(np.asarray(v, np.float32).reshape(-1, 128).T)
    tile2 = lambda v: np.concatenate([np.asarray(v, np.float32)] * 2)
    hgm = np.zeros((128, 8), np.float32)
    hgm[:, 0] = tile2(inputs['g_q_nsa'][0])
    hgm[:, 1] = np.concatenate([inputs['g_k_slc'][0], inputs['g_k_win'][0]])
    hgm[:, 2] = tile2(inputs['g_q_moba'][0])
    hgm[:, 3] = tile2(inputs['g_k_moba'][0])
    hgm[:, 4] = tile2(inputs['g_k_cmp'][0])
    shared = {
        'bada': f(inputs['b_ada'][0][None, :]),
        'badaT': col(inputs['b_ada'][0]),
        'gcol': np.concatenate([col(inputs['g_attn_norm'][0]), col(inputs['g_ffn_norm'][0])], 1),
        'hg': hgm,
        'cw': np.ascontiguousarray(np.asarray(inputs['conv_w'][0], np.float32).T.reshape(22, 128, 3).transpose(1, 0, 2)),
        'cb': col(inputs['conv_b'][0]),
        'wposk': f(np.asarray(inputs['cmp_k_pos'][0]).T),
        'wposv': f(np.asarray(inputs['cmp_v_pos'][0]).T),
        'w_ada': f(inputs['w_ada'][0]), 'w_in': f(inputs['w_in'][0]),
        'w1k': f(inputs['cmp_k_w1'][0]), 'w2k': f(inputs['cmp_k_w2'][0]),
        'w1v': f(inputs['cmp_v_w1'][0]), 'w2v': f(inputs['cmp_v_w2'][0]),
        'wbn': f(inputs['w_branch_nsa'][0]), 'wbm': f(inputs['w_branch_moba'][0]),
        'wout': f(inputs['w_out'][0]), 'wup': f(inputs['w_ffn_up'][0]), 'wdn': f(inputs['w_ffn_down'][0]),
    }
    for k, v in _consts().items():
        shared['c_' + k] = v
    maps = []
    for ci in range(ncores):
        bl = batches[ci] if batches is not None else list(range(ci * nb, (ci + 1) * nb))
        m = dict(shared)
        m['x'] = f(x[bl])
        m['pos'] = np.ascontiguousarray(pos[bl])
        cc = np.zeros((4, 1024), np.float32)
        cc[:len(bl)] = c[bl]
        m['cT'] = np.ascontiguousarray(cc.T.reshape(8, 128, 4).transpose(1, 0, 2))
        maps.append(m)
    return maps


_CACHE = {}


def kernel(**inputs):
    nb = 4
    if 'nc' not in _CACHE:
        _CACHE['nc'] = build(nb)[0]
    nc = _CACHE['nc']
    maps = make_in_maps(inputs, nb)
    res = run_bass_kernel_spmd(nc, maps, core_ids=list(range(NCORE)))
    out = np.concatenate([np.asarray(r['out']) for r in res.results], axis=0)
    return out.astype(np.float32)
```

```python
import os
import numpy as np
from contextlib import ExitStack
import concourse.bass as bass
import concourse.mybir as mybir
from concourse.bass_utils import run_bass_kernel_spmd

F32 = mybir.dt.float32
BF16 = mybir.dt.bfloat16
I32 = mybir.dt.int32
AF = mybir.ActivationFunctionType
ALU = mybir.AluOpType
AX = mybir.AxisListType

S = 2048
D = 1024
DFF = 2816
INW = 4888
NCORE = 8
EPS = 1e-6
NEGB = -240000.0
ENG = ['sync', 'scalar', 'vector', 'gpsimd', 'tensor']
NSLOT = 8

O_QA, O_KC, O_VC, O_KSL, O_VSL, O_KWN, O_VWN, O_GN, O_QB, O_KB, O_VB, O_GA, O_GB = (
    0, 512, 640, 768, 896, 1024, 1152, 1280, 1304, 1816, 2328, 2840, 3864)


class Sched:
    def __init__(self, nc, es):
        self.nc = nc
        self.streams = {e: [] for e in ENG}
        self.sem = {}
        for e in ENG:
            self.sem[e] = es.enter_context(nc.semaphore('p_' + e))
        self.cnt = {e: 0 for e in ENG}
        self.dq = {'sync': 0, 'gpsimd': 0}
        for q in self.dq:
            for i in range(NSLOT):
                self.sem[(q, i)] = es.enter_context(nc.semaphore('d_%s%d' % (q, i)))
        self.seen = {e: {} for e in ENG}
        self.lw = {}
        self.rd = {}

    def _deps(self, reads, writes):
        deps = {}

        def add(tok):
            if tok is not None:
                deps[tok[0]] = max(deps.get(tok[0], 0), tok[1])
        for k in reads:
            add(self.lw.get(k))
        for k in writes:
            add(self.lw.get(k))
            for sk, v in self.rd.get(k, {}).items():
                add((sk, v))
        return deps

    def _emit_waits(self, eng, deps):
        for sk, v in deps.items():
            if eng == 'tensor' and sk == 'tensor':
                continue
            if self.seen[eng].get(sk, 0) < v:
                self.seen[eng][sk] = v
                h = self.sem[sk]
                self.streams[eng].append(lambda e, h=h, v=v: e.wait_ge(h, v))

    def _commit(self, tok, reads, writes):
        for k in reads:
            d = self.rd.setdefault(k, {})
            d[tok[0]] = max(d.get(tok[0], 0), tok[1])
        for k in writes:
            self.lw[k] = tok
            self.rd[k] = {}

    def group(self, eng, fns, reads=(), writes=()):
        deps = self._deps(reads, writes)
        self._emit_waits(eng, deps)
        self.cnt[eng] += 1
        h = self.sem[eng]
        for f in fns[:-1]:
            self.streams[eng].append(lambda e, f=f: f(e))
        f = fns[-1]
        self.streams[eng].append(lambda e, f=f, h=h: f(e).then_inc(h, 1))
        tok = (eng, self.cnt[eng])
        self._commit(tok, reads, writes)
        return tok

    def op(self, eng, fn, reads=(), writes=()):
        return self.group(eng, [fn], reads, writes)

    def dma(self, q, out, in_, reads=(), writes=()):
        n = self.dq[q]
        self.dq[q] += 1
        slot = n % NSLOT
        val = 16 * (n // NSLOT + 1)
        sk = (q, slot)
        deps = self._deps(reads, writes)
        if val > 16:
            deps[sk] = max(deps.get(sk, 0), val - 16)
        self._emit_waits(q, deps)
        h = self.sem[sk]
        self.streams[q].append(lambda e, out=out, in_=in_, h=h: e.dma_start(out=out, in_=in_).then_inc(h, 16))
        tok = (sk, val)
        self._commit(tok, reads, writes)
        return tok

    def _all_tokens(self):
        d = {e: self.cnt[e] for e in ENG if self.cnt[e] > 0}
        for q, n in self.dq.items():
            for slot in range(NSLOT):
                if n > slot:
                    d[(q, slot)] = 16 * ((n - 1 - slot) // NSLOT + 1)
        return d

    def barrier(self):
        allt = self._all_tokens()
        for e in ENG:
            deps = {k: v for k, v in allt.items() if k != e}
            self._emit_waits(e, deps)

    def finish(self):
        allt = self._all_tokens()
        self._emit_waits('sync', {k: v for k, v in allt.items() if k != 'sync'})


def _consts():
    c = {}
    c['ident'] = np.eye(128, dtype=np.float32)
    bd = np.zeros((128, 128), np.float32)
    bd[:64, :64] = 1
    bd[64:, 64:] = 1
    c['bd64'] = bd
    rot = np.zeros((128, 128), np.float32)
    for m in range(128):
        partner = m + 32 if (m % 64) < 32 else m - 32
        rot[partner, m] = 1
    c['rotm'] = rot
    p = np.arange(128)
    invf = (10000.0 ** (-(p % 32).astype(np.float64) / 32.0)) / (2 * np.pi)
    sgn = np.where((p % 64) < 32, -1.0, 1.0) * 6.28318
    c['invf'] = np.stack([invf, sgn, np.full(128, EPS), np.full(128, 6.28318)], 1).astype(np.float32)
    k = np.arange(128)[:, None]
    t = np.arange(128)[None, :]
    c['mlow'] = (k <= t).astype(np.float32)
    c['mup'] = (k > t).astype(np.float32)
    cm = np.zeros((128, 16, 128), np.float32)
    cc = np.arange(128)[:, None, None]
    tt = (np.arange(16)[None, :, None] * 128 + np.arange(128)[None, None, :])
    cm[:] = (16 * cc + 31 <= tt)
    cm[127] = 0
    c['cmpmask'] = cm
    ovl = np.zeros((128, 33), np.float32)
    ovl[:, 0] = 1
    cs = np.arange(127)[:, None] * 16
    js = np.arange(32)[None, :]
    ovl[:127, 1:] = ((cs < (js + 1) * 64) & (cs + 32 > js * 64)).astype(np.float32)
    c['ovl'] = ovl
    cn = np.zeros((128, 8, 32), np.float32)
    for qi in range(8):
        tq = (qi + 8) * 128 + np.arange(128)
        own = tq // 64
        jb = np.arange(32)[None, :]
        a = np.zeros((128, 32), np.float32)
        a[jb == 0 + 0 * own[:, None]] = 1e4
        a = np.where(jb == own[:, None], 2e4, a)
        a = np.where(jb == own[:, None] - 1, 3e4, a)
        a = np.where(jb > own[:, None], -1e30, a)
        cn[:, qi, :] = a
    c['cn'] = cn
    cmo = np.zeros((128, 8, 8), np.float32)
    for qi in range(8):
        cur = (qi + 8) // 2
        cmo[:, qi, cur:] = -1e30
    c['cmo'] = cmo
    kk = np.arange(2048)[None, :]
    c['ind32'] = (kk // 64 == np.arange(32)[:, None]).astype(np.float32)
    c['ind8'] = (kk // 256 == np.arange(8)[:, None]).astype(np.float32)
    oh = np.zeros((4, 4, 128), np.float32)
    for j in range(4):
        oh[j, j, :] = 1
    c['oh4'] = oh
    return c


CONST_SHAPES = {k: v.shape for k, v in _consts().items()}


class StopBuild(Exception):
    pass


def build(nb=4, dbg=None, stage=99):
    dbg = dbg or set()

    def stg(n):
        if stage <= n:
            raise StopBuild()
    nc = bass.Bass("TRN2", target_bir_lowering=False)
    es = ExitStack()
    uid = [0]

    def din(name, shape, dt=F32):
        return nc.dram_tensor(name, list(shape), dt, kind="ExternalInput").ap()

    def dscr(name, shape, dt=BF16):
        return nc.dram_tensor(name, list(shape), dt, kind="Internal").ap()

    def sbt(shape, dt, scope=None, name='t'):
        uid[0] += 1
        return (scope or es).enter_context(nc.sbuf_tensor("%s_%d" % (name, uid[0]), list(shape), dt))

    x_d = din("x", [nb, S, D])
    pos_d = din("pos", [nb, S], I32)
    cT_d = din("cT", [128, 8, 4])
    bada_d = din("bada", [1, 6144])
    badaT_d = din("badaT", [128, 48])
    gcol_d = din("gcol", [128, 16])
    hg_d = din("hg", [128, 8])
    cw_d = din("cw", [128, 22, 3])
    cb_d = din("cb", [128, 22])
    wposk_d = din("wposk", [64, 32])
    wposv_d = din("wposv", [64, 32])
    wada_d = din("w_ada", [1024, 6144])
    win_d = din("w_in", [1024, INW])
    w1k_d = din("w1k", [2048, 64])
    w2k_d = din("w2k", [64, 64])
    w1v_d = din("w1v", [2048, 64])
    w2v_d = din("w2v", [64, 64])
    wbn_d = din("wbn", [512, 1024])
    wbm_d = din("wbm", [512, 1024])
    wout_d = din("wout", [1024, 1024])
    wup_d = din("wup", [1024, 2 * DFF])
    wdn_d = din("wdn", [DFF, 1024])
    C = {k: din("c_" + k, shp) for k, shp in CONST_SHAPES.items()}
    out_d = nc.dram_tensor("out", [nb, S, D], F32, kind="ExternalOutput").ap()
    dbg_outs = {}

    wada_s = dscr("wada_s", [1024, 6144])
    win_s = dscr("win_s", [1024, INW])
    wbn_s = dscr("wbn_s", [512, 1024])
    wbm_s = dscr("wbm_s", [512, 1024])
    wout_s = dscr("wout_s", [1024, 1024])
    wup_s = dscr("wup_s", [1024, 2 * DFF])
    wdn_s = dscr("wdn_s", [DFF, 1024])
    x1_s = dscr("x1_s", [S, D], F32)
    gt_scr = dscr("gt_scr", [4, 2048], F32)

    sch = Sched(nc, es)
    ps = [es.enter_context(nc.psum_tensor("ps%d" % i, [128, 512], F32)) for i in range(8)]
    SB = [ps[0], ps[1]]
    OB = [ps[2], ps[3]]
    PJ = [ps[4], ps[5]]
    TRf = ps[6]
    MS = ps[7]
    TR = TRf[:].bitcast(BF16)
    TRB = [(TR, 'TR'), (MS[:].bitcast(BF16), 'MS')]
    SK = ['S0', 'S1']
    OK_ = ['O0', 'O1']
    PK = ['PJ0', 'PJ1']

    def act(out, in_, func, reads, writes, bias=None, scale=None, accum=None):
        kw = {}
        if bias is not None:
            kw['bias'] = bias
        if scale is not None:
            kw['scale'] = scale
        if accum is not None:
            kw['accum_out'] = accum
        return sch.op('scalar', lambda e: e.activation(out=out, in_=in_, func=func, **kw), reads, writes)

    def tt(eng, out, in0, in1, op, reads, writes):
        return sch.op(eng, lambda e: e.tensor_tensor(out=out, in0=in0, in1=in1, op=op), reads, writes)

    def ts(eng, out, in0, s1, s2, op0, op1, reads, writes):
        if s2 is None:
            return sch.op(eng, lambda e: e.tensor_scalar(out=out, in0=in0, scalar1=s1, scalar2=None, op0=op0), reads, writes)
        return sch.op(eng, lambda e: e.tensor_scalar(out=out, in0=in0, scalar1=s1, scalar2=s2, op0=op0, op1=op1), reads, writes)

    def stt(eng, out, in0, scalar, in1, op0, op1, reads, writes):
        return sch.op(eng, lambda e: e.scalar_tensor_tensor(out=out, in0=in0, scalar=scalar, in1=in1, op0=op0, op1=op1), reads, writes)

    def cp(eng, out, in_, reads, writes):
        if eng == 'scalar':
            return act(out, in_, AF.Copy, reads, writes)
        return sch.op(eng, lambda e: e.tensor_copy(out=out, in_=in_), reads, writes)

    def mset(eng, ap, val, writes):
        return sch.op(eng, lambda e: e.memset(ap, val), (), writes)

    def mm(out, lhsT, rhs, start=True, stop=True):
        return lambda e: e.matmul(out, lhsT, rhs, start=start, stop=stop)

    def mmg(out, pairs, reads, writes):
        n = len(pairs)
        return sch.group('tensor', [mm(out, l, r, i == 0, i == n - 1) for i, (l, r) in enumerate(pairs)], reads, writes)

    def trp(out, in_, ident_ap, reads, writes):
        return sch.op('tensor', lambda e: e.transpose(out, in_, ident_ap), reads, writes)

    def dump(name, ap, shape, dt, reads):
        if name not in dbg:
            return
        t = nc.dram_tensor("dbg_" + name, list(shape), dt, kind="ExternalOutput").ap()
        dbg_outs[name] = t
        sch.dma('sync', t, ap, reads=reads)

    def wkeys(name, n=8):
        return [(name, i) for i in range(n)]

    def conv(dst, src, rows, name):
        for r in range(0, rows, 128):
            sch.dma('gpsimd', dst[r:r + 128, :], src[r:r + 128, :], writes=[(name, r // 128)])

    conv(wada_s, wada_d, 1024, 'wada')
    cst = {}

    def cload(name, src, shape, dt, q=None):
        t = sbt(shape, dt, name=name)
        q = q or ('gpsimd' if dt == BF16 else 'sync')
        sch.dma(q, t[:], src, writes=[name])
        cst[name] = t
        return t

    ident = cload('ident', C['ident'], [128, 128], BF16)
    bd64 = cload('bd64', C['bd64'], [128, 128], BF16)
    rotm = cload('rotm', C['rotm'], [128, 128], BF16)
    invf = cload('invf', C['invf'], [128, 4], F32)
    mlow = cload('mlow', C['mlow'], [128, 128], BF16)
    mup = cload('mup', C['mup'], [128, 128], BF16)
    cmpmask = cload('cmpmask', C['cmpmask'], [128, 16, 128], BF16)
    cn_t = cload('cn', C['cn'], [128, 8, 32], F32)
    cmo_t = cload('cmo', C['cmo'], [128, 8, 8], F32)
    oh4 = cload('oh4', C['oh4'], [4, 4, 128], F32)
    cTb = cload('cTb', cT_d, [128, 8, 4], BF16)
    badaT = cload('badaT', badaT_d, [128, 48], F32)
    gcol = cload('gcol', gcol_d, [128, 16], F32)
    hg = cload('hg', hg_d, [128, 8], F32)
    cw = cload('cw', cw_d, [128, 22, 3], F32)
    cb = cload('cb', cb_d, [128, 22], F32)
    w1k = cload('w1k', w1k_d.rearrange("(l d) j -> d l j", d=64), [64, 32, 64], BF16)
    w1v = cload('w1v', w1v_d.rearrange("(l d) j -> d l j", d=64), [64, 32, 64], BF16)
    w2k = cload('w2k', w2k_d, [64, 64], BF16)
    w2v = cload('w2v', w2v_d, [64, 64], BF16)
    wposk = cload('wposk', wposk_d, [64, 32], BF16)
    wposv = cload('wposv', wposv_d, [64, 32], BF16)
    VCO = sbt([128, 97], BF16, name='VCO')
    sch.dma('gpsimd', VCO[:, 64:97], C['ovl'], writes=['VCO'])
    bada4 = sbt([4, 2048], F32, name='bada4')
    sch.dma('sync', bada4[:, 0:1024], bada_d[0:1, 2048:3072].to_broadcast([4, 1024]), writes=['bada4a'])
    sch.dma('sync', bada4[:, 1024:2048], bada_d[0:1, 5120:6144].to_broadcast([4, 1024]), writes=['bada4b'])

    conv(win_s, win_d, 1024, 'win')
    conv(wbn_s, wbn_d, 512, 'wbn')
    conv(wbm_s, wbm_d, 512, 'wbm')
    conv(wout_s, wout_d, 1024, 'wout')
    conv(wup_s, wup_d, 1024, 'wup')
    conv(wdn_s, wdn_d, DFF, 'wdn')

    modT = sbt([128, 32, 4], F32, name='modT')
    rows_gt = sbt([4, 2048], F32, name='rowsgt')
    a1 = sbt([128, 8, 4], F32, name='a1')
    a2 = sbt([128, 8, 4], F32, name='a2')
    cposk = sbt([64, 1], F32, name='cposk')
    cposv = sbt([64, 1], F32, name='cposv')
    with ExitStack() as sc:
        wts = [sbt([128, 8, 512], BF16, sc, 'wadat') for _ in range(2)]
        tmp84 = sbt([128, 8, 4], F32, sc, 'tmp84')
        for ct in range(12):
            wt = wts[ct % 2]
            wk = 'wadat%d' % (ct % 2)
            sch.dma('sync', wt[:], wada_s[:, ct * 512:(ct + 1) * 512].rearrange("(kc p) c -> p kc c", p=128),
                    reads=wkeys('wada'), writes=[wk])
            sec = ct // 2
            if sec in (2, 5):
                pj = PJ[ct % 2]
                mmg(pj[0:4, :], [(cTb[:, kc, :], wt[:, kc, :]) for kc in range(8)], [wk, 'cTb'], [PK[ct % 2]])
                gi = (0 if sec == 2 else 1) * 1024 + (ct % 2) * 512
                tt('vector', rows_gt[:, gi:gi + 512], pj[0:4, :], bada4[:, gi:gi + 512], ALU.add,
                   [PK[ct % 2], 'bada4a', 'bada4b'], [('rowsgt', gi // 512)])
            else:
                mi = {0: 0, 1: 8, 3: 16, 4: 24}[sec] + (ct % 2) * 4
                for j in range(4):
                    mmg(MS[:, j * 4:(j + 1) * 4], [(wt[:, kc, j * 128:(j + 1) * 128], cTb[:, kc, :]) for kc in range(8)],
                        [wk, 'cTb'], ['MS'])
                tt('vector', modT[:, mi:mi + 4, :], MS[:, 0:16].rearrange("p (j b) -> p j b", j=4),
                   badaT[:, ct * 4:ct * 4 + 4].unsqueeze(2).to_broadcast([128, 4, 4]), ALU.add,
                   ['MS', 'badaT'], [('modT', mi // 4)])
        ts('vector', tmp84[:], modT[:, 8:16, :], 1.0, None, ALU.add, None, [('modT', 2), ('modT', 3)], ['tmp84'])
        tt('vector', a1[:], tmp84[:], gcol[:, 0:8].unsqueeze(2).to_broadcast([128, 8, 4]), ALU.mult, ['tmp84', 'gcol'], ['a1'])
        ts('vector', tmp84[:], modT[:, 24:32, :], 1.0, None, ALU.add, None, [('modT', 6), ('modT', 7)], ['tmp84'])
        tt('vector', a2[:], tmp84[:], gcol[:, 8:16].unsqueeze(2).to_broadcast([128, 8, 4]), ALU.mult, ['tmp84', 'gcol'], ['a2'])
        for (w1t, wpt, cpo, nm) in ((w1k, wposk, cposk, 'k'), (w1v, wposv, cposv, 'v')):
            mmg(MS[0:64, 0:1], [(w1t[:, l, :], wpt[:, l:l + 1]) for l in range(32)], ['w1' + nm, 'wpos' + nm], ['MS'])
            cp('vector', cpo[:], MS[0:64, 0:1], ['MS'], ['cpos' + nm])
    sch.dma('sync', gt_scr[:, :], rows_gt[:], reads=[('rowsgt', i) for i in range(4)], writes=['gtscr'])
    b1 = modT[:, 0:8, :]
    b2 = modT[:, 16:24, :]
    MODK = [('modT', i) for i in range(8)] + ['a1', 'a2']
    sch.barrier()
    stage0 = stage <= 0

    def norm_transpose(src_tile, srck, aa, bb, b, tti, tmp):
        junk, ss, lnv, rstd, xn = tmp
        CUT = int(os.environ.get('PHASEA_CUT', '99'))
        mset('vector', ss[:], 0.0, ['ss'])
        act(junk[:], src_tile, AF.Square, srck + ['ss'], ['junk', 'ss'], accum=ss[:, 0:1])
        if CUT < 1:
            return
        act(lnv[:], ss[:], AF.Ln, ['ss'], ['lnv'], bias=invf[:, 2:3], scale=1.0 / D)
        act(rstd[:], lnv[:], AF.Exp, ['lnv'], ['rstd'], scale=-0.5)
        if CUT < 2:
            return
        ts('vector', xn[:], src_tile, rstd[:, 0:1], None, ALU.mult, None, srck + ['rstd'], ['xn'])
        if CUT < 3:
            return
        TRx, trk = TRB[tti % 2]
        sch.group('tensor', [(lambda e, c=c: e.transpose(TRx[:, c * 128:(c + 1) * 128], xn[:, c * 128:(c + 1) * 128], ident[:])) for c in range(8)],
                  ['xn', 'ident'], [trk])
        if CUT < 4:
            return
        for c in range(8):
            dst = hT[:, c, tti * 128:(tti + 1) * 128]
            if True:
                act(dst, TRx[:, c * 128:(c + 1) * 128], AF.Identity, [trk] + MODK, [('hT', c, tti // 4)],
                    bias=bb[:, c, b:b + 1], scale=aa[:, c, b:b + 1])
            else:
                stt('vector', dst, TRx[:, c * 128:(c + 1) * 128], aa[:, c, b:b + 1], bb[:, c, b:b + 1].to_broadcast([128, 128]), ALU.mult, ALU.add,
                    [trk] + MODK, [('hT', c, tti // 4)])

    def hTk(tb):
        return [('hT', c, tb) for c in range(8)]

    for b in range(0 if stage0 else nb):
      bs = ExitStack()
      ms = ExitStack()
      asx = ExitStack()
      try:
            cosT = sbt([128, S], BF16, bs, 'cosT')
            sinS = sbt([128, S], BF16, bs, 'sinS')
            gtbc = sbt([128, 2048], F32, bs, 'gtbc')
            hT = sbt([128, 8, S], BF16, bs, 'hT')
            with ExitStack() as sc:
                posi = sbt([128, S], I32, sc, 'posi')
                u = sbt([128, S], F32, sc, 'u')
                ni = sbt([128, S], I32, sc, 'ni')
                nf = sbt([128, S], F32, sc, 'nf')
                sch.dma('sync', posi[:], pos_d[b:b + 1, :].to_broadcast([128, S]), writes=['posi'])
                cp('vector', u[:], posi[:], ['posi'], ['u'])
                ts('vector', u[:], u[:], invf[:, 0:1], None, ALU.mult, None, ['u', 'invf'], ['u'])
                cp('vector', ni[:], u[:], ['u'], ['ni'])
                cp('vector', nf[:], ni[:], ['ni'], ['nf'])
                tt('vector', nf[:], u[:], nf[:], ALU.subtract, ['u', 'nf'], ['nf'])
                act(sinS[:], nf[:], AF.Sin, ['nf', 'invf'], ['sinS'], scale=invf[:, 1:2])
                ts('vector', u[:], u[:], 0.25, None, ALU.add, None, ['u'], ['u'])
                cp('vector', ni[:], u[:], ['u'], ['ni'])
                cp('vector', nf[:], ni[:], ['ni'], ['nf'])
                tt('vector', nf[:], u[:], nf[:], ALU.subtract, ['u', 'nf'], ['nf'])
                act(cosT[:], nf[:], AF.Sin, ['nf', 'invf'], ['cosT'], scale=invf[:, 3:4])
                sch.barrier()
            stg(0.3)
            sch.dma('sync', gtbc[:], gt_scr[b:b + 1, :].to_broadcast([128, 2048]), reads=['gtscr'], writes=[('gtbc', i) for i in range(4)])
            stg(0.6)
            with ExitStack() as sc:
                xts = [sbt([128, D], F32, sc, 'xt') for _ in range(2)]
                tmpA = (sbt([128, D], BF16, sc, 'junk'), sbt([128, 1], F32, sc, 'ss'), sbt([128, 1], F32, sc, 'lnv'),
                        sbt([128, 1], F32, sc, 'rstd'), sbt([128, D], BF16, sc, 'xn'))
                for tti in range(16):
                    xt = xts[tti % 2]
                    sch.dma('sync', xt[:], x_d[b, tti * 128:(tti + 1) * 128, :], writes=['xt%d' % (tti % 2)])
                    norm_transpose(xt[:], ['xt%d' % (tti % 2)], a1, b1, b, tti, tmpA)
                sch.barrier()
            dump('hT', hT[:], [128, 8, S], BF16, [k for tb in range(4) for k in hTk(tb)])
            dump('cosT', cosT[:], [128, S], BF16, ['cosT'])
            dump('sinS', sinS[:], [128, S], BF16, ['sinS'])
            stg(1)

            oT = sbt([128, 8, S], BF16, ms, 'oT')
            Qa = sbt([96, 4, S], BF16, asx, 'Qa')
            Kb = sbt([96, 4, S], BF16, asx, 'Kb')
            Vb = sbt([128, 16, 4, 65], BF16, asx, 'Vb')
            oacc = sbt([128, 16, 4, 64], F32, asx, 'oacc')
            gates = sbt([128, 16, 12], F32, asx, 'gates')
            wt_fm = [sbt([128, 8, 128], BF16, asx, 'wtfm') for _ in range(2)]
            wt_tm = sbt([128, 8, 256], BF16, asx, 'wttm')
            qn = sbt([128, 512], BF16, asx, 'qn')
            qnB = sbt([128, 512], BF16, asx, 'qnB')
            t3B = sbt([128, 512], BF16, asx, 't3B')
            sq = sbt([128, 512], BF16, asx, 'sq')
            t3 = sbt([128, 512], BF16, asx, 't3')
            t1 = sbt([128, 512], F32, asx, 't1')
            t2 = sbt([128, 512], F32, asx, 't2')
            lnt = sbt([128, 512], F32, asx, 'lnt')
            rst = sbt([128, 512], F32, asx, 'rst')
            Pt = [sbt([128, 512], BF16, asx, 'P') for _ in range(3)]
            Tst = sbt([128, 96], BF16, asx, 'Tst')
            T2 = sbt([128, 4, 72], BF16, asx, 'T2')
            hid = sbt([64, 128], BF16, asx, 'hid')
            kcn = sbt([64, 128], BF16, asx, 'kcn')
            sm = sbt([128, 64], F32, asx, 'sm')
            imp3 = sbt([128, 4, 32], F32, asx, 'imp3')
            impm = sbt([128, 32], F32, asx, 'impm')
            impr = sbt([128, 32], F32, asx, 'impr')
            m8 = sbt([128, 4, 8], F32, asx, 'm8')
            gm = sbt([128, 4, 8], F32, asx, 'gm')
            lt8 = sbt([128, 4, 8], F32, asx, 'lt8')
            ksf = sbt([64, 4, 8], F32, asx, 'ksf')
            ksb = sbt([64, 4, 8], BF16, asx, 'ksb')
            otmp = sbt([128, 4, 64], F32, asx, 'otmp')
            obf = sbt([128, 4, 256], BF16, asx, 'obf')
            mset('vector', Vb[:, :, :, 64:65], 1.0, ['Vb_ones'])
            mset('vector', Tst[:], 0.0, ['Tst'])
            mset('vector', T2[:], 0.0, ['T2'])
            cnt = {'pj': 0, 's': 0, 'o': 0, 'p': 0, 'w': 0}

            def nxt(k, n):
                v = cnt[k] % n
                cnt[k] += 1
                return v

            def load_w_fm(pieces):
                i = nxt('w', 2)
                wt = wt_fm[i]
                c = 0
                for (c0, n) in pieces:
                    sch.dma('sync', wt[:, :, c:c + n], win_s[:, c0:c0 + n].rearrange("(kc p) c -> p kc c", p=128),
                            reads=wkeys('win'), writes=[('wtfm', i, c)])
                    c += n
                return wt, [('wtfm', i, cc) for cc in (0, 64)]

            qn2 = [qn, qnB]
            t32 = [t3, t3B]

            def proj_fm_multi(specs):
                its = [(si_, tb) for si_ in range(len(specs)) for tb in range(4)]
                wts_ = {}

                def getw(si_):
                    if si_ not in wts_ and si_ < len(specs):
                        wts_[si_] = load_w_fm(specs[si_][0])
                    return wts_.get(si_)
                getw(0)
                stA = {}

                def A(k):
                    si_, tb = its[k]
                    wt, wk = getw(si_)
                    if tb == 0:
                        getw(si_ + 1)
                    pi = nxt('pj', 2)
                    mmg(PJ[pi][:, :], [(wt[:, kc, :], hT[:, kc, tb * 512:(tb + 1) * 512]) for kc in range(8)],
                        wk + hTk(tb), [PK[pi]])
                    stA[k] = pi

                def BCD(k):
                    si_, tb = its[k]
                    pieces, norm, gaincol, dests = specs[si_]
                    pi = stA.pop(k)
                    pj = PJ[pi]
                    q_ = qn2[k % 2]
                    qk = 'qn%d' % (k % 2)
                    t3_ = t32[k % 2]
                    t3k = 't3%d' % (k % 2)
                    anyrope = any(kd == 'rope' for kd, _, _ in dests)
                    if norm:
                        act(sq[:], pj[:, :], AF.Square, [PK[pi]], ['sq'])
                        sch.op('tensor', mm(MS[:, :], bd64[:], sq[:]), ['sq', 'bd64'], ['MS'])
                        act(lnt[:], MS[:, :], AF.Ln, ['MS'], ['lnt'], bias=invf[:, 2:3], scale=1.0 / 64)
                        act(rst[:], lnt[:], AF.Exp, ['lnt'], ['rst'], scale=-0.5)
                        stt('vector', q_[:], pj[:, :], gaincol, rst[:], ALU.mult, ALU.mult, [PK[pi], 'rst', 'hg'], [qk])
                    else:
                        cp('scalar', q_[:], pj[:, :], [PK[pi]], [qk])
                    if anyrope:
                        si2 = nxt('s', 2)
                        sch.op('tensor', mm(SB[si2][:, :], rotm[:], q_[:]), [qk, 'rotm'], [SK[si2]])
                        tt('gpsimd', t1[:], q_[:], cosT[:, tb * 512:(tb + 1) * 512], ALU.mult, [qk, 'cosT'], ['t1'])
                        tt('vector', t2[:], SB[si2][:, :], sinS[:, tb * 512:(tb + 1) * 512], ALU.mult, [SK[si2], 'sinS'], ['t2'])
                        tt('vector', t3_[:], t1[:], t2[:], ALU.add, ['t1', 't2'], [t3k])
                    for hf, (kind, dfn, kfn) in enumerate(dests):
                        src = t3_ if kind == 'rope' else q_
                        sk = t3k if kind == 'rope' else qk
                        pr = slice(0, 64) if hf == 0 else slice(64, 128)
                        if hf == 0:
                            cp('gpsimd', dfn(tb), src[pr, :], [sk], kfn(tb))
                        else:
                            cp('scalar', dfn(tb), src[pr, :], [sk], kfn(tb))
                A(0)
                for k in range(len(its)):
                    if k + 1 < len(its):
                        A(k + 1)
                    BCD(k)

            def tblk(tb):
                return slice(tb * 512, (tb + 1) * 512)

            def qkeys(h, tb):
                return [('Qa', h, 4 * tb + i) for i in range(4)]

            def kkeys(i, tb):
                return [('Kb', i, 4 * tb + j) for j in range(4)]

            pend = [None]

            def flush():
                if pend[0] is not None:
                    f = pend[0]
                    pend[0] = None
                    f()

            def attn_round(units, vfn, ofirst_key, norm_fn):
                oi = nxt('o', 2)
                O = OB[oi]
                st = {'first': True}
                nU = len(units)
                for ui, un in enumerate(units):
                    si = nxt('s', 2)
                    Sb = SB[si]
                    n, ncol = un['n'], un['ncol']
                    sch.op('tensor', mm(Sb[0:n, 0:ncol], un['lhsT'], un['rhs']), un['reads'], [SK[si]])
                    flush()
                    pi = nxt('p', 3)
                    P = Pt[pi]
                    pk = 'P%d' % pi
                    act(P[0:n, 0:ncol], Sb[0:n, 0:ncol], AF.Exp, [SK[si]], [pk], scale=0.125)
                    if un.get('mask') is not None:
                        mo, mi_, mk = un['mask'](P)
                        tt('gpsimd', mo, mo, mi_, ALU.mult, [pk, mk], [pk])

                    def pv(un=un, P=P, pk=pk, n=n, last=(ui == nU - 1)):
                        fns = []
                        for (c0, oreg) in un['pv']:
                            fns.append(mm(oreg(O), P[0:n, c0:c0 + 128], un['v'], st['first'], True))
                            st['first'] = False
                        sch.group('tensor', fns, [pk] + un['vreads'], [OK_[oi]])
                        if last:
                            norm_fn(O, OK_[oi])
                    pend[0] = pv

            for pas in range(4):
                is_nsa = pas < 2
                g = pas % 2
                if is_nsa:
                    specs = []
                    for i in range(2):
                        c0 = O_QA + g * 256 + i * 128
                        specs.append(([(c0, 128)], True, hg[:, 0:1],
                                      [('rope', (lambda tb, h=2 * i: Qa[0:64, h, tblk(tb)]), (lambda tb, h=2 * i: qkeys(h, tb))),
                                       ('rope', (lambda tb, h=2 * i + 1: Qa[0:64, h, tblk(tb)]), (lambda tb, h=2 * i + 1: qkeys(h, tb)))]))
                    specs.append(([(O_KSL + g * 64, 64), (O_KWN + g * 64, 64)], True, hg[:, 1:2],
                                  [('rope', (lambda tb: Kb[0:64, 0, tblk(tb)]), (lambda tb: kkeys(0, tb))),
                                   ('rope', (lambda tb: Kb[0:64, 1, tblk(tb)]), (lambda tb: kkeys(1, tb)))]))
                    specs.append(([(O_KC + g * 64, 64), (O_VC + g * 64, 64)], False, None,
                                  [('rope', (lambda tb: Kb[0:64, 2, tblk(tb)]), (lambda tb: kkeys(2, tb))),
                                   ('copy', (lambda tb: Kb[0:64, 3, tblk(tb)]), (lambda tb: kkeys(3, tb)))]))
                    proj_fm_multi(specs)
                    sch.dma('gpsimd', Kb[64:96, 0, :], C['ind32'], writes=[('KbI', 0)])
                    wtk = []
                    for (c0, n, cc) in ((O_VSL + g * 64, 64, 0), (O_VWN + g * 64, 64, 64), (O_GN + g * 12, 12, 128)):
                        sch.dma('sync', wt_tm[:, :, cc:cc + n], win_s[:, c0:c0 + n].rearrange("(kc p) c -> p kc c", p=128),
                                reads=wkeys('win'), writes=[('wttm', cc)])
                        wtk.append(('wttm', cc))
                    for tti in range(16):
                        pi = nxt('pj', 2)
                        pj = PJ[pi]
                        mmg(pj[:, 0:140], [(hT[:, kc, tti * 128:(tti + 1) * 128], wt_tm[:, kc, 0:140]) for kc in range(8)],
                            wtk + hTk(tti // 4), [PK[pi]])
                        cp('vector', Vb[:, tti, 0:2, 0:64], pj[:, 0:128].rearrange("p (a d) -> p a d", a=2), [PK[pi]], [('Vb', tti)])
                        act(gates[:, tti, :], pj[:, 128:140], AF.Sigmoid, [PK[pi]], [('gates', tti)])
                    KC_ALL = [k for tb in range(4) for k in kkeys(2, tb)]
                    VC_ALL = [k for tb in range(4) for k in kkeys(3, tb)]
                    for (hi, w1t, cpo, nm, kall) in ((2, w1k, cposk, 'k', KC_ALL), (3, w1v, cposv, 'v', VC_ALL)):
                        mmg(MS[0:64, 0:127], [(w1t[:, l, :], Kb[0:64, hi, l:l + 16 * 126 + 1:16]) for l in range(32)],
                            kall + ['w1' + nm], ['MS'])
                        act(hid[:, 0:127], MS[0:64, 0:127], AF.Gelu_apprx_tanh, ['MS', 'cpos' + nm], ['hid'], bias=cpo[:, 0:1])
                        if nm == 'k':
                            si = nxt('s', 2)
                            sch.op('tensor', mm(SB[si][0:64, 0:127], w2k[:], hid[:, 0:127]), ['hid', 'w2k'], [SK[si]])
                            act(sq[0:64, 0:127], SB[si][0:64, 0:127], AF.Square, [SK[si]], ['sq'])
                            sch.op('tensor', mm(MS[0:64, 0:127], bd64[0:64, 0:64], sq[0:64, 0:127]), ['sq', 'bd64'], ['MS'])
                            act(lnt[0:64, 0:127], MS[0:64, 0:127], AF.Ln, ['MS'], ['lnt'], bias=invf[0:64, 2:3], scale=1.0 / 64)
                            act(rst[0:64, 0:127], lnt[0:64, 0:127], AF.Exp, ['lnt'], ['rst'], scale=-0.5)
                            stt('vector', kcn[:, 0:127], SB[si][0:64, 0:127], hg[0:64, 4:5], rst[0:64, 0:127], ALU.mult, ALU.mult,
                                [SK[si], 'rst', 'hg'], ['kcn'])
                        else:
                            si = nxt('s', 2)
                            sch.op('tensor', mm(SB[si][0:127, 0:64], hid[:, 0:127], w2v[:]), ['hid', 'w2v'], [SK[si]])
                            cp('vector', VCO[0:127, 0:64], SB[si][0:127, 0:64], [SK[si]], ['VCOv'])
                    if b == 0 and pas == 0:
                        dump('Qa', Qa[0:64, :, :], [64, 4, S], BF16, [k for h in range(4) for tb in range(4) for k in qkeys(h, tb)])
                        dump('Kb', Kb[0:64, :, :], [64, 4, S], BF16, [k for h in range(4) for tb in range(4) for k in kkeys(h, tb)])
                        dump('Vb', Vb[:], [128, 16, 4, 65], BF16, [('Vb', i) for i in range(16)] + ['Vb_ones'])
                        dump('gates', gates[:], [128, 16, 12], F32, [('gates', i) for i in range(16)])
                        dump('kcn', kcn[:], [64, 128], BF16, ['kcn'])
                        dump('VCO', VCO[:], [128, 97], BF16, ['VCOv', 'VCO'])
                        stg(2)

                    def nsa_norm(qt, br, first_branch, imp_out):
                        def fn(O, ok):
                            Ov = O[:, 0:388].rearrange("p (r c) -> p r c", r=4) if br == 0 else \
                                O[:, 0:260].rearrange("p (r c) -> p r c", r=4)
                            ts('vector', sm[:, 0:4], Ov[:, :, 64], 1e-30, None, ALU.max, None, [ok], ['sm0'])
                            sch.op('vector', lambda e: e.reciprocal(out=sm[:, 4:8], in_=sm[:, 0:4]), ['sm0'], ['sm1'])
                            tt('vector', sm[:, 8:12], sm[:, 4:8], gates[:, qt, br:12:3], ALU.mult, ['sm1', ('gates', qt)], ['sm2'])
                            fb = sm[:, 8:12].unsqueeze(2).to_broadcast([128, 4, 64])
                            if first_branch:
                                tt('vector', oacc[:, qt, :, :], Ov[:, :, 0:64], fb, ALU.mult, [ok, 'sm2'], [('oacc', qt)])
                            else:
                                tt('vector', otmp[:], Ov[:, :, 0:64], fb, ALU.mult, [ok, 'sm2'], ['otmp'])
                                tt('gpsimd', oacc[:, qt, :, :], oacc[:, qt, :, :], otmp[:], ALU.add, ['otmp', ('oacc', qt)], [('oacc', qt)])
                            if imp_out:
                                tt('vector', imp3[:], Ov[:, :, 65:97], sm[:, 4:8].unsqueeze(2).to_broadcast([128, 4, 32]), ALU.mult,
                                   [ok, 'sm1'], ['imp3'])
                        return fn

                    for qt in range(16):
                        ncv = min(127, 8 * qt + 7)
                        qs = slice(qt * 128, (qt + 1) * 128)
                        unit = dict(lhsT=kcn[:, 0:ncv], rhs=Qa[0:64, 0:4, qs], n=ncv, ncol=512,
                                    reads=['kcn'] + [('Qa', h, qt) for h in range(4)],
                                    mask=(lambda P, ncv=ncv, qt=qt: (P[0:ncv, :].rearrange("p (r t) -> p r t", r=4),
                                                                   cmpmask[0:ncv, qt, :].unsqueeze(1).to_broadcast([ncv, 4, 128]), 'cmpmask')),
                                    pv=[(r * 128, (lambda O, r=r: O[:, r * 97:(r + 1) * 97])) for r in range(4)],
                                    v=VCO[0:ncv, 0:97], vreads=['VCOv', 'VCO'])
                        def prenorm(O, ok, qt=qt, qs=qs):
                            nsa_norm(qt, 0, True, qt >= 8)(O, ok)
                            if qt >= 8:
                                sch.op('vector', lambda e: e.tensor_reduce(out=impr[:], in_=imp3[:].rearrange("p r j -> p j r"),
                                                                            axis=AX.X, op=ALU.add), ['imp3'], ['impr'])
                                tt('vector', impm[:], impr[:], cn_t[:, qt - 8, :], ALU.add, ['impr', 'cn'], ['impm'])
                                sch.op('vector', lambda e: e.max(out=m8[:, 0, :], in_=impm[:]), ['impm'], ['m8'])
                                sch.op('vector', lambda e: e.match_replace(out=impr[:], in_to_replace=m8[:, 0, :], in_values=impm[:],
                                                                            imm_value=-3e38), ['impm', 'm8'], ['impr'])
                                sch.op('vector', lambda e: e.max(out=m8[:, 1, :], in_=impr[:]), ['impr', 'm8'], ['m8'])
                                ts('vector', Tst[:, 64:96], impm[:], m8[:, 1, 7:8], NEGB, ALU.is_lt, ALU.mult, ['impm', 'm8'], ['Tst'])
                                trp(TR[0:96, 0:128], Tst[:], ident[:], ['Tst', 'ident'], ['TR'])
                                cp('scalar', Qa[64:96, 0:4, qs], TR[64:96, 0:128].unsqueeze(1).to_broadcast([32, 4, 128]),
                                   ['TR'], [('QaB', qt)])
                        attn_round([unit], None, None, prenorm)
                    for qt in range(16):
                        qs = slice(qt * 128, (qt + 1) * 128)
                        units = []
                        for kt in range(max(0, qt - 4), qt + 1):
                            ks = slice(kt * 128, (kt + 1) * 128)
                            mk = None
                            if kt == qt:
                                mk = (lambda P: (P[:, :].rearrange("p (r t) -> p r t", r=4), mlow[:].unsqueeze(1).to_broadcast([128, 4, 128]), 'mlow'))
                            elif kt == qt - 4:
                                mk = (lambda P: (P[:, :].rearrange("p (r t) -> p r t", r=4), mup[:].unsqueeze(1).to_broadcast([128, 4, 128]), 'mup'))
                            units.append(dict(lhsT=Kb[0:64, 1, ks], rhs=Qa[0:64, 0:4, qs], n=128, ncol=512,
                                              reads=[('Kb', 1, kt)] + [('Qa', h, qt) for h in range(4)], mask=mk,
                                              pv=[(r * 128, (lambda O, r=r: O[:, r * 65:(r + 1) * 65])) for r in range(4)],
                                              v=Vb[:, kt, 1, :], vreads=[('Vb', kt), 'Vb_ones']))
                        attn_round(units, None, None, nsa_norm(qt, 2, False, False))
                    for qt in range(16):
                        qs = slice(qt * 128, (qt + 1) * 128)
                        kd = 64 if qt < 8 else 96
                        units = []
                        for kt in range(0, qt + 1):
                            ks = slice(kt * 128, (kt + 1) * 128)
                            mk = None
                            if kt == qt:
                                mk = (lambda P: (P[:, :].rearrange("p (r t) -> p r t", r=4), mlow[:].unsqueeze(1).to_broadcast([128, 4, 128]), 'mlow'))
                            rd = [('Kb', 0, kt)] + [('Qa', h, qt) for h in range(4)]
                            if kd == 96:
                                rd += [('KbI', 0), ('QaB', qt)]
                            units.append(dict(lhsT=Kb[0:kd, 0, ks], rhs=Qa[0:kd, 0:4, qs], n=128, ncol=512, reads=rd, mask=mk,
                                              pv=[(r * 128, (lambda O, r=r: O[:, r * 65:(r + 1) * 65])) for r in range(4)],
                                              v=Vb[:, kt, 0, :], vreads=[('Vb', kt), 'Vb_ones']))
                        attn_round(units, None, None, nsa_norm(qt, 1, False, False))
                    chunk0 = 2 * g
                else:
                    hgp = g
                    specs = []
                    for i in range(2):
                        c0 = O_QB + (4 * hgp + 2 * i) * 64
                        specs.append(([(c0, 128)], True, hg[:, 2:3],
                                      [('rope', (lambda tb, h=2 * i: Qa[0:64, h, tblk(tb)]), (lambda tb, h=2 * i: qkeys(h, tb))),
                                       ('rope', (lambda tb, h=2 * i + 1: Qa[0:64, h, tblk(tb)]), (lambda tb, h=2 * i + 1: qkeys(h, tb)))]))
                    for i in range(2):
                        c0 = O_KB + (4 * hgp + 2 * i) * 64
                        specs.append(([(c0, 128)], True, hg[:, 3:4],
                                      [('rope', (lambda tb, h=2 * i: Kb[0:64, h, tblk(tb)]), (lambda tb, h=2 * i: kkeys(h, tb))),
                                       ('rope', (lambda tb, h=2 * i + 1: Kb[0:64, h, tblk(tb)]), (lambda tb, h=2 * i + 1: kkeys(h, tb)))]))
                    proj_fm_multi(specs)
                    for h in range(4):
                        sch.dma('gpsimd', Kb[64:72, h, :], C['ind8'], writes=[('KbI', h)])
                    c0 = O_VB + hgp * 256
                    sch.dma('sync', wt_tm[:, :, 0:256], win_s[:, c0:c0 + 256].rearrange("(kc p) c -> p kc c", p=128),
                            reads=wkeys('win'), writes=[('wttm', 0), ('wttm', 64), ('wttm', 128)])
                    for tti in range(16):
                        pi = nxt('pj', 2)
                        pj = PJ[pi]
                        mmg(pj[:, 0:256], [(hT[:, kc, tti * 128:(tti + 1) * 128], wt_tm[:, kc, 0:256]) for kc in range(8)],
                            [('wttm', 0), ('wttm', 64), ('wttm', 128)] + hTk(tti // 4), [PK[pi]])
                        cp('vector', Vb[:, tti, :, 0:64], pj[:, 0:256].rearrange("p (a d) -> p a d", a=4), [PK[pi]], [('Vb', tti)])
                    stg(3.65)
                    for h in range(4):
                        sch.op('vector', lambda e, h=h: e.tensor_reduce(out=ksf[:, h, :], in_=Kb[0:64, h, :].rearrange("p (j k) -> p j k", k=256),
                                                                        axis=AX.X, op=ALU.add),
                               [k for tb in range(4) for k in kkeys(h, tb)], [('ksf', h)])
                    cp('vector', ksb[:], ksf[:], [('ksf', h) for h in range(4)], ['ksb'])
                    for qt in range(8, 16):
                        qs = slice(qt * 128, (qt + 1) * 128)
                        cur = qt // 2
                        for h in range(4):
                            sch.op('tensor', mm(MS[:, h * 8:(h + 1) * 8], Qa[0:64, h, qs], ksb[:, h, :]), [('Qa', h, qt), 'ksb'], ['MS'])
                        tt('vector', gm[:], MS[:, 0:32].rearrange("p (h j) -> p h j", h=4),
                           cmo_t[:, qt - 8, :].unsqueeze(1).to_broadcast([128, 4, 8]), ALU.add, ['MS', 'cmo'], ['gm'])
                        for h in range(4):
                            sch.op('vector', lambda e, h=h: e.max(out=m8[:, h, :], in_=gm[:, h, :]), ['gm', 'm8'], ['m8'])
                        tt('vector', lt8[:], gm[:], m8[:, :, 2:3].to_broadcast([128, 4, 8]), ALU.is_lt, ['gm', 'm8'], ['lt8'])
                        ts('vector', T2[:, :, 64:72], lt8[:], NEGB, None, ALU.mult, None, ['lt8'], ['T2'])
                        mset('vector', T2[:, :, 64 + cur:65 + cur], 0.0, ['T2'])
                        sch.group('tensor', [(lambda e, h=h: e.transpose(TR[0:72, h * 128:(h + 1) * 128], T2[:, h, :], ident[:])) for h in range(4)],
                                  ['T2', 'ident'], ['TR'])
                        cp('scalar', Qa[64:72, 0:4, qs], TR[64:72, 0:512].rearrange("p (h t) -> p h t", h=4),
                           ['TR'], [('QaB', qt)])
                    if b == 0 and pas == 2:
                        dump('Qm', Qa[0:72, :, :], [72, 4, S], BF16, [k for h in range(4) for tb in range(4) for k in qkeys(h, tb)] + [('QaB', q) for q in range(8, 16)])
                        dump('Km', Kb[0:72, :, :], [72, 4, S], BF16, [k for h in range(4) for tb in range(4) for k in kkeys(h, tb)] + [('KbI', h) for h in range(4)])
                    stg(3.7)
                    for h in range(4):
                        for QB in range(4):
                            kd = 64 if QB < 2 else 72
                            units = []
                            for kt in range(0, 4 * QB + 4):
                                ql0 = max(0, kt - 4 * QB)
                                ncol = (4 - ql0) * 128
                                ks = slice(kt * 128, (kt + 1) * 128)
                                q0 = (4 * QB + ql0) * 128
                                mk = None
                                if kt >= 4 * QB:
                                    mk = (lambda P: (P[:, 0:128], mlow[:], 'mlow'))
                                rd = [('Kb', h, kt)] + [('Qa', h, 4 * QB + ql) for ql in range(ql0, 4)]
                                if kd == 72:
                                    rd += [('KbI', h)] + [('QaB', 4 * QB + ql) for ql in range(ql0, 4)]
                                units.append(dict(lhsT=Kb[0:kd, h, ks], rhs=Qa[0:kd, h, q0:(4 * QB + 4) * 128], n=128, ncol=ncol, reads=rd, mask=mk,
                                                  pv=[((ql - ql0) * 128, (lambda O, ql=ql: O[:, ql * 65:(ql + 1) * 65])) for ql in range(ql0, 4)],
                                                  v=Vb[:, kt, h, :], vreads=[('Vb', kt), 'Vb_ones']))

                            def mnorm(O, ok, h=h, QB=QB):
                                Ov = O[:, 0:260].rearrange("p (r c) -> p r c", r=4)
                                sch.op('vector', lambda e: e.reciprocal(out=sm[:, 4:8], in_=Ov[:, :, 64]), [ok], ['sm1'])
                                tt('vector', oacc[:, 4 * QB:4 * QB + 4, h, :], Ov[:, :, 0:64],
                                   sm[:, 4:8].unsqueeze(2).to_broadcast([128, 4, 64]), ALU.mult, [ok, 'sm1'],
                                   [('oacc', 4 * QB + i) for i in range(4)])
                            attn_round(units, None, None, mnorm)
                    chunk0 = 4 + 2 * g
                flush()
                if b == 0 and pas in (0, 2):
                    dump('oacc%d' % pas, oacc[:], [128, 16, 4, 64], F32, [('oacc', i) for i in range(16)])
                    if pas == 0:
                        stg(3)
                for q4 in range(4):
                    cp('vector', obf[:], oacc[:, 4 * q4:4 * q4 + 4, :, :].rearrange("p q h d -> p q (h d)"),
                       [('oacc', 4 * q4 + i) for i in range(4)], ['obf'])
                    OTC = int(os.environ.get('OT_CUT', '9'))
                    if OTC < 1:
                        continue
                    TRx, trk = TRB[q4 % 2]
                    sch.group('tensor', [(lambda e, ci=ci, ql=ql, TRx=TRx: e.transpose(TRx[:, (ci * 4 + ql) * 128:(ci * 4 + ql + 1) * 128],
                                                                              obf[:, ql, ci * 128:(ci + 1) * 128], ident[:]))
                                         for ci in range(2) for ql in range(4)], ['obf', 'ident'], [trk])
                    if OTC == 3:
                        mset('vector', oT[:, chunk0, q4 * 512:(q4 + 1) * 512], 1.0, [('oT', chunk0, q4)])
                        continue
                    if OTC == 4:
                        act(oT[:, chunk0, q4 * 512:q4 * 512 + 128], obf[:, 0, 0:128], AF.Identity, ['obf'], [('oT', chunk0, q4)])
                        continue
                    if OTC == 5:
                        act(t3[:, 0:128], TRx[:, 0:128], AF.Identity, [trk], ['t3'])
                        continue
                    for ci in range(2 if OTC >= 2 else 0):
                        for ql in range(4):
                            act(oT[:, chunk0 + ci, (q4 * 4 + ql) * 128:(q4 * 4 + ql + 1) * 128], TRx[:, (ci * 4 + ql) * 128:(ci * 4 + ql + 1) * 128],
                                AF.Identity, [trk], [('oT', chunk0 + ci, q4)])
                stg(3.2 + 0.2 * pas)
            sch.barrier()
            asx.close()
            dump('oT', oT[:], [128, 8, S], BF16, [('oT', c, q) for c in range(8) for q in range(4)])
            stg(4)

            with ExitStack() as xs:
                mixT = sbt([128, 8, S], BF16, xs, 'mixT')
                woutt = sbt([128, 8, 1024], BF16, xs, 'woutt')
                wb_t = [[sbt([128, 4, 128], BF16, xs, 'wbt') for _ in range(2)] for _ in range(2)]
                wg_t = [[sbt([128, 8, 128], BF16, xs, 'wgt') for _ in range(2)] for _ in range(2)]
                sga = sbt([128, 512], F32, xs, 'sga')
                sgb = sbt([128, 512], F32, xs, 'sgb')
                m1 = sbt([128, 512], F32, xs, 'm1')
                m2 = sbt([128, 512], F32, xs, 'm2')
                xts = [sbt([128, D], F32, xs, 'xt') for _ in range(2)]
                x1t = [sbt([128, D], F32, xs, 'x1t') for _ in range(2)]
                tmpx = sbt([128, 512], F32, xs, 'tmpx')
                tmpA = (sbt([128, D], BF16, xs, 'junk'), sbt([128, 1], F32, xs, 'ss'), sbt([128, 1], F32, xs, 'lnv'),
                        sbt([128, 1], F32, xs, 'rstd'), sbt([128, D], BF16, xs, 'xn'))
                sch.dma('sync', woutt[:], wout_s[:, :].rearrange("(kc p) c -> p kc c", p=128), reads=wkeys('wout'), writes=['woutt'])
                for fc in range(8):
                    wi = fc % 2
                    fs = slice(fc * 128, (fc + 1) * 128)
                    sch.dma('sync', wb_t[0][wi][:], wbn_s[:, fs].rearrange("(kc p) c -> p kc c", p=128), reads=wkeys('wbn', 4), writes=[('wbt', 0, wi)])
                    sch.dma('sync', wb_t[1][wi][:], wbm_s[:, fs].rearrange("(kc p) c -> p kc c", p=128), reads=wkeys('wbm', 4), writes=[('wbt', 1, wi)])
                    sch.dma('sync', wg_t[0][wi][:], win_s[:, O_GA + fc * 128:O_GA + (fc + 1) * 128].rearrange("(kc p) c -> p kc c", p=128),
                            reads=wkeys('win'), writes=[('wgt', 0, wi)])
                    sch.dma('sync', wg_t[1][wi][:], win_s[:, O_GB + fc * 128:O_GB + (fc + 1) * 128].rearrange("(kc p) c -> p kc c", p=128),
                            reads=wkeys('win'), writes=[('wgt', 1, wi)])
                    for tb in range(4):
                        tsl = tblk(tb)
                        mmg(PJ[0][:, :], [(wb_t[0][wi][:, kc, :], oT[:, kc, tsl]) for kc in range(4)],
                            [('wbt', 0, wi)] + [('oT', kc, tb) for kc in range(4)], ['PJ0'])
                        mmg(PJ[1][:, :], [(wg_t[0][wi][:, kc, :], hT[:, kc, tsl]) for kc in range(8)],
                            [('wgt', 0, wi)] + hTk(tb), ['PJ1'])
                        act(sga[:], PJ[1][:, :], AF.Sigmoid, ['PJ1'], ['sga'])
                        tt('vector', m1[:], PJ[0][:, :], sga[:], ALU.mult, ['PJ0', 'sga'], ['m1'])
                        mmg(SB[0][:, :], [(wb_t[1][wi][:, kc, :], oT[:, 4 + kc, tsl]) for kc in range(4)],
                            [('wbt', 1, wi)] + [('oT', 4 + kc, tb) for kc in range(4)], ['S0'])
                        mmg(SB[1][:, :], [(wg_t[1][wi][:, kc, :], hT[:, kc, tsl]) for kc in range(8)],
                            [('wgt', 1, wi)] + hTk(tb), ['S1'])
                        act(sgb[:], SB[1][:, :], AF.Sigmoid, ['S1'], ['sgb'])
                        tt('vector', m2[:], SB[0][:, :], sgb[:], ALU.mult, ['S0', 'sgb'], ['m2'])
                        tt('gpsimd', mixT[:, fc, tsl], m1[:], m2[:], ALU.add, ['m1', 'm2'], [('mixT', fc, tb)])
                for tti in range(16):
                    xi = tti % 2
                    tsl = slice(tti * 128, (tti + 1) * 128)
                    sch.dma('sync', xts[xi][:], x_d[b, tsl, :], writes=['xt%d' % xi])
                    for half in range(2):
                        hs = slice(half * 512, (half + 1) * 512)
                        mmg(PJ[half][:, :], [(mixT[:, kc, tsl], woutt[:, kc, hs]) for kc in range(8)],
                            ['woutt'] + [('mixT', kc, tti // 4) for kc in range(8)], [PK[half]])
                        tt('vector', tmpx[:], PJ[half][:, :], gtbc[:, hs], ALU.mult, [PK[half], ('gtbc', half)], ['tmpx'])
                        tt('gpsimd', x1t[xi][:, hs], tmpx[:], xts[xi][:, hs], ALU.add, ['tmpx', 'xt%d' % xi], [('x1t', xi, half)])
                    sch.dma('sync', x1_s[tsl, :], x1t[xi][:], reads=[('x1t', xi, 0), ('x1t', xi, 1)], writes=[('x1s', tti)])
                    norm_transpose(x1t[xi][:], [('x1t', xi, 0), ('x1t', xi, 1)], a2, b2, b, tti, tmpA)
                sch.barrier()
            ms.close()
            dump('h2T', hT[:], [128, 8, S], BF16, [k for tb in range(4) for k in hTk(tb)])
            stg(5)

            with ExitStack() as fs_:
                yT = sbt([128, 22, 1024], BF16, fs_, 'yT')
                wu = [sbt([128, 8, 256], BF16, fs_, 'wu') for _ in range(2)]
                wd = [sbt([128, 22, 256], BF16, fs_, 'wd') for _ in range(2)]
                aS = [sbt([128, 514], F32, fs_, 'aS') for _ in range(2)]
                halo = sbt([128, 22, 2], F32, fs_, 'halo')
                c1 = sbt([128, 512], F32, fs_, 'c1')
                c2 = sbt([128, 512], F32, fs_, 'c2')
                c3 = sbt([128, 512], F32, fs_, 'c3')
                gl = sbt([128, 512], F32, fs_, 'gl')
                x1q = [sbt([128, 256], F32, fs_, 'x1q') for _ in range(2)]
                oq = [sbt([128, 256], F32, fs_, 'oq') for _ in range(2)]
                tmpo = sbt([128, 256], F32, fs_, 'tmpo')
                mset('vector', halo[:], 0.0, ['halo'])
                blk = 0
                for hf in range(2):
                    for fc in range(22):
                        wi = fc % 2
                        sch.dma('sync', wu[wi][:, :, 0:128], wup_s[:, fc * 128:(fc + 1) * 128].rearrange("(kc p) c -> p kc c", p=128),
                                reads=wkeys('wup'), writes=[('wu', wi, 0)])
                        sch.dma('sync', wu[wi][:, :, 128:256], wup_s[:, DFF + fc * 128:DFF + (fc + 1) * 128].rearrange("(kc p) c -> p kc c", p=128),
                                reads=wkeys('wup'), writes=[('wu', wi, 1)])
                        for tb2 in range(2):
                            tok0 = hf * 1024 + tb2 * 512
                            tbg = tok0 // 512
                            ai = blk % 2
                            blk += 1
                            a_ = aS[ai]
                            ak = 'aS%d' % ai
                            Ab, Ak = [(PJ[0], 'PJ0'), (SB[0], 'S0')][blk % 2]
                            Vb_, Vk = [(PJ[1], 'PJ1'), (SB[1], 'S1'), (OB[0], 'O0'), (OB[1], 'O1')][blk % 4]
                            mmg(Ab[:, :], [(wu[wi][:, kc, 0:128], hT[:, kc, tok0:tok0 + 512]) for kc in range(8)],
                                [('wu', wi, 0)] + hTk(tbg), [Ak])
                            mmg(Vb_[:, :], [(wu[wi][:, kc, 128:256], hT[:, kc, tok0:tok0 + 512]) for kc in range(8)],
                                [('wu', wi, 1)] + hTk(tbg), [Vk])
                            cp('gpsimd', a_[:, 0:2], halo[:, fc, :], ['halo'], [ak])
                            cp('scalar', a_[:, 2:514], Ab[:, :], [Ak], [ak])
                            act(c1[:], a_[:, 0:512], AF.Identity, [ak, 'cw', 'cb'], ['c1'], bias=cb[:, fc:fc + 1], scale=cw[:, fc, 0:1])
                            stt('vector', c2[:], a_[:, 1:513], cw[:, fc, 1:2], c1[:], ALU.mult, ALU.add, [ak, 'c1', 'cw'], ['c2'])
                            stt('vector', c3[:], a_[:, 2:514], cw[:, fc, 2:3], c2[:], ALU.mult, ALU.add, [ak, 'c2', 'cw'], ['c3'])
                            cp('gpsimd', halo[:, fc, :], a_[:, 512:514], [ak], ['halo'])
                            act(gl[:], c3[:], AF.Gelu_apprx_tanh, ['c3'], ['gl'])
                            tt('vector', yT[:, fc, tb2 * 512:(tb2 + 1) * 512], gl[:], Vb_[:, :], ALU.mult, ['gl', Vk], [('yT', fc, tb2)])
                    for nq in range(4):
                        wi = nq % 2
                        ns = slice(nq * 256, (nq + 1) * 256)
                        for (r0, r1) in ((0, 8), (8, 16), (16, 22)):
                            sch.dma('sync', wd[wi][:, r0:r1, :], wdn_s[r0 * 128:r1 * 128, ns].rearrange("(kc p) c -> p kc c", p=128),
                                    reads=wkeys('wdn', 22), writes=[('wd', wi, r0)])
                        for t8 in range(8):
                            tti = hf * 8 + t8
                            tsl = slice(tti * 128, (tti + 1) * 128)
                            pi = nxt('pj', 2)
                            oi = t8 % 2
                            mmg(PJ[pi][:, 0:256], [(yT[:, fc, t8 * 128:(t8 + 1) * 128], wd[wi][:, fc, :]) for fc in range(22)],
                                [('wd', wi, 0), ('wd', wi, 8), ('wd', wi, 16)] + [('yT', fc, t8 // 4) for fc in range(22)], [PK[pi]])
                            sch.dma('sync', x1q[oi][:], x1_s[tsl, ns], reads=[('x1s', tti)], writes=['x1q%d' % oi])
                            tt('vector', tmpo[:], PJ[pi][:, 0:256], gtbc[:, 1024 + nq * 256:1024 + (nq + 1) * 256], ALU.mult,
                               [PK[pi], ('gtbc', 2), ('gtbc', 3)], ['tmpo'])
                            tt('gpsimd', oq[oi][:], tmpo[:], x1q[oi][:], ALU.add, ['tmpo', 'x1q%d' % oi], ['oq%d' % oi])
                            sch.dma('sync', out_d[b, tsl, ns], oq[oi][:], reads=['oq%d' % oi], writes=[('out', b, tti, nq)])
                sch.barrier()
            bs.close()
      except StopBuild:
        sch.barrier()
        asx.close(); ms.close(); bs.close()
        break

    sch.finish()
    with nc.Block() as block:
        @block.sync
        def _(e):
            for f in sch.streams['sync']:
                f(e)

        @block.scalar
        def _(e):
            for f in sch.streams['scalar']:
                f(e)

        @block.vector
        def _(e):
            for f in sch.streams['vector']:
                f(e)

        @block.gpsimd
        def _(e):
            for f in sch.streams['gpsimd']:
                f(e)

        @block.tensor
        def _(e):
            for f in sch.streams['tensor']:
                f(e)
    es.close()
    return nc, dbg_outs, sch


def make_in_maps(inputs, nb=4, ncores=NCORE, batches=None):
    f = lambda a: np.ascontiguousarray(np.asarray(a, dtype=np.float32))
    x = np.asarray(inputs['x'])
    c = np.asarray(inputs['c'], dtype=np.float32)
    pos = np.asarray(inputs['positions']).astype(np.int32)
    col = lambda v: np.ascontiguousarray(np.asarray(v, np.float32).reshape(-1, 128).T)
    tile2 = lambda v: np.concatenate([np.asarray(v, np.float32)] * 2)
    hgm = np.zeros((128, 8), np.float32)
    hgm[:, 0] = tile2(inputs['g_q_nsa'][0])
    hgm[:, 1] = np.concatenate([inputs['g_k_slc'][0], inputs['g_k_win'][0]])
    hgm[:, 2] = tile2(inputs['g_q_moba'][0])
    hgm[:, 3] = tile2(inputs['g_k_moba'][0])
    hgm[:, 4] = tile2(inputs['g_k_cmp'][0])
    shared = {
        'bada': f(inputs['b_ada'][0][None, :]),
        'badaT': col(inputs['b_ada'][0]),
        'gcol': np.concatenate([col(inputs['g_attn_norm'][0]), col(inputs['g_ffn_norm'][0])], 1),
        'hg': hgm,
        'cw': np.ascontiguousarray(np.asarray(inputs['conv_w'][0], np.float32).T.reshape(22, 128, 3).transpose(1, 0, 2)),
        'cb': col(inputs['conv_b'][0]),
        'wposk': f(np.asarray(inputs['cmp_k_pos'][0]).T),
        'wposv': f(np.asarray(inputs['cmp_v_pos'][0]).T),
        'w_ada': f(inputs['w_ada'][0]), 'w_in': f(inputs['w_in'][0]),
        'w1k': f(inputs['cmp_k_w1'][0]), 'w2k': f(inputs['cmp_k_w2'][0]),
        'w1v': f(inputs['cmp_v_w1'][0]), 'w2v': f(inputs['cmp_v_w2'][0]),
        'wbn': f(inputs['w_branch_nsa'][0]), 'wbm': f(inputs['w_branch_moba'][0]),
        'wout': f(inputs['w_out'][0]), 'wup': f(inputs['w_ffn_up'][0]), 'wdn': f(inputs['w_ffn_down'][0]),
    }
    for k, v in _consts().items():
        shared['c_' + k] = v
    maps = []
    for ci in range(ncores):
        bl = batches[ci] if batches is not None else list(range(ci * nb, (ci + 1) * nb))
        m = dict(shared)
        m['x'] = f(x[bl])
        m['pos'] = np.ascontiguousarray(pos[bl])
        cc = np.zeros((4, 1024), np.float32)
        cc[:len(bl)] = c[bl]
        m['cT'] = np.ascontiguousarray(cc.T.reshape(8, 128, 4).transpose(1, 0, 2))
        maps.append(m)
    return maps


_CACHE = {}


def kernel(**inputs):
    nb = 4
    if 'nc' not in _CACHE:
        _CACHE['nc'] = build(nb)[0]
    nc = _CACHE['nc']
    maps = make_in_maps(inputs, nb)
    res = run_bass_kernel_spmd(nc, maps, core_ids=list(range(NCORE)))
    out = np.concatenate([np.asarray(r['out']) for r in res.results], axis=0)
    return out.astype(np.float32)
```

```python
import os
import numpy as np
from contextlib import ExitStack
import concourse.bass as bass
import concourse.mybir as mybir
from concourse.bass_utils import run_bass_kernel_spmd

F32 = mybir.dt.float32
BF16 = mybir.dt.bfloat16
I32 = mybir.dt.int32
AF = mybir.ActivationFunctionType
ALU = mybir.AluOpType
AX = mybir.AxisListType

S = 2048
D = 1024
DFF = 2816
INW = 4888
NCORE = 8
EPS = 1e-6
NEGB = -240000.0
ENG = ['sync', 'scalar', 'vector', 'gpsimd', 'tensor']
NSLOT = 8

O_QA, O_KC, O_VC, O_KSL, O_VSL, O_KWN, O_VWN, O_GN, O_QB, O_KB, O_VB, O_GA, O_GB = (
    0, 512, 640, 768, 896, 1024, 1152, 1280, 1304, 1816, 2328, 2840, 3864)


class Sched:
    def __init__(self, nc, es):
        self.nc = nc
        self.streams = {e: [] for e in ENG}
        self.sem = {}
        for e in ENG:
            self.sem[e] = es.enter_context(nc.semaphore('p_' + e))
        self.cnt = {e: 0 for e in ENG}
        self.dq = {'sync': 0, 'gpsimd': 0, 'bg': 0}
        for q in self.dq:
            for i in range(NSLOT):
                self.sem[(q, i)] = es.enter_context(nc.semaphore('d_%s%d' % (q, i)))
        self.seen = {e: {} for e in ENG}
        self.lw = {}
        self.rd = {}

    def _deps(self, reads, writes):
        deps = {}

        def add(tok):
            if tok is not None:
                deps[tok[0]] = max(deps.get(tok[0], 0), tok[1])
        for k in reads:
            add(self.lw.get(k))
        for k in writes:
            add(self.lw.get(k))
            for sk, v in self.rd.get(k, {}).items():
                add((sk, v))
        return deps

    def _emit_waits(self, eng, deps):
        for sk, v in deps.items():
            if eng == 'tensor' and sk == 'tensor':
                continue
            if self.seen[eng].get(sk, 0) < v:
                self.seen[eng][sk] = v
                h = self.sem[sk]
                self.streams[eng].append(lambda e, h=h, v=v: e.wait_ge(h, v))

    def _commit(self, tok, reads, writes):
        for k in reads:
            d = self.rd.setdefault(k, {})
            d[tok[0]] = max(d.get(tok[0], 0), tok[1])
        for k in writes:
            self.lw[k] = tok
            self.rd[k] = {}

    def group(self, eng, fns, reads=(), writes=()):
        deps = self._deps(reads, writes)
        self._emit_waits(eng, deps)
        self.cnt[eng] += 1
        h = self.sem[eng]
        for f in fns[:-1]:
            self.streams[eng].append(lambda e, f=f: f(e))
        f = fns[-1]
        self.streams[eng].append(lambda e, f=f, h=h: f(e).then_inc(h, 1))
        tok = (eng, self.cnt[eng])
        self._commit(tok, reads, writes)
        return tok

    def op(self, eng, fn, reads=(), writes=()):
        return self.group(eng, [fn], reads, writes)

    def dma(self, q, out, in_, reads=(), writes=()):
        n = self.dq[q]
        self.dq[q] += 1
        slot = n % NSLOT
        val = 16 * (n // NSLOT + 1)
        sk = (q, slot)
        deps = self._deps(reads, writes)
        if val > 16:
            deps[sk] = max(deps.get(sk, 0), val - 16)
        qe = 'gpsimd' if q == 'bg' else q
        self._emit_waits(qe, deps)
        h = self.sem[sk]
        self.streams[qe].append(lambda e, out=out, in_=in_, h=h: e.dma_start(out=out, in_=in_).then_inc(h, 16))
        tok = (sk, val)
        self._commit(tok, reads, writes)
        return tok

    def _all_tokens(self, bg=False):
        d = {e: self.cnt[e] for e in ENG if self.cnt[e] > 0}
        for q, n in self.dq.items():
            if q == 'bg' and not bg:
                continue
            for slot in range(NSLOT):
                if n > slot:
                    d[(q, slot)] = 16 * ((n - 1 - slot) // NSLOT + 1)
        return d

    def barrier(self):
        allt = self._all_tokens()
        for e in ENG:
            deps = {k: v for k, v in allt.items() if k != e}
            self._emit_waits(e, deps)

    def finish(self):
        allt = self._all_tokens(bg=True)
        self._emit_waits('sync', {k: v for k, v in allt.items() if k != 'sync'})


def _consts():
    c = {}
    c['ident'] = np.eye(128, dtype=np.float32)
    bd = np.zeros((128, 128), np.float32)
    bd[:64, :64] = 1
    bd[64:, 64:] = 1
    c['bd64'] = bd
    rot = np.zeros((128, 128), np.float32)
    for m in range(128):
        partner = m + 32 if (m % 64) < 32 else m - 32
        rot[partner, m] = 1
    c['rotm'] = rot
    p = np.arange(128)
    invf = (10000.0 ** (-(p % 32).astype(np.float64) / 32.0)) / (2 * np.pi)
    sgn = np.where((p % 64) < 32, -1.0, 1.0) * 6.28318
    c['invf'] = np.stack([invf, sgn, np.full(128, EPS), np.full(128, 6.28318)], 1).astype(np.float32)
    k = np.arange(128)[:, None]
    t = np.arange(128)[None, :]
    c['mlow'] = (k <= t).astype(np.float32)
    c['mup'] = (k > t).astype(np.float32)
    cm = np.zeros((128, 16, 128), np.float32)
    cc = np.arange(128)[:, None, None]
    tt = (np.arange(16)[None, :, None] * 128 + np.arange(128)[None, None, :])
    cm[:] = (16 * cc + 31 <= tt)
    cm[127] = 0
    c['cmpmask'] = cm
    ovl = np.zeros((128, 33), np.float32)
    ovl[:, 0] = 1
    cs = np.arange(127)[:, None] * 16
    js = np.arange(32)[None, :]
    ovl[:127, 1:] = ((cs < (js + 1) * 64) & (cs + 32 > js * 64)).astype(np.float32)
    c['ovl'] = ovl
    cn = np.zeros((128, 8, 32), np.float32)
    for qi in range(8):
        tq = (qi + 8) * 128 + np.arange(128)
        own = tq // 64
        jb = np.arange(32)[None, :]
        a = np.zeros((128, 32), np.float32)
        a[jb == 0 + 0 * own[:, None]] = 1e4
        a = np.where(jb == own[:, None], 2e4, a)
        a = np.where(jb == own[:, None] - 1, 3e4, a)
        a = np.where(jb > own[:, None], -1e30, a)
        cn[:, qi, :] = a
    c['cn'] = cn
    cmo = np.zeros((128, 8, 8), np.float32)
    for qi in range(8):
        cur = (qi + 8) // 2
        cmo[:, qi, cur:] = -1e30
    c['cmo'] = cmo
    kk = np.arange(2048)[None, :]
    c['ind32'] = (kk // 64 == np.arange(32)[:, None]).astype(np.float32)
    c['ind8'] = (kk // 256 == np.arange(8)[:, None]).astype(np.float32)
    oh = np.zeros((4, 4, 128), np.float32)
    for j in range(4):
        oh[j, j, :] = 1
    c['oh4'] = oh
    return c


CONST_SHAPES = {k: v.shape for k, v in _consts().items()}


class StopBuild(Exception):
    pass


def build(nb=4, dbg=None, stage=99):
    dbg = dbg or set()

    def stg(n):
        if stage <= n:
            raise StopBuild()
    nc = bass.Bass("TRN2", target_bir_lowering=False)
    es = ExitStack()
    uid = [0]

    def din(name, shape, dt=F32):
        return nc.dram_tensor(name, list(shape), dt, kind="ExternalInput").ap()

    def dscr(name, shape, dt=BF16):
        return nc.dram_tensor(name, list(shape), dt, kind="Internal").ap()

    def sbt(shape, dt, scope=None, name='t'):
        uid[0] += 1
        return (scope or es).enter_context(nc.sbuf_tensor("%s_%d" % (name, uid[0]), list(shape), dt))

    x_d = din("x", [nb, S, D])
    pos_d = din("pos", [nb, S], I32)
    cT_d = din("cT", [128, 8, 4])
    bada_d = din("bada", [1, 6144])
    badaT_d = din("badaT", [128, 48])
    gcol_d = din("gcol", [128, 16])
    hg_d = din("hg", [128, 8])
    cw_d = din("cw", [128, 22, 3])
    cb_d = din("cb", [128, 22])
    wposk_d = din("wposk", [64, 32])
    wposv_d = din("wposv", [64, 32])
    wada_d = din("w_ada", [1024, 6144])
    win_d = din("w_in", [1024, INW])
    w1k_d = din("w1k", [2048, 64])
    w2k_d = din("w2k", [64, 64])
    w1v_d = din("w1v", [2048, 64])
    w2v_d = din("w2v", [64, 64])
    wbn_d = din("wbn", [512, 1024])
    wbm_d = din("wbm", [512, 1024])
    wout_d = din("wout", [1024, 1024])
    wup_d = din("wup", [1024, 2 * DFF])
    wdn_d = din("wdn", [DFF, 1024])
    C = {k: din("c_" + k, shp) for k, shp in CONST_SHAPES.items()}
    out_d = nc.dram_tensor("out", [nb, S, D], F32, kind="ExternalOutput").ap()
    dbg_outs = {}

    wada_s = dscr("wada_s", [1024, 6144])
    win_s = dscr("win_s", [1024, INW])
    wbn_s = dscr("wbn_s", [512, 1024])
    wbm_s = dscr("wbm_s", [512, 1024])
    wout_s = dscr("wout_s", [1024, 1024])
    wup_s = dscr("wup_s", [1024, 2 * DFF])
    wdn_s = dscr("wdn_s", [DFF, 1024])
    x1_s = dscr("x1_s", [S, D], F32)
    gt_scr = dscr("gt_scr", [4, 2048], F32)

    sch = Sched(nc, es)
    ps = [es.enter_context(nc.psum_tensor("ps%d" % i, [128, 512], F32)) for i in range(8)]
    SB = [ps[0], ps[1]]
    OB = [ps[2], ps[3]]
    PJ = [ps[4], ps[5]]
    TRf = ps[6]
    MS = ps[7]
    TR = TRf[:].bitcast(BF16)
    TRB = [(TR, 'TR'), (MS[:].bitcast(BF16), 'MS')]
    SK = ['S0', 'S1']
    OK_ = ['O0', 'O1']
    PK = ['PJ0', 'PJ1']

    def act(out, in_, func, reads, writes, bias=None, scale=None, accum=None):
        kw = {}
        if bias is not None:
            kw['bias'] = bias
        if scale is not None:
            kw['scale'] = scale
        if accum is not None:
            kw['accum_out'] = accum
        return sch.op('scalar', lambda e: e.activation(out=out, in_=in_, func=func, **kw), reads, writes)

    def tt(eng, out, in0, in1, op, reads, writes):
        return sch.op(eng, lambda e: e.tensor_tensor(out=out, in0=in0, in1=in1, op=op), reads, writes)

    def ts(eng, out, in0, s1, s2, op0, op1, reads, writes):
        if s2 is None:
            return sch.op(eng, lambda e: e.tensor_scalar(out=out, in0=in0, scalar1=s1, scalar2=None, op0=op0), reads, writes)
        return sch.op(eng, lambda e: e.tensor_scalar(out=out, in0=in0, scalar1=s1, scalar2=s2, op0=op0, op1=op1), reads, writes)

    def stt(eng, out, in0, scalar, in1, op0, op1, reads, writes):
        return sch.op(eng, lambda e: e.scalar_tensor_tensor(out=out, in0=in0, scalar=scalar, in1=in1, op0=op0, op1=op1), reads, writes)

    def cp(eng, out, in_, reads, writes):
        if eng == 'scalar':
            return act(out, in_, AF.Copy, reads, writes)
        return sch.op(eng, lambda e: e.tensor_copy(out=out, in_=in_), reads, writes)

    def mset(eng, ap, val, writes):
        return sch.op(eng, lambda e: e.memset(ap, val), (), writes)

    def mm(out, lhsT, rhs, start=True, stop=True):
        return lambda e: e.matmul(out, lhsT, rhs, start=start, stop=stop)

    def mmg(out, pairs, reads, writes):
        n = len(pairs)
        return sch.group('tensor', [mm(out, l, r, i == 0, i == n - 1) for i, (l, r) in enumerate(pairs)], reads, writes)

    def trp(out, in_, ident_ap, reads, writes):
        return sch.op('tensor', lambda e: e.transpose(out, in_, ident_ap), reads, writes)

    def dump(name, ap, shape, dt, reads):
        if name not in dbg:
            return
        t = nc.dram_tensor("dbg_" + name, list(shape), dt, kind="ExternalOutput").ap()
        dbg_outs[name] = t
        sch.dma('sync', t, ap, reads=reads)

    def wkeys(name, n=8):
        return [(name, i) for i in range(n)]

    def conv(dst, src, rows, name):
        for r in range(0, rows, 128):
            sch.dma('bg', dst[r:r + 128, :], src[r:r + 128, :], writes=[(name, r // 128)])

    conv(wada_s, wada_d, 1024, 'wada')
    cst = {}

    def cload(name, src, shape, dt, q=None):
        t = sbt(shape, dt, name=name)
        q = q or ('gpsimd' if dt == BF16 else 'sync')
        sch.dma(q, t[:], src, writes=[name])
        cst[name] = t
        return t

    ident = cload('ident', C['ident'], [128, 128], BF16)
    bd64 = cload('bd64', C['bd64'], [128, 128], BF16)
    rotm = cload('rotm', C['rotm'], [128, 128], BF16)
    invf = cload('invf', C['invf'], [128, 4], F32)
    mlow = cload('mlow', C['mlow'], [128, 128], BF16)
    mup = cload('mup', C['mup'], [128, 128], BF16)
    cmpmask = cload('cmpmask', C['cmpmask'], [128, 16, 128], BF16)
    cn_t = cload('cn', C['cn'], [128, 8, 32], F32)
    cmo_t = cload('cmo', C['cmo'], [128, 8, 8], F32)
    oh4 = cload('oh4', C['oh4'], [4, 4, 128], F32)
    cTb = cload('cTb', cT_d, [128, 8, 4], BF16)
    badaT = cload('badaT', badaT_d, [128, 48], F32)
    gcol = cload('gcol', gcol_d, [128, 16], F32)
    hg = cload('hg', hg_d, [128, 8], F32)
    cw = cload('cw', cw_d, [128, 22, 3], F32)
    cb = cload('cb', cb_d, [128, 22], F32)
    w1k = cload('w1k', w1k_d.rearrange("(l d) j -> d l j", d=64), [64, 32, 64], BF16)
    w1v = cload('w1v', w1v_d.rearrange("(l d) j -> d l j", d=64), [64, 32, 64], BF16)
    w2k = cload('w2k', w2k_d, [64, 64], BF16)
    w2v = cload('w2v', w2v_d, [64, 64], BF16)
    wposk = cload('wposk', wposk_d, [64, 32], BF16)
    wposv = cload('wposv', wposv_d, [64, 32], BF16)
    VCO = sbt([128, 97], BF16, name='VCO')
    sch.dma('gpsimd', VCO[:, 64:97], C['ovl'], writes=['VCO'])
    bada4 = sbt([4, 2048], F32, name='bada4')
    sch.dma('sync', bada4[:, 0:1024], bada_d[0:1, 2048:3072].to_broadcast([4, 1024]), writes=['bada4a'])
    sch.dma('sync', bada4[:, 1024:2048], bada_d[0:1, 5120:6144].to_broadcast([4, 1024]), writes=['bada4b'])

    conv(win_s, win_d, 1024, 'win')
    conv(wbn_s, wbn_d, 512, 'wbn')
    conv(wbm_s, wbm_d, 512, 'wbm')
    conv(wout_s, wout_d, 1024, 'wout')
    conv(wup_s, wup_d, 1024, 'wup')
    conv(wdn_s, wdn_d, DFF, 'wdn')

    modT = sbt([128, 32, 4], F32, name='modT')
    rows_gt = sbt([4, 2048], F32, name='rowsgt')
    a1 = sbt([128, 8, 4], F32, name='a1')
    a2 = sbt([128, 8, 4], F32, name='a2')
    cposk = sbt([64, 1], F32, name='cposk')
    cposv = sbt([64, 1], F32, name='cposv')
    with ExitStack() as sc:
        wts = [sbt([128, 8, 512], BF16, sc, 'wadat') for _ in range(2)]
        tmp84 = sbt([128, 8, 4], F32, sc, 'tmp84')
        for ct in range(12):
            wt = wts[ct % 2]
            wk = 'wadat%d' % (ct % 2)
            sch.dma('sync', wt[:], wada_s[:, ct * 512:(ct + 1) * 512].rearrange("(kc p) c -> p kc c", p=128),
                    reads=wkeys('wada'), writes=[wk])
            sec = ct // 2
            if sec in (2, 5):
                pj = PJ[ct % 2]
                mmg(pj[0:4, :], [(cTb[:, kc, :], wt[:, kc, :]) for kc in range(8)], [wk, 'cTb'], [PK[ct % 2]])
                gi = (0 if sec == 2 else 1) * 1024 + (ct % 2) * 512
                tt('vector', rows_gt[:, gi:gi + 512], pj[0:4, :], bada4[:, gi:gi + 512], ALU.add,
                   [PK[ct % 2], 'bada4a', 'bada4b'], [('rowsgt', gi // 512)])
            else:
                mi = {0: 0, 1: 8, 3: 16, 4: 24}[sec] + (ct % 2) * 4
                for j in range(4):
                    mmg(MS[:, j * 4:(j + 1) * 4], [(wt[:, kc, j * 128:(j + 1) * 128], cTb[:, kc, :]) for kc in range(8)],
                        [wk, 'cTb'], ['MS'])
                tt('vector', modT[:, mi:mi + 4, :], MS[:, 0:16].rearrange("p (j b) -> p j b", j=4),
                   badaT[:, ct * 4:ct * 4 + 4].unsqueeze(2).to_broadcast([128, 4, 4]), ALU.add,
                   ['MS', 'badaT'], [('modT', mi // 4)])
        ts('vector', tmp84[:], modT[:, 8:16, :], 1.0, None, ALU.add, None, [('modT', 2), ('modT', 3)], ['tmp84'])
        tt('vector', a1[:], tmp84[:], gcol[:, 0:8].unsqueeze(2).to_broadcast([128, 8, 4]), ALU.mult, ['tmp84', 'gcol'], ['a1'])
        ts('vector', tmp84[:], modT[:, 24:32, :], 1.0, None, ALU.add, None, [('modT', 6), ('modT', 7)], ['tmp84'])
        tt('vector', a2[:], tmp84[:], gcol[:, 8:16].unsqueeze(2).to_broadcast([128, 8, 4]), ALU.mult, ['tmp84', 'gcol'], ['a2'])
        for (w1t, wpt, cpo, nm) in ((w1k, wposk, cposk, 'k'), (w1v, wposv, cposv, 'v')):
            mmg(MS[0:64, 0:1], [(w1t[:, l, :], wpt[:, l:l + 1]) for l in range(32)], ['w1' + nm, 'wpos' + nm], ['MS'])
            cp('vector', cpo[:], MS[0:64, 0:1], ['MS'], ['cpos' + nm])
    sch.dma('sync', gt_scr[:, :], rows_gt[:], reads=[('rowsgt', i) for i in range(4)], writes=['gtscr'])
    b1 = modT[:, 0:8, :]
    b2 = modT[:, 16:24, :]
    MODK = [('modT', i) for i in range(8)] + ['a1', 'a2']
    sch.barrier()
    stage0 = stage <= 0

    def norm_transpose(src_tile, srck, aa, bb, b, tti, tmp):
        junk, ss, lnv, rstd, xn = tmp
        CUT = int(os.environ.get('PHASEA_CUT', '99'))
        mset('vector', ss[:], 0.0, ['ss'])
        act(junk[:], src_tile, AF.Square, srck + ['ss'], ['junk', 'ss'], accum=ss[:, 0:1])
        if CUT < 1:
            return
        act(lnv[:], ss[:], AF.Ln, ['ss'], ['lnv'], bias=invf[:, 2:3], scale=1.0 / D)
        act(rstd[:], lnv[:], AF.Exp, ['lnv'], ['rstd'], scale=-0.5)
        if CUT < 2:
            return
        ts('vector', xn[:], src_tile, rstd[:, 0:1], None, ALU.mult, None, srck + ['rstd'], ['xn'])
        if CUT < 3:
            return
        TRx, trk = TRB[tti % 2]
        sch.group('tensor', [(lambda e, c=c: e.transpose(TRx[:, c * 128:(c + 1) * 128], xn[:, c * 128:(c + 1) * 128], ident[:])) for c in range(8)],
                  ['xn', 'ident'], [trk])
        if CUT < 4:
            return
        for c in range(8):
            dst = hT[:, c, tti * 128:(tti + 1) * 128]
            if True:
                act(dst, TRx[:, c * 128:(c + 1) * 128], AF.Identity, [trk] + MODK, [('hT', c, tti // 4)],
                    bias=bb[:, c, b:b + 1], scale=aa[:, c, b:b + 1])
            else:
                stt('vector', dst, TRx[:, c * 128:(c + 1) * 128], aa[:, c, b:b + 1], bb[:, c, b:b + 1].to_broadcast([128, 128]), ALU.mult, ALU.add,
                    [trk] + MODK, [('hT', c, tti // 4)])

    def hTk(tb):
        return [('hT', c, tb) for c in range(8)]

    for b in range(0 if stage0 else nb):
      bs = ExitStack()
      ms = ExitStack()
      asx = ExitStack()
      try:
            cosT = sbt([128, S], BF16, bs, 'cosT')
            sinS = sbt([128, S], BF16, bs, 'sinS')
            gtbc = sbt([128, 2048], F32, bs, 'gtbc')
            hT = sbt([128, 8, S], BF16, bs, 'hT')
            with ExitStack() as sc:
                posi = sbt([128, S], I32, sc, 'posi')
                u = sbt([128, S], F32, sc, 'u')
                ni = sbt([128, S], I32, sc, 'ni')
                nf = sbt([128, S], F32, sc, 'nf')
                sch.dma('sync', posi[:], pos_d[b:b + 1, :].to_broadcast([128, S]), writes=['posi'])
                cp('vector', u[:], posi[:], ['posi'], ['u'])
                ts('vector', u[:], u[:], invf[:, 0:1], None, ALU.mult, None, ['u', 'invf'], ['u'])
                cp('vector', ni[:], u[:], ['u'], ['ni'])
                cp('vector', nf[:], ni[:], ['ni'], ['nf'])
                tt('vector', nf[:], u[:], nf[:], ALU.subtract, ['u', 'nf'], ['nf'])
                act(sinS[:], nf[:], AF.Sin, ['nf', 'invf'], ['sinS'], scale=invf[:, 1:2])
                ts('vector', u[:], u[:], 0.25, None, ALU.add, None, ['u'], ['u'])
                cp('vector', ni[:], u[:], ['u'], ['ni'])
                cp('vector', nf[:], ni[:], ['ni'], ['nf'])
                tt('vector', nf[:], u[:], nf[:], ALU.subtract, ['u', 'nf'], ['nf'])
                act(cosT[:], nf[:], AF.Sin, ['nf', 'invf'], ['cosT'], scale=invf[:, 3:4])
                sch.barrier()
            stg(0.3)
            sch.dma('sync', gtbc[:], gt_scr[b:b + 1, :].to_broadcast([128, 2048]), reads=['gtscr'], writes=[('gtbc', i) for i in range(4)])
            stg(0.6)
            with ExitStack() as sc:
                xts = [sbt([128, D], F32, sc, 'xt') for _ in range(2)]
                tmpA = (sbt([128, D], BF16, sc, 'junk'), sbt([128, 1], F32, sc, 'ss'), sbt([128, 1], F32, sc, 'lnv'),
                        sbt([128, 1], F32, sc, 'rstd'), sbt([128, D], BF16, sc, 'xn'))
                for tti in range(16):
                    xt = xts[tti % 2]
                    sch.dma('sync', xt[:], x_d[b, tti * 128:(tti + 1) * 128, :], writes=['xt%d' % (tti % 2)])
                    norm_transpose(xt[:], ['xt%d' % (tti % 2)], a1, b1, b, tti, tmpA)
                sch.barrier()
            dump('hT', hT[:], [128, 8, S], BF16, [k for tb in range(4) for k in hTk(tb)])
            dump('cosT', cosT[:], [128, S], BF16, ['cosT'])
            dump('sinS', sinS[:], [128, S], BF16, ['sinS'])
            stg(1)

            oT = sbt([128, 8, S], BF16, ms, 'oT')
            Qa = sbt([96, 4, S], BF16, asx, 'Qa')
            Kb = sbt([96, 4, S], BF16, asx, 'Kb')
            Vb = sbt([128, 16, 4, 65], BF16, asx, 'Vb')
            oacc = sbt([128, 16, 4, 64], F32, asx, 'oacc')
            gates = sbt([128, 16, 12], F32, asx, 'gates')
            wt_fm = [sbt([128, 8, 128], BF16, asx, 'wtfm') for _ in range(2)]
            wt_tm = sbt([128, 8, 256], BF16, asx, 'wttm')
            qn = sbt([128, 512], BF16, asx, 'qn')
            qnB = sbt([128, 512], BF16, asx, 'qnB')
            t3B = sbt([128, 512], BF16, asx, 't3B')
            sq = sbt([128, 512], BF16, asx, 'sq')
            t3 = sbt([128, 512], BF16, asx, 't3')
            t1 = sbt([128, 512], F32, asx, 't1')
            t2 = sbt([128, 512], F32, asx, 't2')
            lnt = sbt([128, 512], F32, asx, 'lnt')
            rst = sbt([128, 512], F32, asx, 'rst')
            Pt = [sbt([128, 512], BF16, asx, 'P') for _ in range(3)]
            Tst = sbt([128, 96], BF16, asx, 'Tst')
            T2 = sbt([128, 4, 72], BF16, asx, 'T2')
            hid = sbt([64, 128], BF16, asx, 'hid')
            kcn = sbt([64, 128], BF16, asx, 'kcn')
            sm = sbt([128, 64], F32, asx, 'sm')
            imp3 = sbt([128, 4, 32], F32, asx, 'imp3')
            impm = sbt([128, 32], F32, asx, 'impm')
            impr = sbt([128, 32], F32, asx, 'impr')
            m8 = sbt([128, 4, 8], F32, asx, 'm8')
            gm = sbt([128, 4, 8], F32, asx, 'gm')
            lt8 = sbt([128, 4, 8], F32, asx, 'lt8')
            ksf = sbt([64, 4, 8], F32, asx, 'ksf')
            ksb = sbt([64, 4, 8], BF16, asx, 'ksb')
            otmp = sbt([128, 4, 64], F32, asx, 'otmp')
            obf = sbt([128, 4, 256], BF16, asx, 'obf')
            mset('vector', Vb[:, :, :, 64:65], 1.0, ['Vb_ones'])
            mset('vector', Tst[:], 0.0, ['Tst'])
            mset('vector', T2[:], 0.0, ['T2'])
            cnt = {'pj': 0, 's': 0, 'o': 0, 'p': 0, 'w': 0}

            def nxt(k, n):
                v = cnt[k] % n
                cnt[k] += 1
                return v

            def load_w_fm(pieces):
                i = nxt('w', 2)
                wt = wt_fm[i]
                c = 0
                for (c0, n) in pieces:
                    sch.dma('sync', wt[:, :, c:c + n], win_s[:, c0:c0 + n].rearrange("(kc p) c -> p kc c", p=128),
                            reads=wkeys('win'), writes=[('wtfm', i, c)])
                    c += n
                return wt, [('wtfm', i, cc) for cc in (0, 64)]

            qn2 = [qn, qnB]
            t32 = [t3, t3B]

            def proj_fm_multi(specs):
                its = [(si_, tb) for si_ in range(len(specs)) for tb in range(4)]
                wts_ = {}

                def getw(si_):
                    if si_ not in wts_ and si_ < len(specs):
                        wts_[si_] = load_w_fm(specs[si_][0])
                    return wts_.get(si_)
                getw(0)
                stA = {}

                def A(k):
                    si_, tb = its[k]
                    wt, wk = getw(si_)
                    if tb == 0:
                        getw(si_ + 1)
                    pi = nxt('pj', 2)
                    mmg(PJ[pi][:, :], [(wt[:, kc, :], hT[:, kc, tb * 512:(tb + 1) * 512]) for kc in range(8)],
                        wk + hTk(tb), [PK[pi]])
                    stA[k] = pi

                def BCD(k):
                    si_, tb = its[k]
                    pieces, norm, gaincol, dests = specs[si_]
                    pi = stA.pop(k)
                    pj = PJ[pi]
                    q_ = qn2[k % 2]
                    qk = 'qn%d' % (k % 2)
                    t3_ = t32[k % 2]
                    t3k = 't3%d' % (k % 2)
                    anyrope = any(kd == 'rope' for kd, _, _ in dests)
                    if norm:
                        act(sq[:], pj[:, :], AF.Square, [PK[pi]], ['sq'])
                        sch.op('tensor', mm(MS[:, :], bd64[:], sq[:]), ['sq', 'bd64'], ['MS'])
                        act(lnt[:], MS[:, :], AF.Ln, ['MS'], ['lnt'], bias=invf[:, 2:3], scale=1.0 / 64)
                        act(rst[:], lnt[:], AF.Exp, ['lnt'], ['rst'], scale=-0.5)
                        stt('vector', q_[:], pj[:, :], gaincol, rst[:], ALU.mult, ALU.mult, [PK[pi], 'rst', 'hg'], [qk])
                    else:
                        cp('scalar', q_[:], pj[:, :], [PK[pi]], [qk])
                    if anyrope:
                        si2 = nxt('s', 2)
                        sch.op('tensor', mm(SB[si2][:, :], rotm[:], q_[:]), [qk, 'rotm'], [SK[si2]])
                        tt('gpsimd', t1[:], q_[:], cosT[:, tb * 512:(tb + 1) * 512], ALU.mult, [qk, 'cosT'], ['t1'])
                        tt('vector', t2[:], SB[si2][:, :], sinS[:, tb * 512:(tb + 1) * 512], ALU.mult, [SK[si2], 'sinS'], ['t2'])
                        tt('vector', t3_[:], t1[:], t2[:], ALU.add, ['t1', 't2'], [t3k])
                    for hf, (kind, dfn, kfn) in enumerate(dests):
                        src = t3_ if kind == 'rope' else q_
                        sk = t3k if kind == 'rope' else qk
                        pr = slice(0, 64) if hf == 0 else slice(64, 128)
                        if hf == 0:
                            cp('gpsimd', dfn(tb), src[pr, :], [sk], kfn(tb))
                        else:
                            cp('scalar', dfn(tb), src[pr, :], [sk], kfn(tb))
                A(0)
                for k in range(len(its)):
                    if k + 1 < len(its):
                        A(k + 1)
                    BCD(k)

            def tblk(tb):
                return slice(tb * 512, (tb + 1) * 512)

            def qkeys(h, tb):
                return [('Qa', h, 4 * tb + i) for i in range(4)]

            def kkeys(i, tb):
                return [('Kb', i, 4 * tb + j) for j in range(4)]

            pend = [None]

            def flush():
                if pend[0] is not None:
                    f = pend[0]
                    pend[0] = None
                    f()

            def attn_round(units, vfn, ofirst_key, norm_fn):
                oi = nxt('o', 2)
                O = OB[oi]
                st = {'first': True}
                nU = len(units)
                for ui, un in enumerate(units):
                    si = nxt('s', 2)
                    Sb = SB[si]
                    n, ncol = un['n'], un['ncol']
                    sch.op('tensor', mm(Sb[0:n, 0:ncol], un['lhsT'], un['rhs']), un['reads'], [SK[si]])
                    flush()
                    pi = nxt('p', 3)
                    P = Pt[pi]
                    pk = 'P%d' % pi
                    act(P[0:n, 0:ncol], Sb[0:n, 0:ncol], AF.Exp, [SK[si]], [pk], scale=0.125)
                    if un.get('mask') is not None:
                        mo, mi_, mk = un['mask'](P)
                        tt('vector', mo, mo, mi_, ALU.mult, [pk, mk], [pk])

                    def pv(un=un, P=P, pk=pk, n=n, last=(ui == nU - 1)):
                        fns = []
                        for (c0, oreg) in un['pv']:
                            fns.append(mm(oreg(O), P[0:n, c0:c0 + 128], un['v'], st['first'], True))
                            st['first'] = False
                        sch.group('tensor', fns, [pk] + un['vreads'], [OK_[oi]])
                        if last:
                            norm_fn(O, OK_[oi])
                    pend[0] = pv

            for pas in range(4):
                is_nsa = pas < 2
                g = pas % 2
                if is_nsa:
                    specs = []
                    for i in range(2):
                        c0 = O_QA + g * 256 + i * 128
                        specs.append(([(c0, 128)], True, hg[:, 0:1],
                                      [('rope', (lambda tb, h=2 * i: Qa[0:64, h, tblk(tb)]), (lambda tb, h=2 * i: qkeys(h, tb))),
                                       ('rope', (lambda tb, h=2 * i + 1: Qa[0:64, h, tblk(tb)]), (lambda tb, h=2 * i + 1: qkeys(h, tb)))]))
                    specs.append(([(O_KSL + g * 64, 64), (O_KWN + g * 64, 64)], True, hg[:, 1:2],
                                  [('rope', (lambda tb: Kb[0:64, 0, tblk(tb)]), (lambda tb: kkeys(0, tb))),
                                   ('rope', (lambda tb: Kb[0:64, 1, tblk(tb)]), (lambda tb: kkeys(1, tb)))]))
                    specs.append(([(O_KC + g * 64, 64), (O_VC + g * 64, 64)], False, None,
                                  [('rope', (lambda tb: Kb[0:64, 2, tblk(tb)]), (lambda tb: kkeys(2, tb))),
                                   ('copy', (lambda tb: Kb[0:64, 3, tblk(tb)]), (lambda tb: kkeys(3, tb)))]))
                    proj_fm_multi(specs)
                    sch.dma('gpsimd', Kb[64:96, 0, :], C['ind32'], writes=[('KbI', 0)])
                    wtk = []
                    for (c0, n, cc) in ((O_VSL + g * 64, 64, 0), (O_VWN + g * 64, 64, 64), (O_GN + g * 12, 12, 128)):
                        sch.dma('sync', wt_tm[:, :, cc:cc + n], win_s[:, c0:c0 + n].rearrange("(kc p) c -> p kc c", p=128),
                                reads=wkeys('win'), writes=[('wttm', cc)])
                        wtk.append(('wttm', cc))
                    for tti in range(16):
                        pi = nxt('pj', 2)
                        pj = PJ[pi]
                        mmg(pj[:, 0:140], [(hT[:, kc, tti * 128:(tti + 1) * 128], wt_tm[:, kc, 0:140]) for kc in range(8)],
                            wtk + hTk(tti // 4), [PK[pi]])
                        cp('vector', Vb[:, tti, 0:2, 0:64], pj[:, 0:128].rearrange("p (a d) -> p a d", a=2), [PK[pi]], [('Vb', tti)])
                        act(gates[:, tti, :], pj[:, 128:140], AF.Sigmoid, [PK[pi]], [('gates', tti)])
                    KC_ALL = [k for tb in range(4) for k in kkeys(2, tb)]
                    VC_ALL = [k for tb in range(4) for k in kkeys(3, tb)]
                    for (hi, w1t, cpo, nm, kall) in ((2, w1k, cposk, 'k', KC_ALL), (3, w1v, cposv, 'v', VC_ALL)):
                        mmg(MS[0:64, 0:127], [(w1t[:, l, :], Kb[0:64, hi, l:l + 16 * 126 + 1:16]) for l in range(32)],
                            kall + ['w1' + nm], ['MS'])
                        act(hid[:, 0:127], MS[0:64, 0:127], AF.Gelu_apprx_tanh, ['MS', 'cpos' + nm], ['hid'], bias=cpo[:, 0:1])
                        if nm == 'k':
                            si = nxt('s', 2)
                            sch.op('tensor', mm(SB[si][0:64, 0:127], w2k[:], hid[:, 0:127]), ['hid', 'w2k'], [SK[si]])
                            act(sq[0:64, 0:127], SB[si][0:64, 0:127], AF.Square, [SK[si]], ['sq'])
                            sch.op('tensor', mm(MS[0:64, 0:127], bd64[0:64, 0:64], sq[0:64, 0:127]), ['sq', 'bd64'], ['MS'])
                            act(lnt[0:64, 0:127], MS[0:64, 0:127], AF.Ln, ['MS'], ['lnt'], bias=invf[0:64, 2:3], scale=1.0 / 64)
                            act(rst[0:64, 0:127], lnt[0:64, 0:127], AF.Exp, ['lnt'], ['rst'], scale=-0.5)
                            stt('vector', kcn[:, 0:127], SB[si][0:64, 0:127], hg[0:64, 4:5], rst[0:64, 0:127], ALU.mult, ALU.mult,
                                [SK[si], 'rst', 'hg'], ['kcn'])
                        else:
                            si = nxt('s', 2)
                            sch.op('tensor', mm(SB[si][0:127, 0:64], hid[:, 0:127], w2v[:]), ['hid', 'w2v'], [SK[si]])
                            cp('vector', VCO[0:127, 0:64], SB[si][0:127, 0:64], [SK[si]], ['VCOv'])
                    if b == 0 and pas == 0:
                        dump('Qa', Qa[0:64, :, :], [64, 4, S], BF16, [k for h in range(4) for tb in range(4) for k in qkeys(h, tb)])
                        dump('Kb', Kb[0:64, :, :], [64, 4, S], BF16, [k for h in range(4) for tb in range(4) for k in kkeys(h, tb)])
                        dump('Vb', Vb[:], [128, 16, 4, 65], BF16, [('Vb', i) for i in range(16)] + ['Vb_ones'])
                        dump('gates', gates[:], [128, 16, 12], F32, [('gates', i) for i in range(16)])
                        dump('kcn', kcn[:], [64, 128], BF16, ['kcn'])
                        dump('VCO', VCO[:], [128, 97], BF16, ['VCOv', 'VCO'])
                        stg(2)

                    def nsa_norm(qt, br, first_branch, imp_out):
                        def fn(O, ok):
                            Ov = O[:, 0:388].rearrange("p (r c) -> p r c", r=4) if br == 0 else \
                                O[:, 0:260].rearrange("p (r c) -> p r c", r=4)
                            ts('vector', sm[:, 0:4], Ov[:, :, 64], 1e-30, None, ALU.max, None, [ok], ['sm0'])
                            sch.op('vector', lambda e: e.reciprocal(out=sm[:, 4:8], in_=sm[:, 0:4]), ['sm0'], ['sm1'])
                            tt('vector', sm[:, 8:12], sm[:, 4:8], gates[:, qt, br:12:3], ALU.mult, ['sm1', ('gates', qt)], ['sm2'])
                            fb = sm[:, 8:12].unsqueeze(2).to_broadcast([128, 4, 64])
                            if first_branch:
                                tt('vector', oacc[:, qt, :, :], Ov[:, :, 0:64], fb, ALU.mult, [ok, 'sm2'], [('oacc', qt)])
                            else:
                                for r in range(4):
                                    stt('vector', oacc[:, qt, r, :], Ov[:, r, 0:64], sm[:, 8 + r:9 + r], oacc[:, qt, r, :], ALU.mult, ALU.add,
                                        [ok, 'sm2', ('oacc', qt)], [('oacc', qt)])
                            if imp_out:
                                tt('vector', imp3[:], Ov[:, :, 65:97], sm[:, 4:8].unsqueeze(2).to_broadcast([128, 4, 32]), ALU.mult,
                                   [ok, 'sm1'], ['imp3'])
                        return fn

                    for qt in range(16):
                        ncv = min(127, 8 * qt + 7)
                        qs = slice(qt * 128, (qt + 1) * 128)
                        unit = dict(lhsT=kcn[:, 0:ncv], rhs=Qa[0:64, 0:4, qs], n=ncv, ncol=512,
                                    reads=['kcn'] + [('Qa', h, qt) for h in range(4)],
                                    mask=(lambda P, ncv=ncv, qt=qt: (P[0:ncv, :].rearrange("p (r t) -> p r t", r=4),
                                                                   cmpmask[0:ncv, qt, :].unsqueeze(1).to_broadcast([ncv, 4, 128]), 'cmpmask')),
                                    pv=[(r * 128, (lambda O, r=r: O[:, r * 97:(r + 1) * 97])) for r in range(4)],
                                    v=VCO[0:ncv, 0:97], vreads=['VCOv', 'VCO'])
                        def prenorm(O, ok, qt=qt, qs=qs):
                            nsa_norm(qt, 0, True, qt >= 8)(O, ok)
                            if qt >= 8:
                                sch.op('vector', lambda e: e.tensor_reduce(out=impr[:], in_=imp3[:].rearrange("p r j -> p j r"),
                                                                            axis=AX.X, op=ALU.add), ['imp3'], ['impr'])
                                tt('vector', impm[:], impr[:], cn_t[:, qt - 8, :], ALU.add, ['impr', 'cn'], ['impm'])
                                sch.op('vector', lambda e: e.max(out=m8[:, 0, :], in_=impm[:]), ['impm'], ['m8'])
                                sch.op('vector', lambda e: e.match_replace(out=impr[:], in_to_replace=m8[:, 0, :], in_values=impm[:],
                                                                            imm_value=-3e38), ['impm', 'm8'], ['impr'])
                                sch.op('vector', lambda e: e.max(out=m8[:, 1, :], in_=impr[:]), ['impr', 'm8'], ['m8'])
                                ts('vector', Tst[:, 64:96], impm[:], m8[:, 1, 7:8], NEGB, ALU.is_lt, ALU.mult, ['impm', 'm8'], ['Tst'])
                                trp(TR[0:96, 0:128], Tst[:], ident[:], ['Tst', 'ident'], ['TR'])
                                cp('scalar', Qa[64:96, 0:4, qs], TR[64:96, 0:128].unsqueeze(1).to_broadcast([32, 4, 128]),
                                   ['TR'], [('QaB', qt)])
                        attn_round([unit], None, None, prenorm)
                    for qt in range(16):
                        qs = slice(qt * 128, (qt + 1) * 128)
                        units = []
                        for kt in range(max(0, qt - 4), qt + 1):
                            ks = slice(kt * 128, (kt + 1) * 128)
                            mk = None
                            if kt == qt:
                                mk = (lambda P: (P[:, :].rearrange("p (r t) -> p r t", r=4), mlow[:].unsqueeze(1).to_broadcast([128, 4, 128]), 'mlow'))
                            elif kt == qt - 4:
                                mk = (lambda P: (P[:, :].rearrange("p (r t) -> p r t", r=4), mup[:].unsqueeze(1).to_broadcast([128, 4, 128]), 'mup'))
                            units.append(dict(lhsT=Kb[0:64, 1, ks], rhs=Qa[0:64, 0:4, qs], n=128, ncol=512,
                                              reads=[('Kb', 1, kt)] + [('Qa', h, qt) for h in range(4)], mask=mk,
                                              pv=[(r * 128, (lambda O, r=r: O[:, r * 65:(r + 1) * 65])) for r in range(4)],
                                              v=Vb[:, kt, 1, :], vreads=[('Vb', kt), 'Vb_ones']))
                        attn_round(units, None, None, nsa_norm(qt, 2, False, False))
                    for qt in range(16):
                        qs = slice(qt * 128, (qt + 1) * 128)
                        kd = 64 if qt < 8 else 96
                        units = []
                        for kt in range(0, qt + 1):
                            ks = slice(kt * 128, (kt + 1) * 128)
                            mk = None
                            if kt == qt:
                                mk = (lambda P: (P[:, :].rearrange("p (r t) -> p r t", r=4), mlow[:].unsqueeze(1).to_broadcast([128, 4, 128]), 'mlow'))
                            rd = [('Kb', 0, kt)] + [('Qa', h, qt) for h in range(4)]
                            if kd == 96:
                                rd += [('KbI', 0), ('QaB', qt)]
                            units.append(dict(lhsT=Kb[0:kd, 0, ks], rhs=Qa[0:kd, 0:4, qs], n=128, ncol=512, reads=rd, mask=mk,
                                              pv=[(r * 128, (lambda O, r=r: O[:, r * 65:(r + 1) * 65])) for r in range(4)],
                                              v=Vb[:, kt, 0, :], vreads=[('Vb', kt), 'Vb_ones']))
                        attn_round(units, None, None, nsa_norm(qt, 1, False, False))
                    chunk0 = 2 * g
                else:
                    hgp = g
                    specs = []
                    for i in range(2):
                        c0 = O_QB + (4 * hgp + 2 * i) * 64
                        specs.append(([(c0, 128)], True, hg[:, 2:3],
                                      [('rope', (lambda tb, h=2 * i: Qa[0:64, h, tblk(tb)]), (lambda tb, h=2 * i: qkeys(h, tb))),
                                       ('rope', (lambda tb, h=2 * i + 1: Qa[0:64, h, tblk(tb)]), (lambda tb, h=2 * i + 1: qkeys(h, tb)))]))
                    for i in range(2):
                        c0 = O_KB + (4 * hgp + 2 * i) * 64
                        specs.append(([(c0, 128)], True, hg[:, 3:4],
                                      [('rope', (lambda tb, h=2 * i: Kb[0:64, h, tblk(tb)]), (lambda tb, h=2 * i: kkeys(h, tb))),
                                       ('rope', (lambda tb, h=2 * i + 1: Kb[0:64, h, tblk(tb)]), (lambda tb, h=2 * i + 1: kkeys(h, tb)))]))
                    proj_fm_multi(specs)
                    for h in range(4):
                        sch.dma('gpsimd', Kb[64:72, h, :], C['ind8'], writes=[('KbI', h)])
                    c0 = O_VB + hgp * 256
                    sch.dma('sync', wt_tm[:, :, 0:256], win_s[:, c0:c0 + 256].rearrange("(kc p) c -> p kc c", p=128),
                            reads=wkeys('win'), writes=[('wttm', 0), ('wttm', 64), ('wttm', 128)])
                    for tti in range(16):
                        pi = nxt('pj', 2)
                        pj = PJ[pi]
                        mmg(pj[:, 0:256], [(hT[:, kc, tti * 128:(tti + 1) * 128], wt_tm[:, kc, 0:256]) for kc in range(8)],
                            [('wttm', 0), ('wttm', 64), ('wttm', 128)] + hTk(tti // 4), [PK[pi]])
                        cp('vector', Vb[:, tti, :, 0:64], pj[:, 0:256].rearrange("p (a d) -> p a d", a=4), [PK[pi]], [('Vb', tti)])
                    stg(3.65)
                    for h in range(4):
                        sch.op('vector', lambda e, h=h: e.tensor_reduce(out=ksf[:, h, :], in_=Kb[0:64, h, :].rearrange("p (j k) -> p j k", k=256),
                                                                        axis=AX.X, op=ALU.add),
                               [k for tb in range(4) for k in kkeys(h, tb)], [('ksf', h)])
                    cp('vector', ksb[:], ksf[:], [('ksf', h) for h in range(4)], ['ksb'])
                    for qt in range(8, 16):
                        qs = slice(qt * 128, (qt + 1) * 128)
                        cur = qt // 2
                        for h in range(4):
                            sch.op('tensor', mm(MS[:, h * 8:(h + 1) * 8], Qa[0:64, h, qs], ksb[:, h, :]), [('Qa', h, qt), 'ksb'], ['MS'])
                        tt('vector', gm[:], MS[:, 0:32].rearrange("p (h j) -> p h j", h=4),
                           cmo_t[:, qt - 8, :].unsqueeze(1).to_broadcast([128, 4, 8]), ALU.add, ['MS', 'cmo'], ['gm'])
                        for h in range(4):
                            sch.op('vector', lambda e, h=h: e.max(out=m8[:, h, :], in_=gm[:, h, :]), ['gm', 'm8'], ['m8'])
                        tt('vector', lt8[:], gm[:], m8[:, :, 2:3].to_broadcast([128, 4, 8]), ALU.is_lt, ['gm', 'm8'], ['lt8'])
                        ts('vector', T2[:, :, 64:72], lt8[:], NEGB, None, ALU.mult, None, ['lt8'], ['T2'])
                        mset('vector', T2[:, :, 64 + cur:65 + cur], 0.0, ['T2'])
                        sch.group('tensor', [(lambda e, h=h: e.transpose(TR[0:72, h * 128:(h + 1) * 128], T2[:, h, :], ident[:])) for h in range(4)],
                                  ['T2', 'ident'], ['TR'])
                        cp('scalar', Qa[64:72, 0:4, qs], TR[64:72, 0:512].rearrange("p (h t) -> p h t", h=4),
                           ['TR'], [('QaB', qt)])
                    if b == 0 and pas == 2:
                        dump('Qm', Qa[0:72, :, :], [72, 4, S], BF16, [k for h in range(4) for tb in range(4) for k in qkeys(h, tb)] + [('QaB', q) for q in range(8, 16)])
                        dump('Km', Kb[0:72, :, :], [72, 4, S], BF16, [k for h in range(4) for tb in range(4) for k in kkeys(h, tb)] + [('KbI', h) for h in range(4)])
                    stg(3.7)
                    for h in range(4):
                        for QB in range(4):
                            kd = 64 if QB < 2 else 72
                            units = []
                            for kt in range(0, 4 * QB + 4):
                                ql0 = max(0, kt - 4 * QB)
                                ncol = (4 - ql0) * 128
                                ks = slice(kt * 128, (kt + 1) * 128)
                                q0 = (4 * QB + ql0) * 128
                                mk = None
                                if kt >= 4 * QB:
                                    mk = (lambda P: (P[:, 0:128], mlow[:], 'mlow'))
                                rd = [('Kb', h, kt)] + [('Qa', h, 4 * QB + ql) for ql in range(ql0, 4)]
                                if kd == 72:
                                    rd += [('KbI', h)] + [('QaB', 4 * QB + ql) for ql in range(ql0, 4)]
                                units.append(dict(lhsT=Kb[0:kd, h, ks], rhs=Qa[0:kd, h, q0:(4 * QB + 4) * 128], n=128, ncol=ncol, reads=rd, mask=mk,
                                                  pv=[((ql - ql0) * 128, (lambda O, ql=ql: O[:, ql * 65:(ql + 1) * 65])) for ql in range(ql0, 4)],
                                                  v=Vb[:, kt, h, :], vreads=[('Vb', kt), 'Vb_ones']))

                            def mnorm(O, ok, h=h, QB=QB):
                                Ov = O[:, 0:260].rearrange("p (r c) -> p r c", r=4)
                                sch.op('vector', lambda e: e.reciprocal(out=sm[:, 4:8], in_=Ov[:, :, 64]), [ok], ['sm1'])
                                tt('vector', oacc[:, 4 * QB:4 * QB + 4, h, :], Ov[:, :, 0:64],
                                   sm[:, 4:8].unsqueeze(2).to_broadcast([128, 4, 64]), ALU.mult, [ok, 'sm1'],
                                   [('oacc', 4 * QB + i) for i in range(4)])
                            attn_round(units, None, None, mnorm)
                    chunk0 = 4 + 2 * g
                flush()
                if b == 0 and pas in (0, 2):
                    dump('oacc%d' % pas, oacc[:], [128, 16, 4, 64], F32, [('oacc', i) for i in range(16)])
                    if pas == 0:
                        stg(3)
                for q4 in range(4):
                    cp('vector', obf[:], oacc[:, 4 * q4:4 * q4 + 4, :, :].rearrange("p q h d -> p q (h d)"),
                       [('oacc', 4 * q4 + i) for i in range(4)], ['obf'])
                    OTC = int(os.environ.get('OT_CUT', '9'))
                    if OTC < 1:
                        continue
                    TRx, trk = TRB[q4 % 2]
                    sch.group('tensor', [(lambda e, ci=ci, ql=ql, TRx=TRx: e.transpose(TRx[:, (ci * 4 + ql) * 128:(ci * 4 + ql + 1) * 128],
                                                                              obf[:, ql, ci * 128:(ci + 1) * 128], ident[:]))
                                         for ci in range(2) for ql in range(4)], ['obf', 'ident'], [trk])
                    if OTC == 3:
                        mset('vector', oT[:, chunk0, q4 * 512:(q4 + 1) * 512], 1.0, [('oT', chunk0, q4)])
                        continue
                    if OTC == 4:
                        act(oT[:, chunk0, q4 * 512:q4 * 512 + 128], obf[:, 0, 0:128], AF.Identity, ['obf'], [('oT', chunk0, q4)])
                        continue
                    if OTC == 5:
                        act(t3[:, 0:128], TRx[:, 0:128], AF.Identity, [trk], ['t3'])
                        continue
                    for ci in range(2 if OTC >= 2 else 0):
                        for ql in range(4):
                            act(oT[:, chunk0 + ci, (q4 * 4 + ql) * 128:(q4 * 4 + ql + 1) * 128], TRx[:, (ci * 4 + ql) * 128:(ci * 4 + ql + 1) * 128],
                                AF.Identity, [trk], [('oT', chunk0 + ci, q4)])
                stg(3.2 + 0.2 * pas)
            sch.barrier()
            asx.close()
            dump('oT', oT[:], [128, 8, S], BF16, [('oT', c, q) for c in range(8) for q in range(4)])
            stg(4)

            with ExitStack() as xs:
                mixT = sbt([128, 8, S], BF16, xs, 'mixT')
                woutt = sbt([128, 8, 1024], BF16, xs, 'woutt')
                wb_t = [[sbt([128, 4, 128], BF16, xs, 'wbt') for _ in range(2)] for _ in range(2)]
                wg_t = [[sbt([128, 8, 128], BF16, xs, 'wgt') for _ in range(2)] for _ in range(2)]
                sga = sbt([128, 512], F32, xs, 'sga')
                sgb = sbt([128, 512], F32, xs, 'sgb')
                m1 = sbt([128, 512], F32, xs, 'm1')
                m2 = sbt([128, 512], F32, xs, 'm2')
                xts = [sbt([128, D], F32, xs, 'xt') for _ in range(2)]
                x1t = [sbt([128, D], F32, xs, 'x1t') for _ in range(2)]
                tmpx = sbt([128, 512], F32, xs, 'tmpx')
                tmpA = (sbt([128, D], BF16, xs, 'junk'), sbt([128, 1], F32, xs, 'ss'), sbt([128, 1], F32, xs, 'lnv'),
                        sbt([128, 1], F32, xs, 'rstd'), sbt([128, D], BF16, xs, 'xn'))
                sch.dma('sync', woutt[:], wout_s[:, :].rearrange("(kc p) c -> p kc c", p=128), reads=wkeys('wout'), writes=['woutt'])
                for fc in range(8):
                    wi = fc % 2
                    fs = slice(fc * 128, (fc + 1) * 128)
                    sch.dma('sync', wb_t[0][wi][:], wbn_s[:, fs].rearrange("(kc p) c -> p kc c", p=128), reads=wkeys('wbn', 4), writes=[('wbt', 0, wi)])
                    sch.dma('sync', wb_t[1][wi][:], wbm_s[:, fs].rearrange("(kc p) c -> p kc c", p=128), reads=wkeys('wbm', 4), writes=[('wbt', 1, wi)])
                    sch.dma('sync', wg_t[0][wi][:], win_s[:, O_GA + fc * 128:O_GA + (fc + 1) * 128].rearrange("(kc p) c -> p kc c", p=128),
                            reads=wkeys('win'), writes=[('wgt', 0, wi)])
                    sch.dma('sync', wg_t[1][wi][:], win_s[:, O_GB + fc * 128:O_GB + (fc + 1) * 128].rearrange("(kc p) c -> p kc c", p=128),
                            reads=wkeys('win'), writes=[('wgt', 1, wi)])
                    for tb in range(4):
                        tsl = tblk(tb)
                        mmg(PJ[0][:, :], [(wb_t[0][wi][:, kc, :], oT[:, kc, tsl]) for kc in range(4)],
                            [('wbt', 0, wi)] + [('oT', kc, tb) for kc in range(4)], ['PJ0'])
                        mmg(PJ[1][:, :], [(wg_t[0][wi][:, kc, :], hT[:, kc, tsl]) for kc in range(8)],
                            [('wgt', 0, wi)] + hTk(tb), ['PJ1'])
                        act(sga[:], PJ[1][:, :], AF.Sigmoid, ['PJ1'], ['sga'])
                        tt('vector', m1[:], PJ[0][:, :], sga[:], ALU.mult, ['PJ0', 'sga'], ['m1'])
                        mmg(SB[0][:, :], [(wb_t[1][wi][:, kc, :], oT[:, 4 + kc, tsl]) for kc in range(4)],
                            [('wbt', 1, wi)] + [('oT', 4 + kc, tb) for kc in range(4)], ['S0'])
                        mmg(SB[1][:, :], [(wg_t[1][wi][:, kc, :], hT[:, kc, tsl]) for kc in range(8)],
                            [('wgt', 1, wi)] + hTk(tb), ['S1'])
                        act(sgb[:], SB[1][:, :], AF.Sigmoid, ['S1'], ['sgb'])
                        tt('vector', m2[:], SB[0][:, :], sgb[:], ALU.mult, ['S0', 'sgb'], ['m2'])
                        tt('gpsimd', mixT[:, fc, tsl], m1[:], m2[:], ALU.add, ['m1', 'm2'], [('mixT', fc, tb)])
                for tti in range(16):
                    xi = tti % 2
                    tsl = slice(tti * 128, (tti + 1) * 128)
                    sch.dma('sync', xts[xi][:], x_d[b, tsl, :], writes=['xt%d' % xi])
                    for half in range(2):
                        hs = slice(half * 512, (half + 1) * 512)
                        mmg(PJ[half][:, :], [(mixT[:, kc, tsl], woutt[:, kc, hs]) for kc in range(8)],
                            ['woutt'] + [('mixT', kc, tti // 4) for kc in range(8)], [PK[half]])
                        tt('vector', tmpx[:], PJ[half][:, :], gtbc[:, hs], ALU.mult, [PK[half], ('gtbc', half)], ['tmpx'])
                        tt('gpsimd', x1t[xi][:, hs], tmpx[:], xts[xi][:, hs], ALU.add, ['tmpx', 'xt%d' % xi], [('x1t', xi, half)])
                    sch.dma('sync', x1_s[tsl, :], x1t[xi][:], reads=[('x1t', xi, 0), ('x1t', xi, 1)], writes=[('x1s', tti)])
                    norm_transpose(x1t[xi][:], [('x1t', xi, 0), ('x1t', xi, 1)], a2, b2, b, tti, tmpA)
                sch.barrier()
            ms.close()
            dump('h2T', hT[:], [128, 8, S], BF16, [k for tb in range(4) for k in hTk(tb)])
            stg(5)

            with ExitStack() as fs_:
                yT = sbt([128, 22, 1024], BF16, fs_, 'yT')
                wu = [sbt([128, 8, 256], BF16, fs_, 'wu') for _ in range(2)]
                wd = [sbt([128, 22, 512], BF16, fs_, 'wd') for _ in range(2)]
                aS = [sbt([128, 514], F32, fs_, 'aS') for _ in range(2)]
                halo = sbt([128, 22, 2], F32, fs_, 'halo')
                c1 = sbt([128, 512], F32, fs_, 'c1')
                c2 = sbt([128, 512], F32, fs_, 'c2')
                c3 = sbt([128, 512], F32, fs_, 'c3')
                gl = sbt([128, 512], F32, fs_, 'gl')
                x1q = [sbt([128, 512], F32, fs_, 'x1q') for _ in range(2)]
                oq = [sbt([128, 512], F32, fs_, 'oq') for _ in range(2)]
                tmpo = sbt([128, 512], F32, fs_, 'tmpo')
                mset('vector', halo[:], 0.0, ['halo'])
                blk = 0
                for hf in range(2):
                    for fc in range(22):
                        wi = fc % 2
                        sch.dma('sync', wu[wi][:, :, 0:128], wup_s[:, fc * 128:(fc + 1) * 128].rearrange("(kc p) c -> p kc c", p=128),
                                reads=wkeys('wup'), writes=[('wu', wi, 0)])
                        sch.dma('sync', wu[wi][:, :, 128:256], wup_s[:, DFF + fc * 128:DFF + (fc + 1) * 128].rearrange("(kc p) c -> p kc c", p=128),
                                reads=wkeys('wup'), writes=[('wu', wi, 1)])
                        for tb2 in range(2):
                            tok0 = hf * 1024 + tb2 * 512
                            tbg = tok0 // 512
                            ai = blk % 2
                            blk += 1
                            a_ = aS[ai]
                            ak = 'aS%d' % ai
                            Ab, Ak = [(PJ[0], 'PJ0'), (SB[0], 'S0')][blk % 2]
                            Vb_, Vk = [(PJ[1], 'PJ1'), (SB[1], 'S1'), (OB[0], 'O0'), (OB[1], 'O1')][blk % 4]
                            mmg(Ab[:, :], [(wu[wi][:, kc, 0:128], hT[:, kc, tok0:tok0 + 512]) for kc in range(8)],
                                [('wu', wi, 0)] + hTk(tbg), [Ak])
                            mmg(Vb_[:, :], [(wu[wi][:, kc, 128:256], hT[:, kc, tok0:tok0 + 512]) for kc in range(8)],
                                [('wu', wi, 1)] + hTk(tbg), [Vk])
                            cp('gpsimd', a_[:, 0:2], halo[:, fc, :], ['halo'], [ak])
                            cp('scalar', a_[:, 2:514], Ab[:, :], [Ak], [ak])
                            act(c1[:], a_[:, 0:512], AF.Identity, [ak, 'cw', 'cb'], ['c1'], bias=cb[:, fc:fc + 1], scale=cw[:, fc, 0:1])
                            stt('vector', c2[:], a_[:, 1:513], cw[:, fc, 1:2], c1[:], ALU.mult, ALU.add, [ak, 'c1', 'cw'], ['c2'])
                            stt('vector', c3[:], a_[:, 2:514], cw[:, fc, 2:3], c2[:], ALU.mult, ALU.add, [ak, 'c2', 'cw'], ['c3'])
                            cp('gpsimd', halo[:, fc, :], a_[:, 512:514], [ak], ['halo'])
                            act(gl[:], c3[:], AF.Gelu_apprx_tanh, ['c3'], ['gl'])
                            tt('vector', yT[:, fc, tb2 * 512:(tb2 + 1) * 512], gl[:], Vb_[:, :], ALU.mult, ['gl', Vk], [('yT', fc, tb2)])
                    for nq in range(2):
                        wi = nq % 2
                        ns = slice(nq * 512, (nq + 1) * 512)
                        for (r0, r1) in ((0, 8), (8, 16), (16, 22)):
                            sch.dma('sync', wd[wi][:, r0:r1, :], wdn_s[r0 * 128:r1 * 128, ns].rearrange("(kc p) c -> p kc c", p=128),
                                    reads=wkeys('wdn', 22), writes=[('wd', wi, r0)])
                        for t8 in range(8):
                            tti = hf * 8 + t8
                            tsl = slice(tti * 128, (tti + 1) * 128)
                            pi = nxt('pj', 2)
                            oi = t8 % 2
                            mmg(PJ[pi][:, :], [(yT[:, fc, t8 * 128:(t8 + 1) * 128], wd[wi][:, fc, :]) for fc in range(22)],
                                [('wd', wi, 0), ('wd', wi, 8), ('wd', wi, 16)] + [('yT', fc, t8 // 4) for fc in range(22)], [PK[pi]])
                            sch.dma('sync', x1q[oi][:], x1_s[tsl, ns], reads=[('x1s', tti)], writes=['x1q%d' % oi])
                            tt('vector', tmpo[:], PJ[pi][:, :], gtbc[:, 1024 + nq * 512:1024 + (nq + 1) * 512], ALU.mult,
                               [PK[pi], ('gtbc', 2), ('gtbc', 3)], ['tmpo'])
                            tt('gpsimd', oq[oi][:], tmpo[:], x1q[oi][:], ALU.add, ['tmpo', 'x1q%d' % oi], ['oq%d' % oi])
                            sch.dma('sync', out_d[b, tsl, ns], oq[oi][:], reads=['oq%d' % oi], writes=[('out', b, tti, nq)])
                sch.barrier()
            bs.close()
      except StopBuild:
        sch.barrier()
        asx.close(); ms.close(); bs.close()
        break

    sch.finish()
    with nc.Block() as block:
        @block.sync
        def _(e):
            for f in sch.streams['sync']:
                f(e)

        @block.scalar
        def _(e):
            for f in sch.streams['scalar']:
                f(e)

        @block.vector
        def _(e):
            for f in sch.streams['vector']:
                f(e)

        @block.gpsimd
        def _(e):
            for f in sch.streams['gpsimd']:
                f(e)

        @block.tensor
        def _(e):
            for f in sch.streams['tensor']:
                f(e)
    es.close()
    return nc, dbg_outs, sch


def make_in_maps(inputs, nb=4, ncores=NCORE, batches=None):
    f = lambda a: np.ascontiguousarray(np.asarray(a, dtype=np.float32))
    x = np.asarray(inputs['x'])
    c = np.asarray(inputs['c'], dtype=np.float32)
    pos = np.asarray(inputs['positions']).astype(np.int32)
    col = lambda v: np.ascontiguousarray(np.asarray(v, np.float32).reshape(-1, 128).T)
    tile2 = lambda v: np.concatenate([np.asarray(v, np.float32)] * 2)
    hgm = np.zeros((128, 8), np.float32)
    hgm[:, 0] = tile2(inputs['g_q_nsa'][0])
    hgm[:, 1] = np.concatenate([inputs['g_k_slc'][0], inputs['g_k_win'][0]])
    hgm[:, 2] = tile2(inputs['g_q_moba'][0])
    hgm[:, 3] = tile2(inputs['g_k_moba'][0])
    hgm[:, 4] = tile2(inputs['g_k_cmp'][0])
    shared = {
        'bada': f(inputs['b_ada'][0][None, :]),
        'badaT': col(inputs['b_ada'][0]),
        'gcol': np.concatenate([col(inputs['g_attn_norm'][0]), col(inputs['g_ffn_norm'][0])], 1),
        'hg': hgm,
        'cw': np.ascontiguousarray(np.asarray(inputs['conv_w'][0], np.float32).T.reshape(22, 128, 3).transpose(1, 0, 2)),
        'cb': col(inputs['conv_b'][0]),
        'wposk': f(np.asarray(inputs['cmp_k_pos'][0]).T),
        'wposv': f(np.asarray(inputs['cmp_v_pos'][0]).T),
        'w_ada': f(inputs['w_ada'][0]), 'w_in': f(inputs['w_in'][0]),
        'w1k': f(inputs['cmp_k_w1'][0]), 'w2k': f(inputs['cmp_k_w2'][0]),
        'w1v': f(inputs['cmp_v_w1'][0]), 'w2v': f(inputs['cmp_v_w2'][0]),
        'wbn': f(inputs['w_branch_nsa'][0]), 'wbm': f(inputs['w_branch_moba'][0]),
        'wout': f(inputs['w_out'][0]), 'wup': f(inputs['w_ffn_up'][0]), 'wdn': f(inputs['w_ffn_down'][0]),
    }
    for k, v in _consts().items():
        shared['c_' + k] = v
    maps = []
    for ci in range(ncores):
        bl = batches[ci] if batches is not None else list(range(ci * nb, (ci + 1) * nb))
        m = dict(shared)
        m['x'] = f(x[bl])
        m['pos'] = np.ascontiguousarray(pos[bl])
        cc = np.zeros((4, 1024), np.float32)
        cc[:len(bl)] = c[bl]
        m['cT'] = np.ascontiguousarray(cc.T.reshape(8, 128, 4).transpose(1, 0, 2))
        maps.append(m)
    return maps


_CACHE = {}


def kernel(**inputs):
    nb = 4
    if 'nc' not in _CACHE:
        _CACHE['nc'] = build(nb)[0]
    nc = _CACHE['nc']
    maps = make_in_maps(inputs, nb)
    res = run_bass_kernel_spmd(nc, maps, core_ids=list(range(NCORE)))
    out = np.concatenate([np.asarray(r['out']) for r in res.results], axis=0)
    return out.astype(np.float32)
```

```python
import os
import numpy as np
from contextlib import ExitStack
import concourse.bass as bass
import concourse.mybir as mybir
from concourse.bass_utils import run_bass_kernel_spmd

F32 = mybir.dt.float32
BF16 = mybir.dt.bfloat16
I32 = mybir.dt.int32
AF = mybir.ActivationFunctionType
ALU = mybir.AluOpType
AX = mybir.AxisListType

S = 2048
D = 1024
DFF = 2816
INW = 4888
NCORE = 8
EPS = 1e-6
NEGB = -240000.0
ENG = ['sync', 'scalar', 'vector', 'gpsimd', 'tensor']
NSLOT = 8

O_QA, O_KC, O_VC, O_KSL, O_VSL, O_KWN, O_VWN, O_GN, O_QB, O_KB, O_VB, O_GA, O_GB = (
    0, 512, 640, 768, 896, 1024, 1152, 1280, 1304, 1816, 2328, 2840, 3864)


class Sched:
    def __init__(self, nc, es):
        self.nc = nc
        self.streams = {e: [] for e in ENG}
        self.sem = {}
        for e in ENG:
            self.sem[e] = es.enter_context(nc.semaphore('p_' + e))
        self.cnt = {e: 0 for e in ENG}
        self.dq = {'sync': 0, 'gpsimd': 0, 'bg': 0}
        for q in self.dq:
            for i in range(NSLOT):
                self.sem[(q, i)] = es.enter_context(nc.semaphore('d_%s%d' % (q, i)))
        self.seen = {e: {} for e in ENG}
        self.lw = {}
        self.rd = {}

    def _deps(self, reads, writes):
        deps = {}

        def add(tok):
            if tok is not None:
                deps[tok[0]] = max(deps.get(tok[0], 0), tok[1])
        for k in reads:
            add(self.lw.get(k))
        for k in writes:
            add(self.lw.get(k))
            for sk, v in self.rd.get(k, {}).items():
                add((sk, v))
        return deps

    def _emit_waits(self, eng, deps):
        for sk, v in deps.items():
            if eng == 'tensor' and sk == 'tensor':
                continue
            if self.seen[eng].get(sk, 0) < v:
                self.seen[eng][sk] = v
                h = self.sem[sk]
                self.streams[eng].append(lambda e, h=h, v=v: e.wait_ge(h, v))

    def _commit(self, tok, reads, writes):
        for k in reads:
            d = self.rd.setdefault(k, {})
            d[tok[0]] = max(d.get(tok[0], 0), tok[1])
        for k in writes:
            self.lw[k] = tok
            self.rd[k] = {}

    def group(self, eng, fns, reads=(), writes=()):
        deps = self._deps(reads, writes)
        self._emit_waits(eng, deps)
        self.cnt[eng] += 1
        h = self.sem[eng]
        for f in fns[:-1]:
            self.streams[eng].append(lambda e, f=f: f(e))
        f = fns[-1]
        self.streams[eng].append(lambda e, f=f, h=h: f(e).then_inc(h, 1))
        tok = (eng, self.cnt[eng])
        self._commit(tok, reads, writes)
        return tok

    def op(self, eng, fn, reads=(), writes=()):
        return self.group(eng, [fn], reads, writes)

    def dma(self, q, out, in_, reads=(), writes=()):
        n = self.dq[q]
        self.dq[q] += 1
        slot = n % NSLOT
        val = 16 * (n // NSLOT + 1)
        sk = (q, slot)
        deps = self._deps(reads, writes)
        if val > 16:
            deps[sk] = max(deps.get(sk, 0), val - 16)
        qe = 'gpsimd' if q == 'bg' else q
        self._emit_waits(qe, deps)
        h = self.sem[sk]
        self.streams[qe].append(lambda e, out=out, in_=in_, h=h: e.dma_start(out=out, in_=in_).then_inc(h, 16))
        tok = (sk, val)
        self._commit(tok, reads, writes)
        return tok

    def _all_tokens(self, bg=False):
        d = {e: self.cnt[e] for e in ENG if self.cnt[e] > 0}
        for q, n in self.dq.items():
            if q == 'bg' and not bg:
                continue
            for slot in range(NSLOT):
                if n > slot:
                    d[(q, slot)] = 16 * ((n - 1 - slot) // NSLOT + 1)
        return d

    def barrier(self):
        allt = self._all_tokens()
        for e in ENG:
            deps = {k: v for k, v in allt.items() if k != e}
            self._emit_waits(e, deps)

    def finish(self):
        allt = self._all_tokens(bg=True)
        self._emit_waits('sync', {k: v for k, v in allt.items() if k != 'sync'})


def _consts():
    c = {}
    c['ident'] = np.eye(128, dtype=np.float32)
    bd = np.zeros((128, 128), np.float32)
    bd[:64, :64] = 1
    bd[64:, 64:] = 1
    c['bd64'] = bd
    rot = np.zeros((128, 128), np.float32)
    for m in range(128):
        partner = m + 32 if (m % 64) < 32 else m - 32
        rot[partner, m] = 1
    c['rotm'] = rot
    p = np.arange(128)
    invf = (10000.0 ** (-(p % 32).astype(np.float64) / 32.0)) / (2 * np.pi)
    sgn = np.where((p % 64) < 32, -1.0, 1.0) * 6.28318
    c['invf'] = np.stack([invf, sgn, np.full(128, EPS), np.full(128, 6.28318)], 1).astype(np.float32)
    k = np.arange(128)[:, None]
    t = np.arange(128)[None, :]
    c['mlow'] = (k <= t).astype(np.float32)
    c['mup'] = (k > t).astype(np.float32)
    cm = np.zeros((128, 16, 128), np.float32)
    cc = np.arange(128)[:, None, None]
    tt = (np.arange(16)[None, :, None] * 128 + np.arange(128)[None, None, :])
    cm[:] = (16 * cc + 31 <= tt)
    cm[127] = 0
    c['cmpmask'] = cm
    ovl = np.zeros((128, 33), np.float32)
    ovl[:, 0] = 1
    cs = np.arange(127)[:, None] * 16
    js = np.arange(32)[None, :]
    ovl[:127, 1:] = ((cs < (js + 1) * 64) & (cs + 32 > js * 64)).astype(np.float32)
    c['ovl'] = ovl
    cn = np.zeros((128, 8, 32), np.float32)
    for qi in range(8):
        tq = (qi + 8) * 128 + np.arange(128)
        own = tq // 64
        jb = np.arange(32)[None, :]
        a = np.zeros((128, 32), np.float32)
        a[jb == 0 + 0 * own[:, None]] = 1e4
        a = np.where(jb == own[:, None], 2e4, a)
        a = np.where(jb == own[:, None] - 1, 3e4, a)
        a = np.where(jb > own[:, None], -1e30, a)
        cn[:, qi, :] = a
    c['cn'] = cn
    cmo = np.zeros((128, 8, 8), np.float32)
    for qi in range(8):
        cur = (qi + 8) // 2
        cmo[:, qi, cur:] = -1e30
    c['cmo'] = cmo
    kk = np.arange(2048)[None, :]
    c['ind32'] = (kk // 64 == np.arange(32)[:, None]).astype(np.float32)
    c['ind8'] = (kk // 256 == np.arange(8)[:, None]).astype(np.float32)
    oh = np.zeros((4, 4, 128), np.float32)
    for j in range(4):
        oh[j, j, :] = 1
    c['oh4'] = oh
    return c


CONST_SHAPES = {k: v.shape for k, v in _consts().items()}


class StopBuild(Exception):
    pass


def build(nb=4, dbg=None, stage=99):
    dbg = dbg or set()

    def stg(n):
        if stage <= n:
            raise StopBuild()
    nc = bass.Bass("TRN2", target_bir_lowering=False)
    es = ExitStack()
    uid = [0]

    def din(name, shape, dt=F32):
        return nc.dram_tensor(name, list(shape), dt, kind="ExternalInput").ap()

    def dscr(name, shape, dt=BF16):
        return nc.dram_tensor(name, list(shape), dt, kind="Internal").ap()

    def sbt(shape, dt, scope=None, name='t'):
        uid[0] += 1
        return (scope or es).enter_context(nc.sbuf_tensor("%s_%d" % (name, uid[0]), list(shape), dt))

    x_d = din("x", [nb, S, D])
    pos_d = din("pos", [nb, S], I32)
    cT_d = din("cT", [128, 8, 4])
    bada_d = din("bada", [1, 6144])
    badaT_d = din("badaT", [128, 48])
    gcol_d = din("gcol", [128, 16])
    hg_d = din("hg", [128, 8])
    cw_d = din("cw", [128, 22, 3])
    cb_d = din("cb", [128, 22])
    wposk_d = din("wposk", [64, 32])
    wposv_d = din("wposv", [64, 32])
    wada_d = din("w_ada", [1024, 6144])
    win_d = din("w_in", [1024, INW])
    w1k_d = din("w1k", [2048, 64])
    w2k_d = din("w2k", [64, 64])
    w1v_d = din("w1v", [2048, 64])
    w2v_d = din("w2v", [64, 64])
    wbn_d = din("wbn", [512, 1024])
    wbm_d = din("wbm", [512, 1024])
    wout_d = din("wout", [1024, 1024])
    wup_d = din("wup", [1024, 2 * DFF])
    wdn_d = din("wdn", [DFF, 1024])
    C = {k: din("c_" + k, shp) for k, shp in CONST_SHAPES.items()}
    out_d = nc.dram_tensor("out", [nb, S, D], F32, kind="ExternalOutput").ap()
    dbg_outs = {}

    wada_s = dscr("wada_s", [1024, 6144])
    win_s = dscr("win_s", [1024, INW])
    wbn_s = dscr("wbn_s", [512, 1024])
    wbm_s = dscr("wbm_s", [512, 1024])
    wout_s = dscr("wout_s", [1024, 1024])
    wup_s = dscr("wup_s", [1024, 2 * DFF])
    wdn_s = dscr("wdn_s", [DFF, 1024])
    x1_s = dscr("x1_s", [S, D], F32)
    gt_scr = dscr("gt_scr", [4, 2048], F32)

    sch = Sched(nc, es)
    ps = [es.enter_context(nc.psum_tensor("ps%d" % i, [128, 512], F32)) for i in range(8)]
    SB = [ps[0], ps[1]]
    OB = [ps[2], ps[3]]
    PJ = [ps[4], ps[5]]
    TRf = ps[6]
    MS = ps[7]
    TR = TRf[:].bitcast(BF16)
    TRB = [(TR, 'TR'), (MS[:].bitcast(BF16), 'MS')]
    SK = ['S0', 'S1']
    OK_ = ['O0', 'O1']
    PK = ['PJ0', 'PJ1']

    def act(out, in_, func, reads, writes, bias=None, scale=None, accum=None):
        kw = {}
        if bias is not None:
            kw['bias'] = bias
        if scale is not None:
            kw['scale'] = scale
        if accum is not None:
            kw['accum_out'] = accum
        return sch.op('scalar', lambda e: e.activation(out=out, in_=in_, func=func, **kw), reads, writes)

    def tt(eng, out, in0, in1, op, reads, writes):
        return sch.op(eng, lambda e: e.tensor_tensor(out=out, in0=in0, in1=in1, op=op), reads, writes)

    def ts(eng, out, in0, s1, s2, op0, op1, reads, writes):
        if s2 is None:
            return sch.op(eng, lambda e: e.tensor_scalar(out=out, in0=in0, scalar1=s1, scalar2=None, op0=op0), reads, writes)
        return sch.op(eng, lambda e: e.tensor_scalar(out=out, in0=in0, scalar1=s1, scalar2=s2, op0=op0, op1=op1), reads, writes)

    def stt(eng, out, in0, scalar, in1, op0, op1, reads, writes):
        return sch.op(eng, lambda e: e.scalar_tensor_tensor(out=out, in0=in0, scalar=scalar, in1=in1, op0=op0, op1=op1), reads, writes)

    def cp(eng, out, in_, reads, writes):
        if eng == 'scalar':
            return act(out, in_, AF.Copy, reads, writes)
        return sch.op(eng, lambda e: e.tensor_copy(out=out, in_=in_), reads, writes)

    def mset(eng, ap, val, writes):
        return sch.op(eng, lambda e: e.memset(ap, val), (), writes)

    def mm(out, lhsT, rhs, start=True, stop=True):
        return lambda e: e.matmul(out, lhsT, rhs, start=start, stop=stop)

    def mmg(out, pairs, reads, writes):
        n = len(pairs)
        return sch.group('tensor', [mm(out, l, r, i == 0, i == n - 1) for i, (l, r) in enumerate(pairs)], reads, writes)

    def trp(out, in_, ident_ap, reads, writes):
        return sch.op('tensor', lambda e: e.transpose(out, in_, ident_ap), reads, writes)

    def dump(name, ap, shape, dt, reads):
        if name not in dbg:
            return
        t = nc.dram_tensor("dbg_" + name, list(shape), dt, kind="ExternalOutput").ap()
        dbg_outs[name] = t
        sch.dma('sync', t, ap, reads=reads)

    def wkeys(name, n=8):
        return [(name, i) for i in range(n)]

    def conv(dst, src, rows, name):
        for r in range(0, rows, 128):
            sch.dma('bg', dst[r:r + 128, :], src[r:r + 128, :], writes=[(name, r // 128)])

    conv(wada_s, wada_d, 1024, 'wada')
    cst = {}

    def cload(name, src, shape, dt, q=None):
        t = sbt(shape, dt, name=name)
        q = q or ('gpsimd' if dt == BF16 else 'sync')
        sch.dma(q, t[:], src, writes=[name])
        cst[name] = t
        return t

    ident = cload('ident', C['ident'], [128, 128], BF16)
    bd64 = cload('bd64', C['bd64'], [128, 128], BF16)
    rotm = cload('rotm', C['rotm'], [128, 128], BF16)
    invf = cload('invf', C['invf'], [128, 4], F32)
    mlow = cload('mlow', C['mlow'], [128, 128], BF16)
    mup = cload('mup', C['mup'], [128, 128], BF16)
    cmpmask = cload('cmpmask', C['cmpmask'], [128, 16, 128], BF16)
    cn_t = cload('cn', C['cn'], [128, 8, 32], F32)
    cmo_t = cload('cmo', C['cmo'], [128, 8, 8], F32)
    oh4 = cload('oh4', C['oh4'], [4, 4, 128], F32)
    cTb = cload('cTb', cT_d, [128, 8, 4], BF16)
    badaT = cload('badaT', badaT_d, [128, 48], F32)
    gcol = cload('gcol', gcol_d, [128, 16], F32)
    hg = cload('hg', hg_d, [128, 8], F32)
    cw = cload('cw', cw_d, [128, 22, 3], F32)
    cb = cload('cb', cb_d, [128, 22], F32)
    w1k = cload('w1k', w1k_d.rearrange("(l d) j -> d l j", d=64), [64, 32, 64], BF16)
    w1v = cload('w1v', w1v_d.rearrange("(l d) j -> d l j", d=64), [64, 32, 64], BF16)
    w2k = cload('w2k', w2k_d, [64, 64], BF16)
    w2v = cload('w2v', w2v_d, [64, 64], BF16)
    wposk = cload('wposk', wposk_d, [64, 32], BF16)
    wposv = cload('wposv', wposv_d, [64, 32], BF16)
    VCO = sbt([128, 97], BF16, name='VCO')
    sch.dma('gpsimd', VCO[:, 64:97], C['ovl'], writes=['VCO'])
    bada4 = sbt([4, 2048], F32, name='bada4')
    sch.dma('sync', bada4[:, 0:1024], bada_d[0:1, 2048:3072].to_broadcast([4, 1024]), writes=['bada4a'])
    sch.dma('sync', bada4[:, 1024:2048], bada_d[0:1, 5120:6144].to_broadcast([4, 1024]), writes=['bada4b'])

    conv(win_s, win_d, 1024, 'win')
    conv(wbn_s, wbn_d, 512, 'wbn')
    conv(wbm_s, wbm_d, 512, 'wbm')
    conv(wout_s, wout_d, 1024, 'wout')
    conv(wup_s, wup_d, 1024, 'wup')
    conv(wdn_s, wdn_d, DFF, 'wdn')

    modT = sbt([128, 32, 4], F32, name='modT')
    rows_gt = sbt([4, 2048], F32, name='rowsgt')
    a1 = sbt([128, 8, 4], F32, name='a1')
    a2 = sbt([128, 8, 4], F32, name='a2')
    cposk = sbt([64, 1], F32, name='cposk')
    cposv = sbt([64, 1], F32, name='cposv')
    with ExitStack() as sc:
        wts = [sbt([128, 8, 512], BF16, sc, 'wadat') for _ in range(2)]
        tmp84 = sbt([128, 8, 4], F32, sc, 'tmp84')
        for ct in range(12):
            wt = wts[ct % 2]
            wk = 'wadat%d' % (ct % 2)
            sch.dma('sync', wt[:], wada_s[:, ct * 512:(ct + 1) * 512].rearrange("(kc p) c -> p kc c", p=128),
                    reads=wkeys('wada'), writes=[wk])
            sec = ct // 2
            if sec in (2, 5):
                pj = PJ[ct % 2]
                mmg(pj[0:4, :], [(cTb[:, kc, :], wt[:, kc, :]) for kc in range(8)], [wk, 'cTb'], [PK[ct % 2]])
                gi = (0 if sec == 2 else 1) * 1024 + (ct % 2) * 512
                tt('vector', rows_gt[:, gi:gi + 512], pj[0:4, :], bada4[:, gi:gi + 512], ALU.add,
                   [PK[ct % 2], 'bada4a', 'bada4b'], [('rowsgt', gi // 512)])
            else:
                mi = {0: 0, 1: 8, 3: 16, 4: 24}[sec] + (ct % 2) * 4
                for j in range(4):
                    mmg(MS[:, j * 4:(j + 1) * 4], [(wt[:, kc, j * 128:(j + 1) * 128], cTb[:, kc, :]) for kc in range(8)],
                        [wk, 'cTb'], ['MS'])
                tt('vector', modT[:, mi:mi + 4, :], MS[:, 0:16].rearrange("p (j b) -> p j b", j=4),
                   badaT[:, ct * 4:ct * 4 + 4].unsqueeze(2).to_broadcast([128, 4, 4]), ALU.add,
                   ['MS', 'badaT'], [('modT', mi // 4)])
        ts('vector', tmp84[:], modT[:, 8:16, :], 1.0, None, ALU.add, None, [('modT', 2), ('modT', 3)], ['tmp84'])
        tt('vector', a1[:], tmp84[:], gcol[:, 0:8].unsqueeze(2).to_broadcast([128, 8, 4]), ALU.mult, ['tmp84', 'gcol'], ['a1'])
        ts('vector', tmp84[:], modT[:, 24:32, :], 1.0, None, ALU.add, None, [('modT', 6), ('modT', 7)], ['tmp84'])
        tt('vector', a2[:], tmp84[:], gcol[:, 8:16].unsqueeze(2).to_broadcast([128, 8, 4]), ALU.mult, ['tmp84', 'gcol'], ['a2'])
        for (w1t, wpt, cpo, nm) in ((w1k, wposk, cposk, 'k'), (w1v, wposv, cposv, 'v')):
            mmg(MS[0:64, 0:1], [(w1t[:, l, :], wpt[:, l:l + 1]) for l in range(32)], ['w1' + nm, 'wpos' + nm], ['MS'])
            cp('vector', cpo[:], MS[0:64, 0:1], ['MS'], ['cpos' + nm])
    sch.dma('sync', gt_scr[:, :], rows_gt[:], reads=[('rowsgt', i) for i in range(4)], writes=['gtscr'])
    b1 = modT[:, 0:8, :]
    b2 = modT[:, 16:24, :]
    MODK = [('modT', i) for i in range(8)] + ['a1', 'a2']
    sch.barrier()
    stage0 = stage <= 0

    def norm_transpose(src_tile, srck, aa, bb, b, tti, tmp):
        junk, ss, lnv, rstd, xn = tmp
        CUT = int(os.environ.get('PHASEA_CUT', '99'))
        mset('vector', ss[:], 0.0, ['ss'])
        act(junk[:], src_tile, AF.Square, srck + ['ss'], ['junk', 'ss'], accum=ss[:, 0:1])
        if CUT < 1:
            return
        act(lnv[:], ss[:], AF.Ln, ['ss'], ['lnv'], bias=invf[:, 2:3], scale=1.0 / D)
        act(rstd[:], lnv[:], AF.Exp, ['lnv'], ['rstd'], scale=-0.5)
        if CUT < 2:
            return
        ts('vector', xn[:], src_tile, rstd[:, 0:1], None, ALU.mult, None, srck + ['rstd'], ['xn'])
        if CUT < 3:
            return
        TRx, trk = TRB[tti % 2]
        sch.group('tensor', [(lambda e, c=c: e.transpose(TRx[:, c * 128:(c + 1) * 128], xn[:, c * 128:(c + 1) * 128], ident[:])) for c in range(8)],
                  ['xn', 'ident'], [trk])
        if CUT < 4:
            return
        for c in range(8):
            dst = hT[:, c, tti * 128:(tti + 1) * 128]
            if True:
                act(dst, TRx[:, c * 128:(c + 1) * 128], AF.Identity, [trk] + MODK, [('hT', c, tti // 4)],
                    bias=bb[:, c, b:b + 1], scale=aa[:, c, b:b + 1])
            else:
                stt('vector', dst, TRx[:, c * 128:(c + 1) * 128], aa[:, c, b:b + 1], bb[:, c, b:b + 1].to_broadcast([128, 128]), ALU.mult, ALU.add,
                    [trk] + MODK, [('hT', c, tti // 4)])

    def hTk(tb):
        return [('hT', c, tb) for c in range(8)]

    for b in range(0 if stage0 else nb):
      bs = ExitStack()
      ms = ExitStack()
      asx = ExitStack()
      try:
            cosT = sbt([128, S], BF16, bs, 'cosT')
            sinS = sbt([128, S], BF16, bs, 'sinS')
            gtbc = sbt([128, 2048], F32, bs, 'gtbc')
            hT = sbt([128, 8, S], BF16, bs, 'hT')
            with ExitStack() as sc:
                posi = sbt([128, S], I32, sc, 'posi')
                u = sbt([128, S], F32, sc, 'u')
                ni = sbt([128, S], I32, sc, 'ni')
                nf = sbt([128, S], F32, sc, 'nf')
                sch.dma('sync', posi[:], pos_d[b:b + 1, :].to_broadcast([128, S]), writes=['posi'])
                cp('vector', u[:], posi[:], ['posi'], ['u'])
                ts('vector', u[:], u[:], invf[:, 0:1], None, ALU.mult, None, ['u', 'invf'], ['u'])
                cp('vector', ni[:], u[:], ['u'], ['ni'])
                cp('vector', nf[:], ni[:], ['ni'], ['nf'])
                tt('vector', nf[:], u[:], nf[:], ALU.subtract, ['u', 'nf'], ['nf'])
                act(sinS[:], nf[:], AF.Sin, ['nf', 'invf'], ['sinS'], scale=invf[:, 1:2])
                ts('vector', u[:], u[:], 0.25, None, ALU.add, None, ['u'], ['u'])
                cp('vector', ni[:], u[:], ['u'], ['ni'])
                cp('vector', nf[:], ni[:], ['ni'], ['nf'])
                tt('vector', nf[:], u[:], nf[:], ALU.subtract, ['u', 'nf'], ['nf'])
                act(cosT[:], nf[:], AF.Sin, ['nf', 'invf'], ['cosT'], scale=invf[:, 3:4])
                sch.barrier()
            stg(0.3)
            sch.dma('sync', gtbc[:], gt_scr[b:b + 1, :].to_broadcast([128, 2048]), reads=['gtscr'], writes=[('gtbc', i) for i in range(4)])
            stg(0.6)
            with ExitStack() as sc:
                xts = [sbt([128, D], F32, sc, 'xt') for _ in range(2)]
                tmpA = (sbt([128, D], BF16, sc, 'junk'), sbt([128, 1], F32, sc, 'ss'), sbt([128, 1], F32, sc, 'lnv'),
                        sbt([128, 1], F32, sc, 'rstd'), sbt([128, D], BF16, sc, 'xn'))
                for tti in range(16):
                    xt = xts[tti % 2]
                    sch.dma('sync', xt[:], x_d[b, tti * 128:(tti + 1) * 128, :], writes=['xt%d' % (tti % 2)])
                    norm_transpose(xt[:], ['xt%d' % (tti % 2)], a1, b1, b, tti, tmpA)
                sch.barrier()
            dump('hT', hT[:], [128, 8, S], BF16, [k for tb in range(4) for k in hTk(tb)])
            dump('cosT', cosT[:], [128, S], BF16, ['cosT'])
            dump('sinS', sinS[:], [128, S], BF16, ['sinS'])
            stg(1)

            oT = sbt([128, 8, S], BF16, ms, 'oT')
            Qa = sbt([96, 4, S], BF16, asx, 'Qa')
            Kb = sbt([96, 4, S], BF16, asx, 'Kb')
            Vb = sbt([128, 16, 4, 65], BF16, asx, 'Vb')
            oacc = sbt([128, 16, 4, 64], F32, asx, 'oacc')
            gates = sbt([128, 16, 12], F32, asx, 'gates')
            wt_fm = [sbt([128, 8, 128], BF16, asx, 'wtfm') for _ in range(2)]
            wt_tm = sbt([128, 8, 256], BF16, asx, 'wttm')
            qn = sbt([128, 512], BF16, asx, 'qn')
            qnB = sbt([128, 512], BF16, asx, 'qnB')
            t3B = sbt([128, 512], BF16, asx, 't3B')
            sq = sbt([128, 512], BF16, asx, 'sq')
            t3 = sbt([128, 512], BF16, asx, 't3')
            t1 = sbt([128, 512], F32, asx, 't1')
            t2 = sbt([128, 512], F32, asx, 't2')
            lnt = sbt([128, 512], F32, asx, 'lnt')
            rst = sbt([128, 512], F32, asx, 'rst')
            Pt = [sbt([128, 512], BF16, asx, 'P') for _ in range(3)]
            Tst = sbt([128, 96], BF16, asx, 'Tst')
            T2 = sbt([128, 4, 72], BF16, asx, 'T2')
            hid = sbt([64, 128], BF16, asx, 'hid')
            kcn = sbt([64, 128], BF16, asx, 'kcn')
            sm = sbt([128, 64], F32, asx, 'sm')
            imp3 = sbt([128, 4, 32], F32, asx, 'imp3')
            impm = sbt([128, 32], F32, asx, 'impm')
            impr = sbt([128, 32], F32, asx, 'impr')
            m8 = sbt([128, 4, 8], F32, asx, 'm8')
            gm = sbt([128, 4, 8], F32, asx, 'gm')
            lt8 = sbt([128, 4, 8], F32, asx, 'lt8')
            ksf = sbt([64, 4, 8], F32, asx, 'ksf')
            ksb = sbt([64, 4, 8], BF16, asx, 'ksb')
            otmp = sbt([128, 4, 64], F32, asx, 'otmp')
            obf = sbt([128, 4, 256], BF16, asx, 'obf')
            mset('vector', Vb[:, :, :, 64:65], 1.0, ['Vb_ones'])
            mset('vector', Tst[:], 0.0, ['Tst'])
            mset('vector', T2[:], 0.0, ['T2'])
            cnt = {'pj': 0, 's': 0, 'o': 0, 'p': 0, 'w': 0}

            def nxt(k, n):
                v = cnt[k] % n
                cnt[k] += 1
                return v

            def load_w_fm(pieces):
                i = nxt('w', 2)
                wt = wt_fm[i]
                c = 0
                for (c0, n) in pieces:
                    sch.dma('sync', wt[:, :, c:c + n], win_s[:, c0:c0 + n].rearrange("(kc p) c -> p kc c", p=128),
                            reads=wkeys('win'), writes=[('wtfm', i, c)])
                    c += n
                return wt, [('wtfm', i, cc) for cc in (0, 64)]

            qn2 = [qn, qnB]
            t32 = [t3, t3B]

            def proj_fm_multi(specs):
                its = [(si_, tb) for si_ in range(len(specs)) for tb in range(4)]
                wts_ = {}

                def getw(si_):
                    if si_ not in wts_ and si_ < len(specs):
                        wts_[si_] = load_w_fm(specs[si_][0])
                    return wts_.get(si_)
                getw(0)
                stA = {}

                def A(k):
                    si_, tb = its[k]
                    wt, wk = getw(si_)
                    if tb == 0:
                        getw(si_ + 1)
                    pi = nxt('pj', 2)
                    mmg(PJ[pi][:, :], [(wt[:, kc, :], hT[:, kc, tb * 512:(tb + 1) * 512]) for kc in range(8)],
                        wk + hTk(tb), [PK[pi]])
                    stA[k] = pi

                def BCD(k):
                    si_, tb = its[k]
                    pieces, norm, gaincol, dests = specs[si_]
                    pi = stA.pop(k)
                    pj = PJ[pi]
                    q_ = qn2[k % 2]
                    qk = 'qn%d' % (k % 2)
                    t3_ = t32[k % 2]
                    t3k = 't3%d' % (k % 2)
                    anyrope = any(kd == 'rope' for kd, _, _ in dests)
                    if norm:
                        act(sq[:], pj[:, :], AF.Square, [PK[pi]], ['sq'])
                        sch.op('tensor', mm(MS[:, :], bd64[:], sq[:]), ['sq', 'bd64'], ['MS'])
                        act(lnt[:], MS[:, :], AF.Ln, ['MS'], ['lnt'], bias=invf[:, 2:3], scale=1.0 / 64)
                        act(rst[:], lnt[:], AF.Exp, ['lnt'], ['rst'], scale=-0.5)
                        stt('vector', q_[:], pj[:, :], gaincol, rst[:], ALU.mult, ALU.mult, [PK[pi], 'rst', 'hg'], [qk])
                    else:
                        cp('scalar', q_[:], pj[:, :], [PK[pi]], [qk])
                    if anyrope:
                        si2 = nxt('s', 2)
                        sch.op('tensor', mm(SB[si2][:, :], rotm[:], q_[:]), [qk, 'rotm'], [SK[si2]])
                        tt('gpsimd', t1[:], q_[:], cosT[:, tb * 512:(tb + 1) * 512], ALU.mult, [qk, 'cosT'], ['t1'])
                        tt('vector', t2[:], SB[si2][:, :], sinS[:, tb * 512:(tb + 1) * 512], ALU.mult, [SK[si2], 'sinS'], ['t2'])
                        tt('vector', t3_[:], t1[:], t2[:], ALU.add, ['t1', 't2'], [t3k])
                    for hf, (kind, dfn, kfn) in enumerate(dests):
                        src = t3_ if kind == 'rope' else q_
                        sk = t3k if kind == 'rope' else qk
                        pr = slice(0, 64) if hf == 0 else slice(64, 128)
                        if hf == 0:
                            cp('gpsimd', dfn(tb), src[pr, :], [sk], kfn(tb))
                        else:
                            cp('scalar', dfn(tb), src[pr, :], [sk], kfn(tb))
                A(0)
                for k in range(len(its)):
                    if k + 1 < len(its):
                        A(k + 1)
                    BCD(k)

            def tblk(tb):
                return slice(tb * 512, (tb + 1) * 512)

            def qkeys(h, tb):
                return [('Qa', h, 4 * tb + i) for i in range(4)]

            def kkeys(i, tb):
                return [('Kb', i, 4 * tb + j) for j in range(4)]

            pend = [None]

            def flush():
                if pend[0] is not None:
                    f = pend[0]
                    pend[0] = None
                    f()

            def attn_round(units, vfn, ofirst_key, norm_fn):
                oi = nxt('o', 2)
                O = OB[oi]
                st = {'first': True}
                nU = len(units)
                for ui, un in enumerate(units):
                    si = nxt('s', 2)
                    Sb = SB[si]
                    n, ncol = un['n'], un['ncol']
                    sch.op('tensor', mm(Sb[0:n, 0:ncol], un['lhsT'], un['rhs']), un['reads'], [SK[si]])
                    flush()
                    pi = nxt('p', 3)
                    P = Pt[pi]
                    pk = 'P%d' % pi
                    act(P[0:n, 0:ncol], Sb[0:n, 0:ncol], AF.Exp, [SK[si]], [pk], scale=0.125)
                    if un.get('mask') is not None:
                        mo, mi_, mk = un['mask'](P)
                        tt('vector', mo, mo, mi_, ALU.mult, [pk, mk], [pk])

                    def pv(un=un, P=P, pk=pk, n=n, last=(ui == nU - 1)):
                        fns = []
                        npv = len(un['pv'])
                        for ii, (c0, oreg) in enumerate(un['pv']):
                            fns.append(mm(oreg(O), P[0:n, c0:c0 + 128], un['v'], st['first'], last and ii == npv - 1))
                            st['first'] = False
                        sch.group('tensor', fns, [pk] + un['vreads'], [OK_[oi]])
                        if last:
                            norm_fn(O, OK_[oi])
                    pend[0] = pv

            for pas in range(4):
                is_nsa = pas < 2
                g = pas % 2
                if is_nsa:
                    specs = []
                    for i in range(2):
                        c0 = O_QA + g * 256 + i * 128
                        specs.append(([(c0, 128)], True, hg[:, 0:1],
                                      [('rope', (lambda tb, h=2 * i: Qa[0:64, h, tblk(tb)]), (lambda tb, h=2 * i: qkeys(h, tb))),
                                       ('rope', (lambda tb, h=2 * i + 1: Qa[0:64, h, tblk(tb)]), (lambda tb, h=2 * i + 1: qkeys(h, tb)))]))
                    specs.append(([(O_KSL + g * 64, 64), (O_KWN + g * 64, 64)], True, hg[:, 1:2],
                                  [('rope', (lambda tb: Kb[0:64, 0, tblk(tb)]), (lambda tb: kkeys(0, tb))),
                                   ('rope', (lambda tb: Kb[0:64, 1, tblk(tb)]), (lambda tb: kkeys(1, tb)))]))
                    specs.append(([(O_KC + g * 64, 64), (O_VC + g * 64, 64)], False, None,
                                  [('rope', (lambda tb: Kb[0:64, 2, tblk(tb)]), (lambda tb: kkeys(2, tb))),
                                   ('copy', (lambda tb: Kb[0:64, 3, tblk(tb)]), (lambda tb: kkeys(3, tb)))]))
                    proj_fm_multi(specs)
                    sch.dma('gpsimd', Kb[64:96, 0, :], C['ind32'], writes=[('KbI', 0)])
                    wtk = []
                    for (c0, n, cc) in ((O_VSL + g * 64, 64, 0), (O_VWN + g * 64, 64, 64), (O_GN + g * 12, 12, 128)):
                        sch.dma('sync', wt_tm[:, :, cc:cc + n], win_s[:, c0:c0 + n].rearrange("(kc p) c -> p kc c", p=128),
                                reads=wkeys('win'), writes=[('wttm', cc)])
                        wtk.append(('wttm', cc))
                    for tti in range(16):
                        pi = nxt('pj', 2)
                        pj = PJ[pi]
                        mmg(pj[:, 0:140], [(hT[:, kc, tti * 128:(tti + 1) * 128], wt_tm[:, kc, 0:140]) for kc in range(8)],
                            wtk + hTk(tti // 4), [PK[pi]])
                        cp('vector', Vb[:, tti, 0:2, 0:64], pj[:, 0:128].rearrange("p (a d) -> p a d", a=2), [PK[pi]], [('Vb', tti)])
                        act(gates[:, tti, :], pj[:, 128:140], AF.Sigmoid, [PK[pi]], [('gates', tti)])
                    KC_ALL = [k for tb in range(4) for k in kkeys(2, tb)]
                    VC_ALL = [k for tb in range(4) for k in kkeys(3, tb)]
                    for (hi, w1t, cpo, nm, kall) in ((2, w1k, cposk, 'k', KC_ALL), (3, w1v, cposv, 'v', VC_ALL)):
                        mmg(MS[0:64, 0:127], [(w1t[:, l, :], Kb[0:64, hi, l:l + 16 * 126 + 1:16]) for l in range(32)],
                            kall + ['w1' + nm], ['MS'])
                        act(hid[:, 0:127], MS[0:64, 0:127], AF.Gelu_apprx_tanh, ['MS', 'cpos' + nm], ['hid'], bias=cpo[:, 0:1])
                        if nm == 'k':
                            si = nxt('s', 2)
                            sch.op('tensor', mm(SB[si][0:64, 0:127], w2k[:], hid[:, 0:127]), ['hid', 'w2k'], [SK[si]])
                            act(sq[0:64, 0:127], SB[si][0:64, 0:127], AF.Square, [SK[si]], ['sq'])
                            sch.op('tensor', mm(MS[0:64, 0:127], bd64[0:64, 0:64], sq[0:64, 0:127]), ['sq', 'bd64'], ['MS'])
                            act(lnt[0:64, 0:127], MS[0:64, 0:127], AF.Ln, ['MS'], ['lnt'], bias=invf[0:64, 2:3], scale=1.0 / 64)
                            act(rst[0:64, 0:127], lnt[0:64, 0:127], AF.Exp, ['lnt'], ['rst'], scale=-0.5)
                            stt('vector', kcn[:, 0:127], SB[si][0:64, 0:127], hg[0:64, 4:5], rst[0:64, 0:127], ALU.mult, ALU.mult,
                                [SK[si], 'rst', 'hg'], ['kcn'])
                        else:
                            si = nxt('s', 2)
                            sch.op('tensor', mm(SB[si][0:127, 0:64], hid[:, 0:127], w2v[:]), ['hid', 'w2v'], [SK[si]])
                            cp('vector', VCO[0:127, 0:64], SB[si][0:127, 0:64], [SK[si]], ['VCOv'])
                    if b == 0 and pas == 0:
                        dump('Qa', Qa[0:64, :, :], [64, 4, S], BF16, [k for h in range(4) for tb in range(4) for k in qkeys(h, tb)])
                        dump('Kb', Kb[0:64, :, :], [64, 4, S], BF16, [k for h in range(4) for tb in range(4) for k in kkeys(h, tb)])
                        dump('Vb', Vb[:], [128, 16, 4, 65], BF16, [('Vb', i) for i in range(16)] + ['Vb_ones'])
                        dump('gates', gates[:], [128, 16, 12], F32, [('gates', i) for i in range(16)])
                        dump('kcn', kcn[:], [64, 128], BF16, ['kcn'])
                        dump('VCO', VCO[:], [128, 97], BF16, ['VCOv', 'VCO'])
                        stg(2)

                    def nsa_norm(qt, br, first_branch, imp_out):
                        def fn(O, ok):
                            Ov = O[:, 0:388].rearrange("p (r c) -> p r c", r=4) if br == 0 else \
                                O[:, 0:260].rearrange("p (r c) -> p r c", r=4)
                            ts('vector', sm[:, 0:4], Ov[:, :, 64], 1e-30, None, ALU.max, None, [ok], ['sm0'])
                            sch.op('vector', lambda e: e.reciprocal(out=sm[:, 4:8], in_=sm[:, 0:4]), ['sm0'], ['sm1'])
                            tt('vector', sm[:, 8:12], sm[:, 4:8], gates[:, qt, br:12:3], ALU.mult, ['sm1', ('gates', qt)], ['sm2'])
                            fb = sm[:, 8:12].unsqueeze(2).to_broadcast([128, 4, 64])
                            if first_branch:
                                tt('vector', oacc[:, qt, :, :], Ov[:, :, 0:64], fb, ALU.mult, [ok, 'sm2'], [('oacc', qt)])
                            else:
                                for r in range(4):
                                    stt('vector', oacc[:, qt, r, :], Ov[:, r, 0:64], sm[:, 8 + r:9 + r], oacc[:, qt, r, :], ALU.mult, ALU.add,
                                        [ok, 'sm2', ('oacc', qt)], [('oacc', qt)])
                            if imp_out:
                                tt('vector', imp3[:], Ov[:, :, 65:97], sm[:, 4:8].unsqueeze(2).to_broadcast([128, 4, 32]), ALU.mult,
                                   [ok, 'sm1'], ['imp3'])
                        return fn

                    for qt in range(16):
                        ncv = min(127, 8 * qt + 7)
                        qs = slice(qt * 128, (qt + 1) * 128)
                        unit = dict(lhsT=kcn[:, 0:ncv], rhs=Qa[0:64, 0:4, qs], n=ncv, ncol=512,
                                    reads=['kcn'] + [('Qa', h, qt) for h in range(4)],
                                    mask=(lambda P, ncv=ncv, qt=qt: (P[0:ncv, :].rearrange("p (r t) -> p r t", r=4),
                                                                   cmpmask[0:ncv, qt, :].unsqueeze(1).to_broadcast([ncv, 4, 128]), 'cmpmask')),
                                    pv=[(r * 128, (lambda O, r=r: O[:, r * 97:(r + 1) * 97])) for r in range(4)],
                                    v=VCO[0:ncv, 0:97], vreads=['VCOv', 'VCO'])
                        def prenorm(O, ok, qt=qt, qs=qs):
                            nsa_norm(qt, 0, True, qt >= 8)(O, ok)
                            if qt >= 8:
                                sch.op('vector', lambda e: e.tensor_reduce(out=impr[:], in_=imp3[:].rearrange("p r j -> p j r"),
                                                                            axis=AX.X, op=ALU.add), ['imp3'], ['impr'])
                                tt('vector', impm[:], impr[:], cn_t[:, qt - 8, :], ALU.add, ['impr', 'cn'], ['impm'])
                                sch.op('vector', lambda e: e.max(out=m8[:, 0, :], in_=impm[:]), ['impm'], ['m8'])
                                sch.op('vector', lambda e: e.match_replace(out=impr[:], in_to_replace=m8[:, 0, :], in_values=impm[:],
                                                                            imm_value=-3e38), ['impm', 'm8'], ['impr'])
                                sch.op('vector', lambda e: e.max(out=m8[:, 1, :], in_=impr[:]), ['impr', 'm8'], ['m8'])
                                ts('vector', Tst[:, 64:96], impm[:], m8[:, 1, 7:8], NEGB, ALU.is_lt, ALU.mult, ['impm', 'm8'], ['Tst'])
                                trp(TR[0:96, 0:128], Tst[:], ident[:], ['Tst', 'ident'], ['TR'])
                                cp('scalar', Qa[64:96, 0:4, qs], TR[64:96, 0:128].unsqueeze(1).to_broadcast([32, 4, 128]),
                                   ['TR'], [('QaB', qt)])
                        attn_round([unit], None, None, prenorm)
                    for qt in range(16):
                        qs = slice(qt * 128, (qt + 1) * 128)
                        units = []
                        for kt in range(max(0, qt - 4), qt + 1):
                            ks = slice(kt * 128, (kt + 1) * 128)
                            mk = None
                            if kt == qt:
                                mk = (lambda P: (P[:, :].rearrange("p (r t) -> p r t", r=4), mlow[:].unsqueeze(1).to_broadcast([128, 4, 128]), 'mlow'))
                            elif kt == qt - 4:
                                mk = (lambda P: (P[:, :].rearrange("p (r t) -> p r t", r=4), mup[:].unsqueeze(1).to_broadcast([128, 4, 128]), 'mup'))
                            units.append(dict(lhsT=Kb[0:64, 1, ks], rhs=Qa[0:64, 0:4, qs], n=128, ncol=512,
                                              reads=[('Kb', 1, kt)] + [('Qa', h, qt) for h in range(4)], mask=mk,
                                              pv=[(r * 128, (lambda O, r=r: O[:, r * 65:(r + 1) * 65])) for r in range(4)],
                                              v=Vb[:, kt, 1, :], vreads=[('Vb', kt), 'Vb_ones']))
                        attn_round(units, None, None, nsa_norm(qt, 2, False, False))
                    for qt in range(16):
                        qs = slice(qt * 128, (qt + 1) * 128)
                        kd = 64 if qt < 8 else 96
                        units = []
                        for kt in range(0, qt + 1):
                            ks = slice(kt * 128, (kt + 1) * 128)
                            mk = None
                            if kt == qt:
                                mk = (lambda P: (P[:, :].rearrange("p (r t) -> p r t", r=4), mlow[:].unsqueeze(1).to_broadcast([128, 4, 128]), 'mlow'))
                            rd = [('Kb', 0, kt)] + [('Qa', h, qt) for h in range(4)]
                            if kd == 96:
                                rd += [('KbI', 0), ('QaB', qt)]
                            units.append(dict(lhsT=Kb[0:kd, 0, ks], rhs=Qa[0:kd, 0:4, qs], n=128, ncol=512, reads=rd, mask=mk,
                                              pv=[(r * 128, (lambda O, r=r: O[:, r * 65:(r + 1) * 65])) for r in range(4)],
                                              v=Vb[:, kt, 0, :], vreads=[('Vb', kt), 'Vb_ones']))
                        attn_round(units, None, None, nsa_norm(qt, 1, False, False))
                    chunk0 = 2 * g
                else:
                    hgp = g
                    specs = []
                    for i in range(2):
                        c0 = O_QB + (4 * hgp + 2 * i) * 64
                        specs.append(([(c0, 128)], True, hg[:, 2:3],
                                      [('rope', (lambda tb, h=2 * i: Qa[0:64, h, tblk(tb)]), (lambda tb, h=2 * i: qkeys(h, tb))),
                                       ('rope', (lambda tb, h=2 * i + 1: Qa[0:64, h, tblk(tb)]), (lambda tb, h=2 * i + 1: qkeys(h, tb)))]))
                    for i in range(2):
                        c0 = O_KB + (4 * hgp + 2 * i) * 64
                        specs.append(([(c0, 128)], True, hg[:, 3:4],
                                      [('rope', (lambda tb, h=2 * i: Kb[0:64, h, tblk(tb)]), (lambda tb, h=2 * i: kkeys(h, tb))),
                                       ('rope', (lambda tb, h=2 * i + 1: Kb[0:64, h, tblk(tb)]), (lambda tb, h=2 * i + 1: kkeys(h, tb)))]))
                    proj_fm_multi(specs)
                    for h in range(4):
                        sch.dma('gpsimd', Kb[64:72, h, :], C['ind8'], writes=[('KbI', h)])
                    c0 = O_VB + hgp * 256
                    sch.dma('sync', wt_tm[:, :, 0:256], win_s[:, c0:c0 + 256].rearrange("(kc p) c -> p kc c", p=128),
                            reads=wkeys('win'), writes=[('wttm', 0), ('wttm', 64), ('wttm', 128)])
                    for tti in range(16):
                        pi = nxt('pj', 2)
                        pj = PJ[pi]
                        mmg(pj[:, 0:256], [(hT[:, kc, tti * 128:(tti + 1) * 128], wt_tm[:, kc, 0:256]) for kc in range(8)],
                            [('wttm', 0), ('wttm', 64), ('wttm', 128)] + hTk(tti // 4), [PK[pi]])
                        cp('vector', Vb[:, tti, :, 0:64], pj[:, 0:256].rearrange("p (a d) -> p a d", a=4), [PK[pi]], [('Vb', tti)])
                    stg(3.65)
                    for h in range(4):
                        sch.op('vector', lambda e, h=h: e.tensor_reduce(out=ksf[:, h, :], in_=Kb[0:64, h, :].rearrange("p (j k) -> p j k", k=256),
                                                                        axis=AX.X, op=ALU.add),
                               [k for tb in range(4) for k in kkeys(h, tb)], [('ksf', h)])
                    cp('vector', ksb[:], ksf[:], [('ksf', h) for h in range(4)], ['ksb'])
                    for qt in range(8, 16):
                        qs = slice(qt * 128, (qt + 1) * 128)
                        cur = qt // 2
                        for h in range(4):
                            sch.op('tensor', mm(MS[:, h * 8:(h + 1) * 8], Qa[0:64, h, qs], ksb[:, h, :]), [('Qa', h, qt), 'ksb'], ['MS'])
                        tt('vector', gm[:], MS[:, 0:32].rearrange("p (h j) -> p h j", h=4),
                           cmo_t[:, qt - 8, :].unsqueeze(1).to_broadcast([128, 4, 8]), ALU.add, ['MS', 'cmo'], ['gm'])
                        for h in range(4):
                            sch.op('vector', lambda e, h=h: e.max(out=m8[:, h, :], in_=gm[:, h, :]), ['gm', 'm8'], ['m8'])
                        tt('vector', lt8[:], gm[:], m8[:, :, 2:3].to_broadcast([128, 4, 8]), ALU.is_lt, ['gm', 'm8'], ['lt8'])
                        ts('vector', T2[:, :, 64:72], lt8[:], NEGB, None, ALU.mult, None, ['lt8'], ['T2'])
                        mset('vector', T2[:, :, 64 + cur:65 + cur], 0.0, ['T2'])
                        sch.group('tensor', [(lambda e, h=h: e.transpose(TR[0:72, h * 128:(h + 1) * 128], T2[:, h, :], ident[:])) for h in range(4)],
                                  ['T2', 'ident'], ['TR'])
                        cp('scalar', Qa[64:72, 0:4, qs], TR[64:72, 0:512].rearrange("p (h t) -> p h t", h=4),
                           ['TR'], [('QaB', qt)])
                    if b == 0 and pas == 2:
                        dump('Qm', Qa[0:72, :, :], [72, 4, S], BF16, [k for h in range(4) for tb in range(4) for k in qkeys(h, tb)] + [('QaB', q) for q in range(8, 16)])
                        dump('Km', Kb[0:72, :, :], [72, 4, S], BF16, [k for h in range(4) for tb in range(4) for k in kkeys(h, tb)] + [('KbI', h) for h in range(4)])
                    stg(3.7)
                    for h in range(4):
                        for QB in range(4):
                            kd = 64 if QB < 2 else 72
                            units = []
                            for kt in range(0, 4 * QB + 4):
                                ql0 = max(0, kt - 4 * QB)
                                ncol = (4 - ql0) * 128
                                ks = slice(kt * 128, (kt + 1) * 128)
                                q0 = (4 * QB + ql0) * 128
                                mk = None
                                if kt >= 4 * QB:
                                    mk = (lambda P: (P[:, 0:128], mlow[:], 'mlow'))
                                rd = [('Kb', h, kt)] + [('Qa', h, 4 * QB + ql) for ql in range(ql0, 4)]
                                if kd == 72:
                                    rd += [('KbI', h)] + [('QaB', 4 * QB + ql) for ql in range(ql0, 4)]
                                units.append(dict(lhsT=Kb[0:kd, h, ks], rhs=Qa[0:kd, h, q0:(4 * QB + 4) * 128], n=128, ncol=ncol, reads=rd, mask=mk,
                                                  pv=[((ql - ql0) * 128, (lambda O, ql=ql: O[:, ql * 65:(ql + 1) * 65])) for ql in range(ql0, 4)],
                                                  v=Vb[:, kt, h, :], vreads=[('Vb', kt), 'Vb_ones']))

                            def mnorm(O, ok, h=h, QB=QB):
                                Ov = O[:, 0:260].rearrange("p (r c) -> p r c", r=4)
                                sch.op('vector', lambda e: e.reciprocal(out=sm[:, 4:8], in_=Ov[:, :, 64]), [ok], ['sm1'])
                                tt('vector', oacc[:, 4 * QB:4 * QB + 4, h, :], Ov[:, :, 0:64],
                                   sm[:, 4:8].unsqueeze(2).to_broadcast([128, 4, 64]), ALU.mult, [ok, 'sm1'],
                                   [('oacc', 4 * QB + i) for i in range(4)])
                            attn_round(units, None, None, mnorm)
                    chunk0 = 4 + 2 * g
                flush()
                if b == 0 and pas in (0, 2):
                    dump('oacc%d' % pas, oacc[:], [128, 16, 4, 64], F32, [('oacc', i) for i in range(16)])
                    if pas == 0:
                        stg(3)
                for q4 in range(4):
                    cp('vector', obf[:], oacc[:, 4 * q4:4 * q4 + 4, :, :].rearrange("p q h d -> p q (h d)"),
                       [('oacc', 4 * q4 + i) for i in range(4)], ['obf'])
                    OTC = int(os.environ.get('OT_CUT', '9'))
                    if OTC < 1:
                        continue
                    TRx, trk = TRB[q4 % 2]
                    sch.group('tensor', [(lambda e, ci=ci, ql=ql, TRx=TRx: e.transpose(TRx[:, (ci * 4 + ql) * 128:(ci * 4 + ql + 1) * 128],
                                                                              obf[:, ql, ci * 128:(ci + 1) * 128], ident[:]))
                                         for ci in range(2) for ql in range(4)], ['obf', 'ident'], [trk])
                    if OTC == 3:
                        mset('vector', oT[:, chunk0, q4 * 512:(q4 + 1) * 512], 1.0, [('oT', chunk0, q4)])
                        continue
                    if OTC == 4:
                        act(oT[:, chunk0, q4 * 512:q4 * 512 + 128], obf[:, 0, 0:128], AF.Identity, ['obf'], [('oT', chunk0, q4)])
                        continue
                    if OTC == 5:
                        act(t3[:, 0:128], TRx[:, 0:128], AF.Identity, [trk], ['t3'])
                        continue
                    for ci in range(2 if OTC >= 2 else 0):
                        for ql in range(4):
                            act(oT[:, chunk0 + ci, (q4 * 4 + ql) * 128:(q4 * 4 + ql + 1) * 128], TRx[:, (ci * 4 + ql) * 128:(ci * 4 + ql + 1) * 128],
                                AF.Identity, [trk], [('oT', chunk0 + ci, q4)])
                stg(3.2 + 0.2 * pas)
            sch.barrier()
            asx.close()
            dump('oT', oT[:], [128, 8, S], BF16, [('oT', c, q) for c in range(8) for q in range(4)])
            stg(4)

            with ExitStack() as xs:
                mixT = sbt([128, 8, S], BF16, xs, 'mixT')
                woutt = sbt([128, 8, 1024], BF16, xs, 'woutt')
                wb_t = [[sbt([128, 4, 128], BF16, xs, 'wbt') for _ in range(2)] for _ in range(2)]
                wg_t = [[sbt([128, 8, 128], BF16, xs, 'wgt') for _ in range(2)] for _ in range(2)]
                sga = sbt([128, 512], F32, xs, 'sga')
                sgb = sbt([128, 512], F32, xs, 'sgb')
                m1 = sbt([128, 512], F32, xs, 'm1')
                m2 = sbt([128, 512], F32, xs, 'm2')
                xts = [sbt([128, D], F32, xs, 'xt') for _ in range(2)]
                x1t = [sbt([128, D], F32, xs, 'x1t') for _ in range(2)]
                tmpx = sbt([128, 512], F32, xs, 'tmpx')
                tmpA = (sbt([128, D], BF16, xs, 'junk'), sbt([128, 1], F32, xs, 'ss'), sbt([128, 1], F32, xs, 'lnv'),
                        sbt([128, 1], F32, xs, 'rstd'), sbt([128, D], BF16, xs, 'xn'))
                sch.dma('sync', woutt[:], wout_s[:, :].rearrange("(kc p) c -> p kc c", p=128), reads=wkeys('wout'), writes=['woutt'])
                for fc in range(8):
                    wi = fc % 2
                    fs = slice(fc * 128, (fc + 1) * 128)
                    sch.dma('sync', wb_t[0][wi][:], wbn_s[:, fs].rearrange("(kc p) c -> p kc c", p=128), reads=wkeys('wbn', 4), writes=[('wbt', 0, wi)])
                    sch.dma('sync', wb_t[1][wi][:], wbm_s[:, fs].rearrange("(kc p) c -> p kc c", p=128), reads=wkeys('wbm', 4), writes=[('wbt', 1, wi)])
                    sch.dma('sync', wg_t[0][wi][:], win_s[:, O_GA + fc * 128:O_GA + (fc + 1) * 128].rearrange("(kc p) c -> p kc c", p=128),
                            reads=wkeys('win'), writes=[('wgt', 0, wi)])
                    sch.dma('sync', wg_t[1][wi][:], win_s[:, O_GB + fc * 128:O_GB + (fc + 1) * 128].rearrange("(kc p) c -> p kc c", p=128),
                            reads=wkeys('win'), writes=[('wgt', 1, wi)])
                    for tb in range(4):
                        tsl = tblk(tb)
                        mmg(PJ[0][:, :], [(wb_t[0][wi][:, kc, :], oT[:, kc, tsl]) for kc in range(4)],
                            [('wbt', 0, wi)] + [('oT', kc, tb) for kc in range(4)], ['PJ0'])
                        mmg(PJ[1][:, :], [(wg_t[0][wi][:, kc, :], hT[:, kc, tsl]) for kc in range(8)],
                            [('wgt', 0, wi)] + hTk(tb), ['PJ1'])
                        act(sga[:], PJ[1][:, :], AF.Sigmoid, ['PJ1'], ['sga'])
                        tt('vector', m1[:], PJ[0][:, :], sga[:], ALU.mult, ['PJ0', 'sga'], ['m1'])
                        mmg(SB[0][:, :], [(wb_t[1][wi][:, kc, :], oT[:, 4 + kc, tsl]) for kc in range(4)],
                            [('wbt', 1, wi)] + [('oT', 4 + kc, tb) for kc in range(4)], ['S0'])
                        mmg(SB[1][:, :], [(wg_t[1][wi][:, kc, :], hT[:, kc, tsl]) for kc in range(8)],
                            [('wgt', 1, wi)] + hTk(tb), ['S1'])
                        act(sgb[:], SB[1][:, :], AF.Sigmoid, ['S1'], ['sgb'])
                        tt('vector', m2[:], SB[0][:, :], sgb[:], ALU.mult, ['S0', 'sgb'], ['m2'])
                        tt('gpsimd', mixT[:, fc, tsl], m1[:], m2[:], ALU.add, ['m1', 'm2'], [('mixT', fc, tb)])
                for tti in range(16):
                    xi = tti % 2
                    tsl = slice(tti * 128, (tti + 1) * 128)
                    sch.dma('sync', xts[xi][:], x_d[b, tsl, :], writes=['xt%d' % xi])
                    for half in range(2):
                        hs = slice(half * 512, (half + 1) * 512)
                        mmg(PJ[half][:, :], [(mixT[:, kc, tsl], woutt[:, kc, hs]) for kc in range(8)],
                            ['woutt'] + [('mixT', kc, tti // 4) for kc in range(8)], [PK[half]])
                        tt('vector', tmpx[:], PJ[half][:, :], gtbc[:, hs], ALU.mult, [PK[half], ('gtbc', half)], ['tmpx'])
                        tt('gpsimd', x1t[xi][:, hs], tmpx[:], xts[xi][:, hs], ALU.add, ['tmpx', 'xt%d' % xi], [('x1t', xi, half)])
                    sch.dma('sync', x1_s[tsl, :], x1t[xi][:], reads=[('x1t', xi, 0), ('x1t', xi, 1)], writes=[('x1s', tti)])
                    norm_transpose(x1t[xi][:], [('x1t', xi, 0), ('x1t', xi, 1)], a2, b2, b, tti, tmpA)
                sch.barrier()
            ms.close()
            dump('h2T', hT[:], [128, 8, S], BF16, [k for tb in range(4) for k in hTk(tb)])
            stg(5)

            with ExitStack() as fs_:
                yT = sbt([128, 22, 1024], BF16, fs_, 'yT')
                wu = [sbt([128, 8, 256], BF16, fs_, 'wu') for _ in range(2)]
                wd = [sbt([128, 22, 512], BF16, fs_, 'wd') for _ in range(2)]
                aS = [sbt([128, 514], F32, fs_, 'aS') for _ in range(2)]
                halo = sbt([128, 22, 2], F32, fs_, 'halo')
                c1 = sbt([128, 512], F32, fs_, 'c1')
                c2 = sbt([128, 512], F32, fs_, 'c2')
                c3 = sbt([128, 512], F32, fs_, 'c3')
                gl = sbt([128, 512], F32, fs_, 'gl')
                x1q = [sbt([128, 512], F32, fs_, 'x1q') for _ in range(2)]
                oq = [sbt([128, 512], F32, fs_, 'oq') for _ in range(2)]
                tmpo = sbt([128, 512], F32, fs_, 'tmpo')
                mset('vector', halo[:], 0.0, ['halo'])
                blk = 0
                for hf in range(2):
                    for fc in range(22):
                        wi = fc % 2
                        sch.dma('sync', wu[wi][:, :, 0:128], wup_s[:, fc * 128:(fc + 1) * 128].rearrange("(kc p) c -> p kc c", p=128),
                                reads=wkeys('wup'), writes=[('wu', wi, 0)])
                        sch.dma('sync', wu[wi][:, :, 128:256], wup_s[:, DFF + fc * 128:DFF + (fc + 1) * 128].rearrange("(kc p) c -> p kc c", p=128),
                                reads=wkeys('wup'), writes=[('wu', wi, 1)])
                        for tb2 in range(2):
                            tok0 = hf * 1024 + tb2 * 512
                            tbg = tok0 // 512
                            ai = blk % 2
                            blk += 1
                            a_ = aS[ai]
                            ak = 'aS%d' % ai
                            Ab, Ak = [(PJ[0], 'PJ0'), (SB[0], 'S0')][blk % 2]
                            Vb_, Vk = [(PJ[1], 'PJ1'), (SB[1], 'S1'), (OB[0], 'O0'), (OB[1], 'O1')][blk % 4]
                            mmg(Ab[:, :], [(wu[wi][:, kc, 0:128], hT[:, kc, tok0:tok0 + 512]) for kc in range(8)],
                                [('wu', wi, 0)] + hTk(tbg), [Ak])
                            mmg(Vb_[:, :], [(wu[wi][:, kc, 128:256], hT[:, kc, tok0:tok0 + 512]) for kc in range(8)],
                                [('wu', wi, 1)] + hTk(tbg), [Vk])
                            cp('gpsimd', a_[:, 0:2], halo[:, fc, :], ['halo'], [ak])
                            cp('scalar', a_[:, 2:514], Ab[:, :], [Ak], [ak])
                            act(c1[:], a_[:, 0:512], AF.Identity, [ak, 'cw', 'cb'], ['c1'], bias=cb[:, fc:fc + 1], scale=cw[:, fc, 0:1])
                            stt('vector', c2[:], a_[:, 1:513], cw[:, fc, 1:2], c1[:], ALU.mult, ALU.add, [ak, 'c1', 'cw'], ['c2'])
                            stt('vector', c3[:], a_[:, 2:514], cw[:, fc, 2:3], c2[:], ALU.mult, ALU.add, [ak, 'c2', 'cw'], ['c3'])
                            cp('gpsimd', halo[:, fc, :], a_[:, 512:514], [ak], ['halo'])
                            act(gl[:], c3[:], AF.Gelu_apprx_tanh, ['c3'], ['gl'])
                            tt('vector', yT[:, fc, tb2 * 512:(tb2 + 1) * 512], gl[:], Vb_[:, :], ALU.mult, ['gl', Vk], [('yT', fc, tb2)])
                    for nq in range(2):
                        wi = nq % 2
                        ns = slice(nq * 512, (nq + 1) * 512)
                        for (r0, r1) in ((0, 8), (8, 16), (16, 22)):
                            sch.dma('sync', wd[wi][:, r0:r1, :], wdn_s[r0 * 128:r1 * 128, ns].rearrange("(kc p) c -> p kc c", p=128),
                                    reads=wkeys('wdn', 22), writes=[('wd', wi, r0)])
                        for t8 in range(8):
                            tti = hf * 8 + t8
                            tsl = slice(tti * 128, (tti + 1) * 128)
                            pi = nxt('pj', 2)
                            oi = t8 % 2
                            mmg(PJ[pi][:, :], [(yT[:, fc, t8 * 128:(t8 + 1) * 128], wd[wi][:, fc, :]) for fc in range(22)],
                                [('wd', wi, 0), ('wd', wi, 8), ('wd', wi, 16)] + [('yT', fc, t8 // 4) for fc in range(22)], [PK[pi]])
                            sch.dma('sync', x1q[oi][:], x1_s[tsl, ns], reads=[('x1s', tti)], writes=['x1q%d' % oi])
                            tt('vector', tmpo[:], PJ[pi][:, :], gtbc[:, 1024 + nq * 512:1024 + (nq + 1) * 512], ALU.mult,
                               [PK[pi], ('gtbc', 2), ('gtbc', 3)], ['tmpo'])
                            tt('gpsimd', oq[oi][:], tmpo[:], x1q[oi][:], ALU.add, ['tmpo', 'x1q%d' % oi], ['oq%d' % oi])
                            sch.dma('sync', out_d[b, tsl, ns], oq[oi][:], reads=['oq%d' % oi], writes=[('out', b, tti, nq)])
                sch.barrier()
            bs.close()
      except StopBuild:
        sch.barrier()
        asx.close(); ms.close(); bs.close()
        break

    sch.finish()
    with nc.Block() as block:
        @block.sync
        def _(e):
            for f in sch.streams['sync']:
                f(e)

        @block.scalar
        def _(e):
            for f in sch.streams['scalar']:
                f(e)

        @block.vector
        def _(e):
            for f in sch.streams['vector']:
                f(e)

        @block.gpsimd
        def _(e):
            for f in sch.streams['gpsimd']:
                f(e)

        @block.tensor
        def _(e):
            for f in sch.streams['tensor']:
                f(e)
    es.close()
    return nc, dbg_outs, sch


def make_in_maps(inputs, nb=4, ncores=NCORE, batches=None):
    f = lambda a: np.ascontiguousarray(np.asarray(a, dtype=np.float32))
    x = np.asarray(inputs['x'])
    c = np.asarray(inputs['c'], dtype=np.float32)
    pos = np.asarray(inputs['positions']).astype(np.int32)
    col = lambda v: np.ascontiguousarray(np.asarray(v, np.float32).reshape(-1, 128).T)
    tile2 = lambda v: np.concatenate([np.asarray(v, np.float32)] * 2)
    hgm = np.zeros((128, 8), np.float32)
    hgm[:, 0] = tile2(inputs['g_q_nsa'][0])
    hgm[:, 1] = np.concatenate([inputs['g_k_slc'][0], inputs['g_k_win'][0]])
    hgm[:, 2] = tile2(inputs['g_q_moba'][0])
    hgm[:, 3] = tile2(inputs['g_k_moba'][0])
    hgm[:, 4] = tile2(inputs['g_k_cmp'][0])
    shared = {
        'bada': f(inputs['b_ada'][0][None, :]),
        'badaT': col(inputs['b_ada'][0]),
        'gcol': np.concatenate([col(inputs['g_attn_norm'][0]), col(inputs['g_ffn_norm'][0])], 1),
        'hg': hgm,
        'cw': np.ascontiguousarray(np.asarray(inputs['conv_w'][0], np.float32).T.reshape(22, 128, 3).transpose(1, 0, 2)),
        'cb': col(inputs['conv_b'][0]),
        'wposk': f(np.asarray(inputs['cmp_k_pos'][0]).T),
        'wposv': f(np.asarray(inputs['cmp_v_pos'][0]).T),
        'w_ada': f(inputs['w_ada'][0]), 'w_in': f(inputs['w_in'][0]),
        'w1k': f(inputs['cmp_k_w1'][0]), 'w2k': f(inputs['cmp_k_w2'][0]),
        'w1v': f(inputs['cmp_v_w1'][0]), 'w2v': f(inputs['cmp_v_w2'][0]),
        'wbn': f(inputs['w_branch_nsa'][0]), 'wbm': f(inputs['w_branch_moba'][0]),
        'wout': f(inputs['w_out'][0]), 'wup': f(inputs['w_ffn_up'][0]), 'wdn': f(inputs['w_ffn_down'][0]),
    }
    for k, v in _consts().items():
        shared['c_' + k] = v
    maps = []
    for ci in range(ncores):
        bl = batches[ci] if batches is not None else list(range(ci * nb, (ci + 1) * nb))
        m = dict(shared)
        m['x'] = f(x[bl])
        m['pos'] = np.ascontiguousarray(pos[bl])
        cc = np.zeros((4, 1024), np.float32)
        cc[:len(bl)] = c[bl]
        m['cT'] = np.ascontiguousarray(cc.T.reshape(8, 128, 4).transpose(1, 0, 2))
        maps.append(m)
    return maps


_CACHE = {}


def kernel(**inputs):
    nb = 4
    if 'nc' not in _CACHE:
        _CACHE['nc'] = build(nb)[0]
    nc = _CACHE['nc']
    maps = make_in_maps(inputs, nb)
    res = run_bass_kernel_spmd(nc, maps, core_ids=list(range(NCORE)))
    out = np.concatenate([np.asarray(r['out']) for r in res.results], axis=0)
    return out.astype(np.float32)
```
